# Optimizing a Trainium2 kernel written in Bass

```python
import math
import jax, jax.numpy as jnp
from jax import lax
import numpy as np

D_MODEL = 2048
BATCH = 4
SEQ = 4096
DEPTH = 1

N_DIFF_HEADS = 6
HEAD_DIM = 128
V_HEAD_DIM = 2 * HEAD_DIM
ATTN_QK_WIDTH = N_DIFF_HEADS * 2 * HEAD_DIM
ATTN_V_WIDTH = N_DIFF_HEADS * V_HEAD_DIM
LAYER_INDEX = 1
LAMBDA_INIT = 0.8 - 0.6 * math.exp(-0.3 * (LAYER_INDEX - 1))
Q_BLOCK = 128

ROPE_THETA = 500000.0
ROT_DIM = HEAD_DIM // 4

N_FOURIER_GROUPS = 4
FOURIER_GROUP_DIM = 128
FOURIER_WIDTH = N_FOURIER_GROUPS * FOURIER_GROUP_DIM

Q_END = ATTN_QK_WIDTH
K_END = Q_END + ATTN_QK_WIDTH
V_END = K_END + ATTN_V_WIDTH
F_END = V_END + FOURIER_WIDTH
GA_END = F_END + D_MODEL
GF_END = GA_END + D_MODEL
IN_WIDTH = GF_END

D_FF = ((8 * D_MODEL // 3 + 255) // 256) * 256

RMS_EPS = 1e-5

kernel_name = "gated_diffattn_fnet_hybrid_block"


def rmsnorm(x, g):
    xf = x.astype(jnp.float32)
    y = xf * lax.rsqrt(jnp.mean(xf * xf, axis=-1, keepdims=True) + RMS_EPS)
    return (y * g.astype(jnp.float32)).astype(x.dtype)


def rope_tables(seq_len, dtype):
    pos = jnp.arange(seq_len, dtype=jnp.float32)
    inv_freq = ROPE_THETA ** (-jnp.arange(0, ROT_DIM, 2, dtype=jnp.float32) / ROT_DIM)
    ang = pos[:, None] * inv_freq[None, :]
    cos = jnp.cos(ang)[None, :, None, None, :].astype(dtype)
    sin = jnp.sin(ang)[None, :, None, None, :].astype(dtype)
    return cos, sin


def partial_rope(x, cos, sin):
    xr, xp = x[..., :ROT_DIM], x[..., ROT_DIM:]
    x1, x2 = xr[..., : ROT_DIM // 2], xr[..., ROT_DIM // 2:]
    rot = jnp.concatenate([x1 * cos - x2 * sin, x2 * cos + x1 * sin], axis=-1)
    return jnp.concatenate([rot, xp], axis=-1)


def diff_attention(q, k, v, lam):
    b, s = q.shape[0], q.shape[1]
    n_blk = s // Q_BLOCK
    scale = 1.0 / math.sqrt(HEAD_DIM)
    qb = jnp.moveaxis(q.reshape(b, n_blk, Q_BLOCK, N_DIFF_HEADS, 2, HEAD_DIM), 1, 0)

    def one_block(qblk):
        sc = jnp.einsum('bqhcd,bkhcd->bhcqk', qblk, k).astype(jnp.float32) * scale
        p = jax.nn.softmax(sc, axis=-1)
        a = (p[:, :, 0] - lam * p[:, :, 1]).astype(v.dtype)
        return jnp.einsum('bhqk,bkhe->bqhe', a, v)

    o = lax.map(one_block, qb)
    return jnp.moveaxis(o, 0, 1).reshape(b, s, N_DIFF_HEADS, V_HEAD_DIM)


def fourier_mix(f):
    ff = jnp.fft.fft2(f.astype(jnp.float32), axes=(1, 3), norm="ortho")
    return jnp.real(ff).astype(f.dtype)


def setup_inputs(seed: int = 0) -> dict:
    key = jax.random.key(seed)
    ks = jax.random.split(key, 16)
    f32 = jnp.float32

    def dense(k, fan_in, fan_out):
        return jax.random.normal(k, (fan_in, fan_out), f32) * fan_in ** -0.5

    return {
        "x": jax.random.normal(ks[0], (BATCH, SEQ, D_MODEL), f32),
        "g_mix": 1.0 + 0.02 * jax.random.normal(ks[1], (D_MODEL,), f32),
        "w_in": dense(ks[2], D_MODEL, IN_WIDTH),
        "lambda_q1": 0.1 * jax.random.normal(ks[3], (HEAD_DIM,), f32),
        "lambda_k1": 0.1 * jax.random.normal(ks[4], (HEAD_DIM,), f32),
        "lambda_q2": 0.1 * jax.random.normal(ks[5], (HEAD_DIM,), f32),
        "lambda_k2": 0.1 * jax.random.normal(ks[6], (HEAD_DIM,), f32),
        "g_subln": 1.0 + 0.02 * jax.random.normal(ks[7], (V_HEAD_DIM,), f32),
        "w_attn_branch": dense(ks[8], ATTN_V_WIDTH, D_MODEL),
        "w_four_branch": dense(ks[9], FOURIER_WIDTH, D_MODEL),
        "w_out": dense(ks[10], D_MODEL, D_MODEL),
        "g_ffn": 1.0 + 0.02 * jax.random.normal(ks[11], (D_MODEL,), f32),
        "w_gate": dense(ks[12], D_MODEL, D_FF),
        "w_up": dense(ks[13], D_MODEL, D_FF),
        "w_down": dense(ks[14], D_FF, D_MODEL),
        "g_final": 1.0 + 0.02 * jax.random.normal(ks[15], (D_MODEL,), f32),
    }


def reference(x, g_mix, w_in, lambda_q1, lambda_k1, lambda_q2, lambda_k2, g_subln,
              w_attn_branch, w_four_branch, w_out, g_ffn, w_gate, w_up, w_down, g_final):
    b, s, _ = x.shape
    cos, sin = rope_tables(s, x.dtype)
    lam = (jnp.exp(jnp.sum(lambda_q1.astype(jnp.float32) * lambda_k1.astype(jnp.float32)))
           - jnp.exp(jnp.sum(lambda_q2.astype(jnp.float32) * lambda_k2.astype(jnp.float32)))
           + LAMBDA_INIT)

    for _ in range(DEPTH):
        h = rmsnorm(x, g_mix)
        z = h @ w_in
        q = z[..., :Q_END].reshape(b, s, N_DIFF_HEADS, 2, HEAD_DIM)
        k = z[..., Q_END:K_END].reshape(b, s, N_DIFF_HEADS, 2, HEAD_DIM)
        v = z[..., K_END:V_END].reshape(b, s, N_DIFF_HEADS, V_HEAD_DIM)
        f = z[..., V_END:F_END].reshape(b, s, N_FOURIER_GROUPS, FOURIER_GROUP_DIM)
        gate_a = jax.nn.sigmoid(z[..., F_END:GA_END])
        gate_f = jax.nn.sigmoid(z[..., GA_END:GF_END])

        q = partial_rope(q, cos, sin)
        k = partial_rope(k, cos, sin)
        o = diff_attention(q, k, v, lam)
        o = rmsnorm(o, g_subln) * (1.0 - LAMBDA_INIT)
        branch_a = o.reshape(b, s, ATTN_V_WIDTH) @ w_attn_branch

        fm = fourier_mix(f).reshape(b, s, FOURIER_WIDTH)
        branch_f = fm @ w_four_branch

        merged = gate_a * branch_a + gate_f * branch_f
        x = x + merged @ w_out

        h2 = rmsnorm(x, g_ffn)
        x = x + (jax.nn.silu(h2 @ w_gate) * (h2 @ w_up)) @ w_down

    return rmsnorm(x, g_final)
```

```python
import contextlib
import math
import numpy as np
import ml_dtypes
import concourse.bass as bass
import concourse.mybir as mybir
from concourse.bass_utils import run_bass_kernel_spmd

F32 = mybir.dt.float32
BF16 = mybir.dt.bfloat16
AF = mybir.ActivationFunctionType
ALU = mybir.AluOpType
AX = mybir.AxisListType

D = 2048
S_FULL = 4096
T_OWN = 2048
NH = 6
QK_W = 1536
V_W = 1536
F_W = 512
IN_W = 9216
DFF = 5632
LAMBDA_INIT = 0.8 - 0.6 * math.exp(-0.3 * 0)
EPS = 1e-5
ROPE_THETA = 500000.0
ENGS = ['pe', 'act', 'dve', 'pool', 'sp']


class Tile:
    __slots__ = ('name', 'last_w', 'readers')

    def __init__(self, name):
        self.name = name
        self.last_w = None
        self.readers = []


class Op:
    __slots__ = ('eng', 'idx', 'fn', 'waits', 'signal', 'count', 'is_dma', 'sem')


class Sched:
    def __init__(self, nc):
        self.nc = nc
        self.ops = {e: [] for e in ENGS}
        self.waited = {e: {} for e in ENGS}
        self.dma_cnt = {}

    def tile(self, name="t"):
        return Tile(name)

    def tiles(self, n, name="t"):
        return [Tile(f"{name}{i}") for i in range(n)]

    def _add_wait(self, op, d, kind, force=False):
        w = self.waited[op.eng]
        if d.is_dma:
            key = ('dma', d.sem)
            cnt = self.dma_cnt[d.sem]
            if op.is_dma and op.sem == d.sem:
                cnt -= 1
            val = cnt * 16
            if w.get(key, 0) >= val:
                return
            w[key] = val
            op.waits.append((key, val, None))
        else:
            if d.eng == op.eng and not op.is_dma and kind != 'raw' and not force:
                return
            key = ('eng', d.eng)
            if w.get(key, -1) >= d.idx:
                return
            w[key] = d.idx
            d.signal = True
            op.waits.append((key, None, d))

    def _deps(self, op, reads, writes):
        deps = []
        for t in reads:
            if t.last_w is not None:
                deps.append((t.last_w, 'raw'))
        for t in writes:
            if t.last_w is not None:
                deps.append((t.last_w, 'waw'))
            for r in t.readers:
                deps.append((r, 'war'))
        for t in reads:
            t.readers.append(op)
        for t in writes:
            t.last_w = op
            t.readers = []
        for d, kind in deps:
            if d is not op:
                self._add_wait(op, d, kind)

    def _new(self, eng, fn, is_dma, sem):
        o = Op()
        o.eng = eng; o.fn = fn; o.waits = []; o.signal = False; o.is_dma = is_dma; o.sem = sem
        o.count = 0
        o.idx = len(self.ops[eng])
        self.ops[eng].append(o)
        return o

    def op(self, eng, fn, reads=(), writes=()):
        o = self._new(eng, fn, False, None)
        self._deps(o, reads, writes)
        return o

    def dma(self, eng, fn, reads=(), writes=(), sem=None):
        self.dma_cnt[sem] = self.dma_cnt.get(sem, 0) + 1
        o = self._new(eng, fn, True, sem)
        self._deps(o, reads, writes)
        return o

    def barrier(self, tiles):
        deps = []
        for t in tiles:
            if t.last_w is not None:
                deps.append(t.last_w)
            deps.extend(t.readers)
        deps.sort(key=lambda d: -d.idx)
        for e in ENGS:
            o = self._new(e, None, False, None)
            for d in deps:
                self._add_wait(o, d, 'raw', force=True)

    def wait_on(self, eng, target):
        o = self._new(eng, None, False, None)
        self._add_wait(o, target, 'raw', force=True)

    def finish(self, eng, tiles):
        self.op(eng, None, reads=tiles)

    def emit(self):
        nc = self.nc
        for e in ENGS:
            c = 0
            for o in self.ops[e]:
                if o.signal and not o.is_dma:
                    c += 1
                    o.count = c
        with contextlib.ExitStack() as st:
            esem = {e: st.enter_context(nc.semaphore(f"s_{e}")) for e in ENGS}
            dsem = {k: st.enter_context(nc.semaphore(f"d_{k}")) for k in self.dma_cnt}
            block = st.enter_context(nc.Block())

            def run(e, eng):
                for o in self.ops[e]:
                    for key, val, d in o.waits:
                        if key[0] == 'dma':
                            eng.wait_ge(dsem[key[1]], val)
                        else:
                            eng.wait_ge(esem[key[1]], d.count)
                    if o.fn is None:
                        continue
                    ins = o.fn(eng)
                    if o.is_dma:
                        ins.then_inc(dsem[o.sem], 16)
                    elif o.signal:
                        ins.then_inc(esem[e], 1)

            @block.tensor
            def _(eng):
                run('pe', eng)

            @block.scalar
            def _(eng):
                run('act', eng)

            @block.vector
            def _(eng):
                run('dve', eng)

            @block.gpsimd
            def _(eng):
                run('pool', eng)

            @block.sync
            def _(eng):
                run('sp', eng)


def build_nc(dbg=False, stop_after=None):
    nc = bass.Bass("TRN2", target_bir_lowering=False)
    S = Sched(nc)

    def din(name, shape):
        return nc.dram_tensor(name, shape, F32, kind="ExternalInput").ap()

    def dscr(name, shape, dt):
        kind = "ExternalOutput" if (dbg and name in dbg) else "Internal"
        return nc.dram_tensor(name, shape, dt, kind=kind).ap()

    xs = din("xs", [S_FULL, D])
    w_in = din("w_in", [D, IN_W])
    w_attn = din("w_attn", [V_W, D])
    w_four = din("w_four", [F_W, D])
    w_out = din("w_out", [D, D])
    w_gate = din("w_gate", [D, DFF])
    w_up = din("w_up", [D, DFF])
    w_down = din("w_down", [DFF, D])
    g_mix = din("g_mix", [1, D])
    g_ffn = din("g_ffn", [1, D])
    g_final = din("g_final", [1, D])
    g_sub = din("g_sub", [128, 2])
    lams = din("lams", [4, 128])
    rope_c = din("rope_c", [32, S_FULL])
    rope_s = din("rope_s", [32, S_FULL])
    perm = din("perm", [32, 32])
    ident = din("ident", [128, 128])
    dft_c = nc.dram_tensor("dft_c", [S_FULL, T_OWN], BF16, kind="ExternalInput").ap()
    dft_s = nc.dram_tensor("dft_s", [S_FULL, T_OWN], BF16, kind="ExternalInput").ap()
    dft_cd = din("dft_cd", [128, 128])
    dft_nsd = din("dft_nsd", [128, 128])
    out = nc.dram_tensor("out", [T_OWN, D], F32, kind="ExternalOutput").ap()

    hT_scr = dscr("hT_scr", [8, 128, 16, 512], BF16)
    qT_scr = dscr("qT_scr", [12, 128, T_OWN], BF16)
    kT_scr = dscr("kT_scr", [12, 128, S_FULL], BF16)
    v_scr = dscr("v_scr", [S_FULL, V_W], BF16)
    f_scr = dscr("f_scr", [S_FULL, F_W], BF16)
    ga_scr = dscr("ga_scr", [4, 128, 16, 512], BF16)
    gf_scr = dscr("gf_scr", [4, 128, 16, 512], BF16)
    oT_scr = dscr("oT_scr", [4, 128, 12, 512], BF16)
    yT_scr = dscr("yT_scr", [4, 128, 4, 512], BF16)
    mT_scr = dscr("mT_scr", [4, 128, 16, 512], BF16)
    x2_scr = dscr("x2_scr", [T_OWN, D], F32)
    h2T_scr = dscr("h2T_scr", [4, 128, 16, 512], BF16)
    aT_scr = dscr("aT_scr", [4, 128, 44, 512], BF16)
    x3_scr = dscr("x3_scr", [T_OWN, D], F32)

    cv_src = {'wa': w_attn, 'wf': w_four, 'wo': w_out, 'wg': w_gate, 'wu': w_up, 'wd': w_down}
    cv_dst = {k: dscr("cvb_" + k, list(v.shape), BF16) for k, v in cv_src.items()}
    cv_tiles = {k: S.tiles(v.shape[0] // 128, "cv_" + k) for k, v in cv_src.items()}

    cv_sem = {'wa': 'cv_e', 'wf': 'cv_e', 'wo': 'cv_e', 'wg': 'cv_f', 'wu': 'cv_f', 'wd': 'cv_g'}

    cv_list = [(k, r) for k in ['wa', 'wf', 'wo', 'wg', 'wu'] for r in range(cv_src[k].shape[0] // 128)]
    cv_list2 = [('wd', r) for r in range(cv_src['wd'].shape[0] // 128)]
    cv_pos = [0, 0]

    def issue_conversions(n, which=0):
        lst = cv_list if which == 0 else cv_list2
        for k, r in lst[cv_pos[which]:cv_pos[which] + n]:
            dma('pool', cv_dst[k][r * 128:(r + 1) * 128, :], cv_src[k][r * 128:(r + 1) * 128, :], [], [cv_tiles[k][r]], cv_sem[k])
        cv_pos[which] += n

    def load_wb(tile3, key, r0, k_tot, c0, cn, writes, sem, ksplit=4):
        wb = cv_dst[key]
        for k0 in range(0, k_tot, ksplit):
            k1 = min(k_tot, k0 + ksplit)
            dma('sp', tile3[:, k0:k1, :],
                wb[r0 + k0 * 128:r0 + k1 * 128, c0:c0 + cn].rearrange("(k p) n -> p k n", p=128),
                cv_tiles[key][(r0 // 128) + k0:(r0 // 128) + k1], writes, sem)

    def dtiles(n):
        return S.tiles(n, "dr")

    d_hT = dtiles(8); d_qT = dtiles(4); d_kT = dtiles(8); d_v = dtiles(8); d_f = dtiles(8)
    d_ga = dtiles(4); d_gf = dtiles(4); d_oT = dtiles(4); d_yT = dtiles(4); d_mT = dtiles(4)
    d_x2 = dtiles(4); d_h2T = dtiles(4); d_aT = dtiles(4); d_x3 = dtiles(4); d_out = dtiles(4)

    def mm(o, l, r, st, sp, reads, writes):
        S.op('pe', lambda e: e.matmul(o, lhsT=l, rhs=r, start=st, stop=sp), reads, writes)

    def tr(o, i, idn, reads, writes):
        S.op('pe', lambda e: e.transpose(out=o, in_=i, identity=idn), reads, writes)

    def act(o, i, func, reads, writes, scale=1.0, bias=None, accum=None):
        kw = {}
        if bias is not None:
            kw['bias'] = bias
        if accum is not None:
            kw['accum_out'] = accum
        S.op('act', lambda e: e.activation(out=o, in_=i, func=func, scale=scale, **kw), reads, writes)

    def tt(eng, o, a, b, op, reads, writes):
        S.op(eng, lambda e: e.tensor_tensor(out=o, in0=a, in1=b, op=op), reads, writes)

    def ts(eng, o, a, s1, s2, op0, op1, reads, writes):
        if s2 is None:
            S.op(eng, lambda e: e.tensor_scalar(out=o, in0=a, scalar1=s1, scalar2=None, op0=op0), reads, writes)
        else:
            S.op(eng, lambda e: e.tensor_scalar(out=o, in0=a, scalar1=s1, scalar2=s2, op0=op0, op1=op1), reads, writes)

    def stt(eng, o, a, s, b, op0, op1, reads, writes):
        S.op(eng, lambda e: e.scalar_tensor_tensor(out=o, in0=a, scalar=s, in1=b, op0=op0, op1=op1), reads, writes)

    def cp(eng, o, i, reads, writes):
        if eng == 'act':
            S.op('act', lambda e: e.copy(out=o, in_=i), reads, writes)
        else:
            S.op(eng, lambda e: e.tensor_copy(out=o, in_=i), reads, writes)

    def recip(o, i, reads, writes):
        S.op('dve', lambda e: e.reciprocal(out=o, in_=i), reads, writes)

    def dma(q, o, i, reads, writes, sem):
        S.dma(q, lambda e: e.dma_start(out=o, in_=i), reads, writes, sem)

    def bcast_rows(ap_row, n):
        return bass.AP(ap_row.tensor, ap_row.offset, [[0, 128], [1, n]])

    def load_kblock(q, tile3, src3, k_tot, t0, tn, reads, writes, sem, ksplit=4):
        for k0 in range(0, k_tot, ksplit):
            k1 = min(k_tot, k0 + ksplit)
            dma(q, tile3[:, k0:k1, :], src3[k0:k1, :, t0:t0 + tn].rearrange("k p t -> p k t"), reads, writes, sem)

    def store_kblock(q, dst3, tile3, k_tot, t0, tn, reads, writes, sem, ksplit=4):
        for k0 in range(0, k_tot, ksplit):
            k1 = min(k_tot, k0 + ksplit)
            dma(q, dst3[k0:k1, :, t0:t0 + tn].rearrange("k p t -> p k t"), tile3[:, k0:k1, :], reads, writes, sem)

    def load_blk(q, tile3, scr4, blk, k_tot, reads, writes, sem, ksplit=8):
        for k0 in range(0, k_tot, ksplit):
            k1 = min(k_tot, k0 + ksplit)
            dma(q, tile3[:, k0:k1, :], scr4[blk, :, k0:k1, :], reads, writes, sem)

    def store_blk(q, scr4, tile3, blk, k_tot, reads, writes, sem, ksplit=8):
        for k0 in range(0, k_tot, ksplit):
            k1 = min(k_tot, k0 + ksplit)
            dma(q, scr4[blk, :, k0:k1, :], tile3[:, k0:k1, :], reads, writes, sem)

    def load_w(tile3, w2, r0, k_tot, c0, cn, writes, sem, ksplit=4):
        for k0 in range(0, k_tot, ksplit):
            k1 = min(k_tot, k0 + ksplit)
            dma('pool', tile3[:, k0:k1, :],
                w2[r0 + k0 * 128:r0 + k1 * 128, c0:c0 + cn].rearrange("(k p) n -> p k n", p=128), [], writes, sem)

    es = contextlib.ExitStack()

    def sb(name, shape, dt):
        return es.enter_context(nc.sbuf_tensor(name, shape, dt))

    def ps(name, shape, dt=F32):
        return es.enter_context(nc.psum_tensor(name, shape, dt))

    phase_tiles = []

    def T(name="t"):
        t = S.tile(name)
        phase_tiles.append(t)
        return t

    def Ts(n, name="t"):
        return [T(name) for _ in range(n)]

    def end_phase():
        nonlocal es
        S.barrier(phase_tiles)
        phase_tiles.clear()
        es.close()
        es = contextlib.ExitStack()

    done = [False]

    def stop(name):
        if stop_after == name:
            done[0] = True
        return done[0]

    def norm_sq(xt_ap, ssq, junk, t_x, t_ss, t_junk):
        act(junk[:], xt_ap, AF.Square, [t_x], [t_junk, t_ss], accum=ssq[:, 0:1])

    def norm_rest(xt_ap, gb, idb, ssq, hb, pT, t_x, t_g, t_id, t_ss, t_hb, t_pT):
        ts('dve', ssq[:, 1:2], ssq[:, 0:1], 1.0 / D, EPS, ALU.mult, ALU.add, [t_ss], [t_ss])
        act(ssq[:, 2:3], ssq[:, 1:2], AF.Sqrt, [t_ss], [t_ss])
        recip(ssq[:, 3:4], ssq[:, 2:3], [t_ss], [t_ss])
        stt('dve', hb[:], xt_ap, ssq[:, 3:4], gb[:], ALU.mult, ALU.mult, [t_x, t_ss, t_g], [t_hb])
        for k in range(16):
            tr(pT[:, k, :], hb[:, k * 128:(k + 1) * 128], idb[:], [t_hb, t_id], [t_pT])

    es_ab = contextlib.ExitStack()
    hT_own = es_ab.enter_context(nc.sbuf_tensor("hT_own", [128, 16, T_OWN], BF16))
    t_hown = S.tiles(4, "hown")
    if True:
        gb = sb("a_gb", [128, D], F32); t_g = T()
        idb = sb("a_id", [128, 128], BF16); t_id = T()
        xts = [sb(f"a_x{i}", [128, D], F32) for i in range(3)]; t_xs = Ts(3)
        junk = sb("a_junk", [128, D], F32); t_junk = T()
        hb = [sb(f"a_hb{i}", [128, D], BF16) for i in range(2)]; t_hb = Ts(2)
        ssq = [sb(f"a_ss{i}", [128, 4], F32) for i in range(2)]; t_ss = Ts(2)
        hTb = [sb(f"a_hT{i}", [128, 16, 512], BF16) for i in range(2)]; t_hT = Ts(2)
        pT = [ps(f"a_pT{i}", [128, 16, 128], BF16) for i in range(2)]; t_pT = Ts(2)
        dma('sp', gb[:], bcast_rows(g_mix, D), [], [t_g], 'a_g')
        dma('pool', idb[:], ident, [], [t_id], 'a_id')
        NT = S_FULL // 128

        def a_load(i):
            dma('sp', xts[i % 3][:], xs[i * 128:(i + 1) * 128, :], [], [t_xs[i % 3]], f'a_x{i % 3}')

        def a_part2(i):
            b = i % 2; blk = i // 4; sub = i % 4
            eng = 'act' if i % 2 else 'dve'
            if blk < 4:
                cp(eng, hT_own[:, :, i * 128:(i + 1) * 128], pT[b][:], [t_pT[b]], [t_hown[blk]])
            else:
                cp(eng, hTb[blk % 2][:, :, sub * 128:(sub + 1) * 128], pT[b][:], [t_pT[b]], [t_hT[blk % 2]])
                if sub == 3:
                    store_blk('sp', hT_scr, hTb[blk % 2], blk, 16, [t_hT[blk % 2]], [d_hT[blk]], f'a_st{blk % 2}')

        a_load(0)
        a_load(1)
        norm_sq(xts[0][:], ssq[0], junk, t_xs[0], t_ss[0], t_junk)
        for i in range(NT):
            if i + 2 < NT:
                a_load(i + 2)
            if i + 1 < NT:
                j = i + 1
                norm_sq(xts[j % 3][:], ssq[j % 2], junk, t_xs[j % 3], t_ss[j % 2], t_junk)
            b = i % 2
            norm_rest(xts[i % 3][:], gb, idb, ssq[b], hb[b], pT[b], t_xs[i % 3], t_g, t_id, t_ss[b], t_hb[b], t_pT[b])
            if i > 0:
                a_part2(i - 1)
        a_part2(NT - 1)
        end_phase()

    if not stop('A'):
        wsl = [sb(f"b_w{i}", [128, 16, 512], BF16) for i in range(2)]; t_w = Ts(2)
        hbk = [sb(f"b_h{i}", [128, 16, 512], BF16) for i in range(2)]; t_h = Ts(2)
        stg = [sb(f"b_s{i}", [128, 4, 512], BF16) for i in range(2)]; t_stg = [Ts(4) for _ in range(2)]
        rc = sb("b_rc", [32, S_FULL], F32); rs = sb("b_rs", [32, S_FULL], F32); t_rope = T()
        pmb = sb("b_pm", [32, 32], BF16); t_pm = T()
        r1 = [sb(f"b_r1{i}", [32, 512], F32) for i in range(2)]; t_r1 = Ts(2)
        r2 = [sb(f"b_r2{i}", [32, 512], F32) for i in range(2)]; t_r2 = Ts(2)
        pz = [ps(f"b_pz{i}", [128, 512]) for i in range(6)]; t_pz = Ts(6)
        psw = [ps(f"b_psw{i}", [32, 512]) for i in range(2)]; t_psw = Ts(2)
        dma('sp', rc[:], rope_c, [], [t_rope], 'b_rope')
        dma('sp', rs[:], rope_s, [], [t_rope], 'b_rope')
        dma('pool', pmb[:], perm, [], [t_pm], 'b_pm')

        kinds = ['q'] * 3 + ['k'] * 3 + ['v'] * 3 + ['f'] + ['ga'] * 4 + ['gf'] * 4
        steps = []
        for sl in range(18):
            nblk = 8 if kinds[sl] in ('k', 'v', 'f') else 4
            for tb in range(nblk):
                steps.append((sl, tb))
        lsteps = [si for si, (sl, tb) in enumerate(steps) if tb >= 4]
        lslot = {si: n % 2 for n, si in enumerate(lsteps)}
        l_issued = 0
        l_done = 0

        def b_load_w(sl):
            load_w(wsl[sl % 2], w_in, 0, 16, sl * 512, 512, [t_w[sl % 2]], f'b_w{sl % 2}')

        def b_issue_loads():
            nonlocal l_issued
            while l_issued < len(lsteps) and l_issued < l_done + 2:
                si_ = lsteps[l_issued]
                j = lslot[si_]
                tb_ = steps[si_][1]
                load_blk('sp', hbk[j], hT_scr, tb_, 16, [d_hT[tb_]], [t_h[j]], f'b_h{j}')
                l_issued += 1

        b_load_w(0)
        pzi = 0
        swi = 0
        for si, (sl, tb) in enumerate(steps):
            if tb == 0 and sl + 1 < 18:
                b_load_w(sl + 1)
            b_issue_loads()
            kind = kinds[sl]
            W = wsl[sl % 2]; tw = t_w[sl % 2]
            if tb < 4:
                th = t_hown[tb]

                def Hs(k, a_, b_, tb=tb):
                    return hT_own[:, k, tb * 512 + a_:tb * 512 + b_]
            else:
                th = t_h[lslot[si]]

                def Hs(k, a_, b_, j=lslot[si]):
                    return hbk[j][:, k, a_:b_]
            sg = stg[si % 2]; tsg = t_stg[si % 2]
            pending = None
            for c in range(4):
                P = pz[pzi % 6]; tp = t_pz[pzi % 6]; pzi += 1
                if kind in ('v', 'f'):
                    for k in range(16):
                        mm(P[:], Hs(k, c * 128, (c + 1) * 128), W[:, k, :], k == 0, k == 15, [th, tw], [tp])
                    cp('act' if c % 2 else 'dve', sg[:, c, :], P[:], [tp], [tsg[c]])
                else:
                    for k in range(16):
                        mm(P[:], W[:, k, c * 128:(c + 1) * 128], Hs(k, 0, 512), k == 0, k == 15, [th, tw], [tp])
                    if pending is not None:
                        pending(); pending = None
                    if kind in ('ga', 'gf'):
                        act(sg[:, c, :], P[:], AF.Sigmoid, [tp], [tsg[c]])
                    else:
                        cp('act', sg[:, c, :], P[:], [tp], [tsg[c]])

                        def rope(c=c, P=P, tp=tp):
                            nonlocal swi
                            SW = psw[swi % 2]; tsw = t_psw[swi % 2]
                            R1 = r1[swi % 2]; tr1 = t_r1[swi % 2]
                            R2 = r2[swi % 2]; tr2 = t_r2[swi % 2]
                            swi += 1
                            mm(SW[:], pmb[:], sg[0:32, c, :], True, True, [tsg[c], t_pm], [tsw])
                            tt('dve', R1[:], SW[:], rs[:, tb * 512:(tb + 1) * 512], ALU.mult, [tsw, t_rope], [tr1])
                            tt('dve', R2[:], P[0:32, :], rc[:, tb * 512:(tb + 1) * 512], ALU.mult, [tp, t_rope], [tr2])
                            tt('dve', sg[0:32, c, :], R1[:], R2[:], ALU.add, [tr1, tr2], [tsg[c]])
                        pending = rope
            if pending is not None:
                pending(); pending = None
            if tb >= 4:
                l_done += 1
            tsg = list(tsg)
            if kind == 'q':
                dma('sp', qT_scr[sl * 4:(sl + 1) * 4, :, tb * 512:(tb + 1) * 512].rearrange("k p t -> p k t"), sg[:], tsg, [d_qT[tb]], f'b_st{si % 2}')
            elif kind == 'k':
                c0 = (sl - 3) * 4
                dma('sp', kT_scr[c0:c0 + 4, :, tb * 512:(tb + 1) * 512].rearrange("k p t -> p k t"), sg[:], tsg, [d_kT[tb]], f'b_st{si % 2}')
            elif kind == 'v':
                c0 = (sl - 6) * 512
                dma('sp', v_scr[tb * 512:(tb + 1) * 512, c0:c0 + 512].rearrange("(c p) n -> p c n", p=128), sg[:], tsg, [d_v[tb]], f'b_st{si % 2}')
            elif kind == 'f':
                dma('sp', f_scr[tb * 512:(tb + 1) * 512, :].rearrange("(c p) n -> p c n", p=128), sg[:], tsg, [d_f[tb]], f'b_st{si % 2}')
            elif kind == 'ga':
                c0 = (sl - 10) * 4
                dma('sp', ga_scr[tb, :, c0:c0 + 4, :], sg[:], tsg, [d_ga[tb]], f'b_st{si % 2}')
            else:
                c0 = (sl - 14) * 4
                dma('sp', gf_scr[tb, :, c0:c0 + 4, :], sg[:], tsg, [d_gf[tb]], f'b_st{si % 2}')
        phase_tiles.extend(t_hown)
        end_phase()
    es_ab.close()

    if not stop('A') and not stop('B'):
        kT = [sb(f"c_k{i}", [128, 2, S_FULL], BF16) for i in range(2)]; t_k = Ts(2)
        vh = [sb(f"c_v{i}", [128, 32, 256], BF16) for i in range(2)]; t_v = Ts(2)
        qh = [sb(f"c_q{i}", [128, 2, 512], BF16) for i in range(2)]; t_q = Ts(2)
        ones = sb("c_ones", [128, 128], BF16); t_ones = T()
        pT4 = sb("c_p4", [128, 8, 512], BF16); t_p = Ts(8)
        gs2 = [sb(f"c_gs2{i}", [128, 2, 512], BF16) for i in range(2)]; t_gs2 = Ts(2)
        gs1 = [sb(f"c_gs1{i}", [128, 512], BF16) for i in range(2)]; t_gs1 = Ts(2)
        lam_in = sb("c_lam", [128, 4, 128], F32); t_lam = T()
        lam_w = sb("c_lamw", [128, 2, 128], F32)
        lam_s = sb("c_lams", [128, 8], F32)
        gsb = sb("c_gs", [128, 2], F32); gss = sb("c_gss", [128, 2], F32); t_gs = T()
        epsb = sb("c_eps", [128, 1], F32)
        rr = sb("c_rr", [128, 512], F32); t_rr = T()
        oa = sb("c_oa", [128, 2, 512], F32); t_oa = T()
        ob = sb("c_ob", [128, 2, 512], F32); t_ob = T()
        oo = sb("c_oo", [128, 2, 512], F32); t_oo = T()
        sq = sb("c_sq", [128, 2, 512], BF16); t_sq = T()
        rstd = sb("c_rstd", [128, 512], F32); t_rstd = T()
        ssb = sb("c_ssb", [128, 512], F32); t_ssb = T()
        on = [sb(f"c_on{i}", [128, 2, 512], BF16) for i in range(2)]; t_on = Ts(2)
        psc = [ps(f"c_sc{i}", [128, 512]) for i in range(3)]; t_sc = Ts(3)
        pacc = [[ps(f"c_acc{s}{j}", [128, 512]) for j in range(2)] for s in range(2)]
        t_acc = [Ts(2) for _ in range(2)]
        psm = ps("c_psm", [128, 512]); t_psm = T()

        S.op('dve', lambda e: e.memset(ones[:], 1.0), [], [t_ones])
        S.op('dve', lambda e: e.memset(epsb[:], EPS), [], [t_gs])
        dma('sp', lam_in[:].rearrange("p a b -> p (a b)"), bass.AP(lams.tensor, 0, [[0, 128], [1, 512]]), [], [t_lam], 'c_lam')
        dma('sp', gsb[:], g_sub, [], [t_gs], 'c_gs')
        tt('dve', lam_w[:, 0, :], lam_in[:, 0, :], lam_in[:, 1, :], ALU.mult, [t_lam], [t_lam])
        tt('dve', lam_w[:, 1, :], lam_in[:, 2, :], lam_in[:, 3, :], ALU.mult, [t_lam], [t_lam])
        S.op('dve', lambda e: e.reduce_sum(out=lam_s[:, 0:1], in_=lam_w[:, 0, :], axis=AX.X), [t_lam], [t_lam])
        S.op('dve', lambda e: e.reduce_sum(out=lam_s[:, 1:2], in_=lam_w[:, 1, :], axis=AX.X), [t_lam], [t_lam])
        act(lam_s[:, 2:4], lam_s[:, 0:2], AF.Exp, [t_lam], [t_lam])
        tt('dve', lam_s[:, 4:5], lam_s[:, 3:4], lam_s[:, 2:3], ALU.subtract, [t_lam], [t_lam])
        ts('dve', lam_s[:, 5:6], lam_s[:, 4:5], -LAMBDA_INIT, None, ALU.add, None, [t_lam], [t_lam])
        ts('dve', gss[:], gsb[:], 1.0 - LAMBDA_INIT, None, ALU.mult, None, [t_gs], [t_gs])
        neg_lam = lam_s[:, 5:6]
        scale = 1.0 / math.sqrt(128.0)

        csteps = [(h, qb) for h in range(NH) for qb in range(4)]

        def c_load_kv(h):
            b = h % 2
            for c in range(2):
                for t0 in range(0, S_FULL, 1024):
                    dma('sp', kT[b][:, c, t0:t0 + 1024], kT_scr[2 * h + c, :, t0:t0 + 1024], d_kT, [t_k[b]], f'c_k{b}')
            for c0 in range(0, 32, 8):
                dma('sp', vh[b][:, c0:c0 + 8, :],
                    v_scr[c0 * 128:(c0 + 8) * 128, h * 256:(h + 1) * 256].rearrange("(c p) e -> p c e", p=128),
                    d_v, [t_v[b]], f'c_v{b}')

        def c_load_q(si):
            h, qb = csteps[si]
            dma('sp', qh[si % 2][:], qT_scr[2 * h:2 * h + 2, :, qb * 512:(qb + 1) * 512].rearrange("k p t -> p k t"),
                [d_qT[qb]], [t_q[si % 2]], f'c_q{si % 2}')

        c_load_kv(0)
        c_load_q(0)
        items = [(si, c, kc) for si in range(len(csteps)) for c in range(2) for kc in range(32)]

        def rec_S(i):
            si, c, kc = items[i]
            h, qb = csteps[si]
            mm(psc[i % 3][:], kT[h % 2][:, c, kc * 128:(kc + 1) * 128], qh[si % 2][:, c, :], True, True,
               [t_k[h % 2], t_q[si % 2]], [t_sc[i % 3]])

        def epi1a(si):
            for j in range(2):
                stt('dve', oo[:, j, :], ob[:, j, :], neg_lam, oa[:, j, :], ALU.mult, ALU.add, [t_ob, t_oa, t_lam], [t_oo])

        def epi1b(si):
            act(sq[:].rearrange("p a b -> p (a b)"), oo[:].rearrange("p a b -> p (a b)"), AF.Square, [t_oo], [t_sq])

        def epi2a(si, pb, tpb):
            for j in range(2):
                mm(pb[:], ones[:], sq[:, j, :], j == 0, j == 1, [t_ones, t_sq], [tpb])
            cp('dve', ssb[:], pb[:], [tpb], [t_ssb])

        def epi2b(si):
            act(rstd[:], ssb[:], AF.Ln, [t_ssb, t_gs], [t_rstd], scale=1.0 / 256.0, bias=epsb[:, 0:1])
            act(rstd[:], rstd[:], AF.Exp, [t_rstd], [t_rstd], scale=-0.5)

        def epi2c(si):
            h, qb = csteps[si]
            ON = on[si % 2]; ton = t_on[si % 2]
            for j in range(2):
                stt('dve', ON[:, j, :], oo[:, j, :], gss[:, j:j + 1], rstd[:], ALU.mult, ALU.mult, [t_oo, t_gs, t_rstd], [ton])
            dma('sp', oT_scr[qb, :, 2 * h:2 * h + 2, :], ON[:], [ton], [d_oT[qb]], f'c_st{si % 2}')

        def evac_a(si, c):
            act(rr[:], psm[:], AF.Ln, [t_psm], [t_rr])
            act(rr[:], rr[:], AF.Exp, [t_rr], [t_rr], scale=-1.0)

        def evac_b(si, c):
            dst = oa if c == 0 else ob
            tdst = t_oa if c == 0 else t_ob
            for j in range(2):
                tt('dve', dst[:, j, :], pacc[c][j][:], rr[:], ALU.mult, [t_acc[c][j], t_rr], [tdst])

        rec_S(0)
        rec_S(1)
        n_items = len(items)
        pending_evac = None
        pending_ones = None
        for i, (si, c, kc) in enumerate(items):
            h, qb = csteps[si]
            if i + 2 < n_items:
                rec_S(i + 2)
            slot = i % 8
            PT = pT4[:, slot, :]; tpp = t_p[slot]
            V = vh[h % 2]; tv = t_v[h % 2]
            ACC = pacc[c]; tacc = t_acc[c]
            act(PT, psc[i % 3][:], AF.Exp, [t_sc[i % 3]], [tpp], scale=scale)
            pace_op = S.ops['act'][-1]
            mm(ACC[0][:], V[:, kc, 0:128], PT, kc == 0, kc == 31, [tv, tpp], [tacc[0]])
            mm(ACC[1][:], V[:, kc, 128:256], PT, kc == 0, kc == 31, [tv, tpp], [tacc[1]])
            if c == 0 and kc == 0:
                if qb == 0 and h + 1 < NH:
                    c_load_kv(h + 1)
                if si + 1 < len(csteps):
                    c_load_q(si + 1)
            if c == 0 and kc == 16:
                S.wait_on('pool', pace_op)
                issue_conversions((len(cv_list) + len(csteps) - 1) // len(csteps))
            if kc % 4 == 1 and pending_ones is not None:
                pending_ones(); pending_ones = None
            if kc == 4 and pending_evac is not None:
                evac_a(*pending_evac)
            if kc == 6 and pending_evac is not None:
                evac_b(*pending_evac); pending_evac = None
            if kc % 4 == 3:
                base = slot - 3
                gp = (kc // 4) % 2
                tgrp = [t_p[base + u] for u in range(4)]
                tt('dve', gs2[gp][:], pT4[:, base:base + 2, :], pT4[:, base + 2:base + 4, :], ALU.add, tgrp, [t_gs2[gp]])
                tt('dve', gs1[gp][:], gs2[gp][:, 0, :], gs2[gp][:, 1, :], ALU.add, [t_gs2[gp]], [t_gs1[gp]])

                def ones_mm(gp=gp, kc=kc):
                    mm(psm[:], ones[:], gs1[gp][:], kc == 3, kc == 31, [t_ones, t_gs1[gp]], [t_psm])
                pending_ones = ones_mm
            if kc == 31:
                pending_evac = (si, c)
            if si > 0 and c == 0:
                if kc == 9:
                    epi1a(si - 1)
                elif kc == 14:
                    epi1b(si - 1)
                elif kc == 20:
                    epi2a(si - 1, psc[i % 3], t_sc[i % 3])
                elif kc == 26:
                    epi2b(si - 1)
                elif kc == 30:
                    epi2c(si - 1)
        pending_ones()
        evac_a(*pending_evac)
        evac_b(*pending_evac)
        if dbg and 'dbg_small' in dbg:
            dsm = nc.dram_tensor("dbg_small", [128, 10], F32, kind="ExternalOutput").ap()
            doa = nc.dram_tensor("dbg_oa", [128, 2048], F32, kind="ExternalOutput").ap()
            drr = nc.dram_tensor("dbg_rr", [128, 512], F32, kind="ExternalOutput").ap()
            t_dbg = S.tile("dbg")
            dma('sp', dsm[:, 0:8], lam_s[:], [t_lam], [t_dbg], 'dbg')
            dma('sp', dsm[:, 8:10], gss[:], [t_gs], [t_dbg], 'dbg')
            dma('sp', doa[:, 0:1024], oa[:].rearrange("p a b -> p (a b)"), [t_oa], [t_dbg], 'dbg')
            dma('sp', doa[:, 1024:2048], ob[:].rearrange("p a b -> p (a b)"), [t_ob], [t_dbg], 'dbg')
            dma('sp', drr[:], rr[:], [t_rr], [t_dbg], 'dbg')
            dk = nc.dram_tensor("dbg_k", [128, 2 * S_FULL], BF16, kind="ExternalOutput").ap()
            dv = nc.dram_tensor("dbg_v", [128, 32 * 256], BF16, kind="ExternalOutput").ap()
            dq = nc.dram_tensor("dbg_q", [128, 2 * 512], BF16, kind="ExternalOutput").ap()
            dma('sp', dk[:], kT[1][:].rearrange("p a b -> p (a b)"), [t_k[1]], [t_dbg], 'dbg')
            dma('sp', dv[:], vh[1][:].rearrange("p a b -> p (a b)"), [t_v[1]], [t_dbg], 'dbg')
            dma('sp', dq[:], qh[1][:].rearrange("p a b -> p (a b)"), [t_q[1]], [t_dbg], 'dbg')
            phase_tiles.append(t_dbg)
        epi1a(len(csteps) - 1)
        epi1b(len(csteps) - 1)
        epi2a(len(csteps) - 1, psc[0], t_sc[0])
        epi2b(len(csteps) - 1)
        epi2c(len(csteps) - 1)
        end_phase()

    if not stop('A') and not stop('B') and not stop('C'):
        fsb = sb("d_f", [128, 32, 512], BF16); t_f = T()
        tab = [sb(f"d_t{i}", [128, 16, 512], BF16) for i in range(8)]; t_tab = Ts(8)
        cd = sb("d_cd", [128, 128], BF16); nsd = sb("d_nsd", [128, 128], BF16); t_cd = T()
        ab = [sb(f"d_ab{i}", [128, 2, 512], BF16) for i in range(2)]; t_ab = Ts(2)
        yst = [sb(f"d_y{i}", [128, 512], BF16) for i in range(2)]; t_y = Ts(2)
        pab = [ps(f"d_pab{i}", [128, 512]) for i in range(4)]; t_pab = Ts(4)
        py = [ps(f"d_py{i}", [128, 512]) for i in range(2)]; t_py = Ts(2)
        for c0 in range(0, 32, 8):
            dma('sp', fsb[:, c0:c0 + 8, :], f_scr[c0 * 128:(c0 + 8) * 128, :].rearrange("(c p) n -> p c n", p=128), d_f, [t_f], 'd_f')
        cdf = sb("d_cdf", [128, 2, 128], F32); t_cdf = T()
        dma('sp', cdf[:, 0, :], dft_cd, [], [t_cdf], 'd_cd')
        dma('sp', cdf[:, 1, :], dft_nsd, [], [t_cdf], 'd_cd')
        cp('dve', cd[:], cdf[:, 0, :], [t_cdf], [t_cd])
        cp('dve', nsd[:], cdf[:, 1, :], [t_cdf], [t_cd])
        pieces = [(ob_, w, hf) for ob_ in range(4) for w in range(2) for hf in range(2)]

        def d_load(pidx):
            ob_, w, hf = pieces[pidx]
            src = dft_c if w == 0 else dft_s
            for k0 in range(0, 16, 8):
                r0 = hf * 2048 + k0 * 128
                dma('sp', tab[pidx % 8][:, k0:k0 + 8, :],
                    src[r0:r0 + 1024, ob_ * 512:(ob_ + 1) * 512].rearrange("(k p) n -> p k n", p=128),
                    [], [t_tab[pidx % 8]], f'd_t{pidx % 8}')

        for p_ in range(4):
            d_load(p_)
        gi = 0
        for ob_ in range(4):
            if ob_ + 1 < 4:
                for p_ in range(4):
                    d_load((ob_ + 1) * 4 + p_)
            for g in range(4):
                for w in range(2):
                    P = pab[(gi * 2 + w) % 4]; tp = t_pab[(gi * 2 + w) % 4]
                    for hf in range(2):
                        pidx = ob_ * 4 + w * 2 + hf
                        TB = tab[pidx % 8]; ttb = t_tab[pidx % 8]
                        for k in range(16):
                            sc_ = hf * 16 + k
                            mm(P[:], fsb[:, sc_, g * 128:(g + 1) * 128], TB[:, k, :], sc_ == 0, sc_ == 31, [t_f, ttb], [tp])
                    cp('act' if w else 'dve', ab[gi % 2][:, w, :], P[:], [tp], [t_ab[gi % 2]])
                PY = py[gi % 2]; tpy = t_py[gi % 2]
                mm(PY[:], cd[:], ab[gi % 2][:, 0, :], True, False, [t_cd, t_ab[gi % 2]], [tpy])
                mm(PY[:], nsd[:], ab[gi % 2][:, 1, :], False, True, [t_cd, t_ab[gi % 2]], [tpy])
                cp('act', yst[gi % 2][:], PY[:], [tpy], [t_y[gi % 2]])
                dma('sp', yT_scr[ob_, :, g, :], yst[gi % 2][:], [t_y[gi % 2]], [d_yT[ob_]], f'd_st{gi % 2}')
                gi += 1
        end_phase()

    if not any(stop(x) for x in 'ABCD'):
        wa = sb("e_wa", [128, 12, D], BF16); t_wa = T()
        wf = sb("e_wf", [128, 4, D], BF16); t_wf = T()
        onb = [sb(f"e_on{i}", [128, 12, 512], BF16) for i in range(2)]; t_onb = Ts(2)
        yb = [sb(f"e_y{i}", [128, 4, 512], BF16) for i in range(2)]; t_yb = Ts(2)
        gab = [sb(f"e_ga{i}", [128, 512], BF16) for i in range(4)]; t_gab = Ts(4)
        gfb = [sb(f"e_gf{i}", [128, 512], BF16) for i in range(4)]; t_gfb = Ts(4)
        m1 = [sb(f"e_m1{i}", [128, 512], F32) for i in range(2)]; t_m1 = Ts(2)
        m2 = [sb(f"e_m2{i}", [128, 512], F32) for i in range(2)]; t_m2 = Ts(2)
        mst = [sb(f"e_ms{i}", [128, 16, 512], BF16) for i in range(2)]; t_mst = Ts(2)
        pa = [ps(f"e_pa{i}", [128, 512]) for i in range(3)]; t_pa = Ts(3)
        pf = [ps(f"e_pf{i}", [128, 512]) for i in range(3)]; t_pf = Ts(3)
        load_wb(wf, 'wf', 0, 4, 0, D, [t_wf], 'e_wf', ksplit=2)
        load_wb(wa, 'wa', 0, 12, 0, D, [t_wa], 'e_wa', ksplit=2)

        def e_load(tb):
            b = tb % 2
            load_blk('sp', onb[b], oT_scr, tb, 12, [d_oT[tb]], [t_onb[b]], f'e_on{b}', ksplit=12)
            load_blk('sp', yb[b], yT_scr, tb, 4, [d_yT[tb]], [t_yb[b]], f'e_y{b}')

        e_load(0)
        ci = 0
        for tb in range(4):
            if tb + 1 < 4:
                e_load(tb + 1)
            b = tb % 2
            for dc in range(16):
                PA = pa[ci % 3]; tpa = t_pa[ci % 3]
                PF = pf[ci % 3]; tpf = t_pf[ci % 3]
                M1 = m1[ci % 2]; tm1 = t_m1[ci % 2]
                M2 = m2[ci % 2]; tm2 = t_m2[ci % 2]
                GA = gab[ci % 4]; tga = t_gab[ci % 4]
                GF = gfb[ci % 4]; tgf = t_gfb[ci % 4]
                dma('sp', GA[:], ga_scr[tb, :, dc, :], [d_ga[tb]], [tga], f'e_ga{ci % 4}')
                dma('sp', GF[:], gf_scr[tb, :, dc, :], [d_gf[tb]], [tgf], f'e_gf{ci % 4}')
                ci += 1
                for k in range(12):
                    mm(PA[:], wa[:, k, dc * 128:(dc + 1) * 128], onb[b][:, k, :], k == 0, k == 11, [t_wa, t_onb[b]], [tpa])
                for k in range(4):
                    mm(PF[:], wf[:, k, dc * 128:(dc + 1) * 128], yb[b][:, k, :], k == 0, k == 3, [t_wf, t_yb[b]], [tpf])
                tt('dve', M1[:], PA[:], GA[:], ALU.mult, [tpa, tga], [tm1])
                tt('dve', M2[:], PF[:], GF[:], ALU.mult, [tpf, tgf], [tm2])
                tt('pool', mst[b][:, dc, :], M1[:], M2[:], ALU.add, [tm1, tm2], [t_mst[b]])
            store_blk('sp', mT_scr, mst[b], tb, 16, [t_mst[b]], [d_mT[tb]], f'e_st{b}')
        end_phase()

    if not any(stop(x) for x in ['A', 'B', 'C', 'D', 'E1']):
        wo = sb("f_wo", [128, 16, D], BF16); t_wo = T()
        gb = sb("f_gb", [128, D], F32); t_g = T()
        idb = sb("f_id", [128, 128], BF16); t_id = T()
        mb = [sb(f"f_m{i}", [128, 16, 512], BF16) for i in range(2)]; t_mb = Ts(2)
        xts = [sb(f"f_x{i}", [128, D], F32) for i in range(2)]; t_xs = Ts(2)
        x2t = [sb(f"f_x2{i}", [128, D], F32) for i in range(3)]; t_x2 = Ts(3)
        junk = sb("f_junk", [128, D], F32); t_junk = T()
        hb = [sb(f"f_hb{i}", [128, D], BF16) for i in range(2)]; t_hb = Ts(2)
        ssq = [sb(f"f_ss{i}", [128, 4], F32) for i in range(2)]; t_ss = Ts(2)
        hTb = [sb(f"f_hT{i}", [128, 16, 512], BF16) for i in range(2)]; t_hT = Ts(2)
        po = [ps(f"f_po{i}", [128, 512]) for i in range(4)]; t_po = Ts(4)
        pT = [ps(f"f_pT{i}", [128, 16, 128], BF16) for i in range(2)]; t_pT = Ts(2)
        load_wb(wo, 'wo', 0, 16, 0, D, [t_wo], 'f_wo', ksplit=2)
        dma('sp', gb[:], bcast_rows(g_ffn, D), [], [t_g], 'f_g')
        idf = sb("f_idf", [128, 128], F32); t_idf = T()
        dma('sp', idf[:], ident, [], [t_idf], 'f_id')
        cp('dve', idb[:], idf[:], [t_idf], [t_id])

        def f_load_m(tb):
            load_blk('sp', mb[tb % 2], mT_scr, tb, 16, [d_mT[tb]], [t_mb[tb % 2]], f'f_m{tb % 2}')

        def f_load_x(i):
            dma('sp', xts[i % 2][:], xs[i * 128:(i + 1) * 128, :], [], [t_xs[i % 2]], f'f_x{i % 2}')

        def f_part2(i):
            tb = i // 4; sub = i % 4; b = i % 2
            cp('act', hTb[tb % 2][:, :, sub * 128:(sub + 1) * 128], pT[b][:], [t_pT[b]], [t_hT[tb % 2]])
            if sub == 3:
                store_blk('sp', h2T_scr, hTb[tb % 2], tb, 16, [t_hT[tb % 2]], [d_h2T[tb]], f'f_st{tb % 2}')

        def f_stageA(i):
            tb = i // 4; sub = i % 4
            if sub == 0 and tb + 1 < 4:
                f_load_m(tb + 1)
            if i + 1 < 16:
                f_load_x(i + 1)
            b = i % 2; b3 = i % 3
            M = mb[tb % 2]; tm = t_mb[tb % 2]
            for cb in range(4):
                for k in range(16):
                    mm(po[cb][:], M[:, k, sub * 128:(sub + 1) * 128], wo[:, k, cb * 512:(cb + 1) * 512], k == 0, k == 15, [tm, t_wo], [t_po[cb]])
                tt('dve', x2t[b3][:, cb * 512:(cb + 1) * 512], po[cb][:], xts[b][:, cb * 512:(cb + 1) * 512], ALU.add, [t_po[cb], t_xs[b]], [t_x2[b3]])
            dma('sp', x2_scr[i * 128:(i + 1) * 128, :], x2t[b3][:], [t_x2[b3]], [d_x2[tb]], f'f_sx{b3}')
            norm_sq(x2t[b3][:], ssq[b], junk, t_x2[b3], t_ss[b], t_junk)

        f_load_m(0)
        f_load_x(0)
        f_stageA(0)
        for i in range(16):
            if i + 1 < 16:
                f_stageA(i + 1)
            b = i % 2; b3 = i % 3
            norm_rest(x2t[b3][:], gb, idb, ssq[b], hb[b], pT[b], t_x2[b3], t_g, t_id, t_ss[b], t_hb[b], t_pT[b])
            if i > 0:
                f_part2(i - 1)
        f_part2(15)
        end_phase()

    if not any(stop(x) for x in ['A', 'B', 'C', 'D', 'E1', 'E2']):
        wg = [sb(f"g_wg{i}", [128, 16, 512], BF16) for i in range(2)]; t_wg = Ts(2)
        wu = [sb(f"g_wu{i}", [128, 16, 512], BF16) for i in range(2)]; t_wu = Ts(2)
        hbk = [sb(f"g_h{i}", [128, 16, 512], BF16) for i in range(2)]; t_h = Ts(2)
        sl_ = [sb(f"g_s{i}", [128, 512], F32) for i in range(2)]; t_sl = Ts(2)
        ast = [sb(f"g_a{i}", [128, 4, 512], BF16) for i in range(2)]; t_ast = Ts(2)
        pg = [ps(f"g_pg{i}", [128, 512]) for i in range(3)]; t_pg = Ts(3)
        pu = [ps(f"g_pu{i}", [128, 512]) for i in range(3)]; t_pu = Ts(3)
        steps = [(sl, tb) for sl in range(11) for tb in range(4)]

        def g_load_w(sl):
            load_wb(wg[sl % 2], 'wg', 0, 16, sl * 512, 512, [t_wg[sl % 2]], f'g_wg{sl % 2}', ksplit=8)
            load_wb(wu[sl % 2], 'wu', 0, 16, sl * 512, 512, [t_wu[sl % 2]], f'g_wu{sl % 2}', ksplit=8)

        def g_load_h(si):
            sl, tb = steps[si]
            load_blk('sp', hbk[si % 2], h2T_scr, tb, 16, [d_h2T[tb]], [t_h[si % 2]], f'g_h{si % 2}')

        g_load_w(0)
        g_load_h(0)
        ci = 0
        for si, (sl, tb) in enumerate(steps):
            if tb == 0 and sl + 1 < 11:
                g_load_w(sl + 1)
            if si + 1 < len(steps):
                g_load_h(si + 1)
            if tb == 1:
                if si > 1:
                    S.wait_on('pool', S.ops['act'][-1])
                issue_conversions(4, which=1)
            H = hbk[si % 2]; th = t_h[si % 2]
            A = ast[si % 2]; ta = t_ast[si % 2]
            for c in range(4):
                PG = pg[ci % 3]; tpg = t_pg[ci % 3]
                PU = pu[ci % 3]; tpu = t_pu[ci % 3]
                SL = sl_[ci % 2]; tsl = t_sl[ci % 2]
                ci += 1
                for k in range(16):
                    mm(PG[:], wg[sl % 2][:, k, c * 128:(c + 1) * 128], H[:, k, :], k == 0, k == 15, [t_wg[sl % 2], th], [tpg])
                for k in range(16):
                    mm(PU[:], wu[sl % 2][:, k, c * 128:(c + 1) * 128], H[:, k, :], k == 0, k == 15, [t_wu[sl % 2], th], [tpu])
                act(SL[:], PG[:], AF.Silu, [tpg], [tsl])
                tt('dve', A[:, c, :], SL[:], PU[:], ALU.mult, [tsl, tpu], [ta])
            dma('sp', aT_scr[tb, :, sl * 4:(sl + 1) * 4, :], A[:], [ta], [d_aT[tb]], f'g_st{si % 2}')
        end_phase()

    if not any(stop(x) for x in ['A', 'B', 'C', 'D', 'E1', 'E2', 'F']):
        KH = 22
        wdh = sb("h_wd", [128, KH, D], BF16); t_wdh = T()
        ab_ = [sb(f"h_a{i}", [128, KH, 512], BF16) for i in range(2)]; t_ab = Ts(2)
        xin = [sb(f"h_xi{i}", [128, D], F32) for i in range(2)]; t_xin = Ts(2)
        xo = [sb(f"h_xo{i}", [128, D], F32) for i in range(2)]; t_xo = Ts(2)
        ot = [sb(f"h_ot{i}", [128, D], F32) for i in range(2)]; t_ot = Ts(2)
        gb = sb("h_gb", [128, D], F32); t_g = T()
        ssq = [sb(f"h_ss{i}", [128, 4], F32) for i in range(2)]; t_ss = Ts(2)
        pd = [[ps(f"h_pd{s_}{i}", [128, 512]) for i in range(4)] for s_ in range(2)]
        t_pd = [Ts(4) for _ in range(2)]
        dma('sp', gb[:], bcast_rows(g_final, D), [], [t_g], 'h_g')
        gsteps = [(p_, tt_) for p_ in range(2) for tt_ in range(16)]

        def h_load_w(p_):
            load_wb(wdh, 'wd', p_ * KH * 128, KH, 0, D, [t_wdh], 'h_wd', ksplit=2)

        def h_load_a(p_, tb):
            j = (p_ * 4 + tb) % 2
            dma('sp', ab_[j][:, 0:11, :], aT_scr[tb, :, p_ * KH:p_ * KH + 11, :], [d_aT[tb]], [t_ab[j]], f'h_a{j}')
            dma('sp', ab_[j][:, 11:22, :], aT_scr[tb, :, p_ * KH + 11:(p_ + 1) * KH, :], [d_aT[tb]], [t_ab[j]], f'h_a{j}')

        def h_load_x(gi_):
            p_, tt_ = gsteps[gi_]
            src = x2_scr if p_ == 0 else x3_scr
            dep = d_x2[tt_ // 4] if p_ == 0 else d_x3[tt_ // 4]
            dma('sp', xin[gi_ % 2][:], src[tt_ * 128:(tt_ + 1) * 128, :], [dep], [t_xin[gi_ % 2]], f'h_xi{gi_ % 2}')

        def h_norm(gi_):
            p_, tt_ = gsteps[gi_]
            b_ = gi_ % 2
            ts('dve', ssq[b_][:, 1:2], ssq[b_][:, 0:1], 1.0 / D, EPS, ALU.mult, ALU.add, [t_ss[b_]], [t_ss[b_]])
            act(ssq[b_][:, 2:3], ssq[b_][:, 1:2], AF.Sqrt, [t_ss[b_]], [t_ss[b_]])
            recip(ssq[b_][:, 3:4], ssq[b_][:, 2:3], [t_ss[b_]], [t_ss[b_]])
            stt('dve', ot[b_][:], xo[b_][:], ssq[b_][:, 3:4], gb[:], ALU.mult, ALU.mult, [t_xo[b_], t_ss[b_], t_g], [t_ot[b_]])
            dma('sp', out[tt_ * 128:(tt_ + 1) * 128, :], ot[b_][:], [t_ot[b_]], [d_out[tt_ // 4]], f'h_so{b_}')

        h_load_w(0)
        h_load_a(0, 0)
        h_load_x(0)
        for gi_, (p_, tt_) in enumerate(gsteps):
            tb = tt_ // 4; sub = tt_ % 4
            if p_ == 1 and tt_ == 0:
                h_load_w(1)
            if sub == 0:
                nxt = p_ * 4 + tb + 1
                if nxt < 8:
                    h_load_a(nxt // 4, nxt % 4)
            if gi_ + 1 < len(gsteps):
                h_load_x(gi_ + 1)
            A = ab_[(p_ * 4 + tb) % 2]; ta = t_ab[(p_ * 4 + tb) % 2]
            b_ = gi_ % 2
            P = pd[gi_ % 2]; tp = t_pd[gi_ % 2]
            for cb in range(4):
                for k in range(KH):
                    mm(P[cb][:], A[:, k, sub * 128:(sub + 1) * 128], wdh[:, k, cb * 512:(cb + 1) * 512], k == 0, k == KH - 1, [ta, t_wdh], [tp[cb]])
            for cb in range(4):
                tt('dve', xo[b_][:, cb * 512:(cb + 1) * 512], P[cb][:], xin[b_][:, cb * 512:(cb + 1) * 512], ALU.add, [tp[cb], t_xin[b_]], [t_xo[b_]])
            if p_ == 0:
                dma('sp', x3_scr[tt_ * 128:(tt_ + 1) * 128, :], xo[b_][:], [t_xo[b_]], [d_x3[tb]], f'h_sx{b_}')
            else:
                act(ot[b_][:], xo[b_][:], AF.Square, [t_xo[b_]], [t_ot[b_], t_ss[b_]], accum=ssq[b_][:, 0:1])
                if tt_ > 0:
                    h_norm(gi_ - 1)
        h_norm(len(gsteps) - 1)
        end_phase()

    alld = d_hT + d_qT + d_kT + d_v + d_f + d_ga + d_gf + d_oT + d_yT + d_mT + d_x2 + d_h2T + d_aT + d_x3 + d_out
    S.finish('sp', alld)
    es.close()
    S.emit()
    return nc


def host_tables(half):
    pos = np.concatenate([np.arange(half * T_OWN, (half + 1) * T_OWN), np.arange((1 - half) * T_OWN, (2 - half) * T_OWN)])
    inv_freq = (np.float32(ROPE_THETA) ** (-np.arange(0, 32, 2, dtype=np.float32) / np.float32(32))).astype(np.float32)
    ang = pos.astype(np.float32)[:, None] * inv_freq[None, :]
    c = np.cos(ang).astype(np.float32).T
    s = np.sin(ang).astype(np.float32).T
    rope_c = np.ascontiguousarray(np.concatenate([c, c], 0))
    rope_s = np.ascontiguousarray(np.concatenate([-s, s], 0))
    prod = (pos[:, None].astype(np.int64) * pos[None, :T_OWN].astype(np.int64)) % S_FULL
    a = (2.0 * np.pi / S_FULL) * prod.astype(np.float64)
    dft_c = np.cos(a).astype(np.float32).astype(ml_dtypes.bfloat16)
    dft_s = np.sin(a).astype(np.float32).astype(ml_dtypes.bfloat16)
    return rope_c, rope_s, dft_c, dft_s


def host_consts():
    perm = np.zeros((32, 32), np.float32)
    for m in range(32):
        perm[(m + 16) % 32, m] = 1.0
    ident = np.eye(128, dtype=np.float32)
    k = np.arange(128)
    a = (2.0 * np.pi / 128) * ((k[:, None] * k[None, :]) % 128).astype(np.float64)
    nrm = 1.0 / math.sqrt(S_FULL * 128.0)
    cd = (np.cos(a) * nrm).astype(np.float32)
    nsd = (-np.sin(a) * nrm).astype(np.float32)
    return perm, ident, cd, nsd


def make_in_maps(x, g_mix, w_in, lambda_q1, lambda_k1, lambda_q2, lambda_k2, g_subln,
                 w_attn_branch, w_four_branch, w_out, g_ffn, w_gate, w_up, w_down, g_final):
    f = lambda a: np.ascontiguousarray(np.asarray(a, dtype=np.float32))
    x = f(x)
    perm, ident, cd, nsd = host_consts()
    tabs = [host_tables(0), host_tables(1)]
    common = {
        "w_in": f(w_in), "w_attn": f(w_attn_branch), "w_four": f(w_four_branch), "w_out": f(w_out),
        "w_gate": f(w_gate), "w_up": f(w_up), "w_down": f(w_down),
        "g_mix": f(g_mix).reshape(1, D), "g_ffn": f(g_ffn).reshape(1, D), "g_final": f(g_final).reshape(1, D),
        "g_sub": np.ascontiguousarray(f(g_subln).reshape(2, 128).T),
        "lams": np.ascontiguousarray(np.stack([f(lambda_q1), f(lambda_k1), f(lambda_q2), f(lambda_k2)], 0)),
        "perm": perm, "ident": ident, "dft_cd": cd, "dft_nsd": nsd,
    }
    in_maps = []
    for c in range(8):
        b, half = c // 2, c % 2
        xs = np.ascontiguousarray(np.concatenate(
            [x[b, half * T_OWN:(half + 1) * T_OWN], x[b, (1 - half) * T_OWN:(2 - half) * T_OWN]], 0))
        rope_c, rope_s, dft_c, dft_s = tabs[half]
        m = dict(common)
        m.update({"xs": xs, "rope_c": rope_c, "rope_s": rope_s, "dft_c": dft_c, "dft_s": dft_s})
        in_maps.append(m)
    return in_maps


_NC_CACHE = {}


def kernel(**inputs):
    in_maps = make_in_maps(**inputs)
    if 'nc' not in _NC_CACHE:
        _NC_CACHE['nc'] = build_nc()
    nc = _NC_CACHE['nc']
    res = run_bass_kernel_spmd(nc, in_maps, core_ids=list(range(8)))
    outp = np.empty((4, S_FULL, D), np.float32)
    for c in range(8):
        b, half = c // 2, c % 2
        outp[b, half * T_OWN:(half + 1) * T_OWN] = res.results[c]["out"]
    return outp
```

```python
import contextlib
import math
import numpy as np
import ml_dtypes
import concourse.bass as bass
import concourse.mybir as mybir
from concourse.bass_utils import run_bass_kernel_spmd

F32 = mybir.dt.float32
BF16 = mybir.dt.bfloat16
AF = mybir.ActivationFunctionType
ALU = mybir.AluOpType
AX = mybir.AxisListType

D = 2048
S_FULL = 4096
T_OWN = 2048
NH = 6
QK_W = 1536
V_W = 1536
F_W = 512
IN_W = 9216
DFF = 5632
LAMBDA_INIT = 0.8 - 0.6 * math.exp(-0.3 * 0)
EPS = 1e-5
ROPE_THETA = 500000.0
ENGS = ['pe', 'act', 'dve', 'pool', 'sp']


class Tile:
    __slots__ = ('name', 'last_w', 'readers')

    def __init__(self, name):
        self.name = name
        self.last_w = None
        self.readers = []


class Op:
    __slots__ = ('eng', 'idx', 'fn', 'waits', 'signal', 'count', 'is_dma', 'sem')


class Sched:
    def __init__(self, nc):
        self.nc = nc
        self.ops = {e: [] for e in ENGS}
        self.waited = {e: {} for e in ENGS}
        self.dma_cnt = {}

    def tile(self, name="t"):
        return Tile(name)

    def tiles(self, n, name="t"):
        return [Tile(f"{name}{i}") for i in range(n)]

    def _add_wait(self, op, d, kind, force=False):
        w = self.waited[op.eng]
        if d.is_dma:
            key = ('dma', d.sem)
            cnt = self.dma_cnt[d.sem]
            if op.is_dma and op.sem == d.sem:
                cnt -= 1
            val = cnt * 16
            if w.get(key, 0) >= val:
                return
            w[key] = val
            op.waits.append((key, val, None))
        else:
            if d.eng == op.eng and not op.is_dma and kind != 'raw' and not force:
                return
            key = ('eng', d.eng)
            if w.get(key, -1) >= d.idx:
                return
            w[key] = d.idx
            d.signal = True
            op.waits.append((key, None, d))

    def _deps(self, op, reads, writes):
        deps = []
        for t in reads:
            if t.last_w is not None:
                deps.append((t.last_w, 'raw'))
        for t in writes:
            if t.last_w is not None:
                deps.append((t.last_w, 'waw'))
            for r in t.readers:
                deps.append((r, 'war'))
        for t in reads:
            t.readers.append(op)
        for t in writes:
            t.last_w = op
            t.readers = []
        for d, kind in deps:
            if d is not op:
                self._add_wait(op, d, kind)

    def _new(self, eng, fn, is_dma, sem):
        o = Op()
        o.eng = eng; o.fn = fn; o.waits = []; o.signal = False; o.is_dma = is_dma; o.sem = sem
        o.count = 0
        o.idx = len(self.ops[eng])
        self.ops[eng].append(o)
        return o

    def op(self, eng, fn, reads=(), writes=()):
        o = self._new(eng, fn, False, None)
        self._deps(o, reads, writes)
        return o

    def dma(self, eng, fn, reads=(), writes=(), sem=None):
        self.dma_cnt[sem] = self.dma_cnt.get(sem, 0) + 1
        o = self._new(eng, fn, True, sem)
        self._deps(o, reads, writes)
        return o

    def barrier(self, tiles):
        deps = []
        for t in tiles:
            if t.last_w is not None:
                deps.append(t.last_w)
            deps.extend(t.readers)
        deps.sort(key=lambda d: -d.idx)
        for e in ENGS:
            o = self._new(e, None, False, None)
            for d in deps:
                self._add_wait(o, d, 'raw', force=True)

    def wait_on(self, eng, target):
        o = self._new(eng, None, False, None)
        self._add_wait(o, target, 'raw', force=True)

    def finish(self, eng, tiles):
        self.op(eng, None, reads=tiles)

    def emit(self):
        nc = self.nc
        for e in ENGS:
            c = 0
            for o in self.ops[e]:
                if o.signal and not o.is_dma:
                    c += 1
                    o.count = c
        with contextlib.ExitStack() as st:
            esem = {e: st.enter_context(nc.semaphore(f"s_{e}")) for e in ENGS}
            dsem = {k: st.enter_context(nc.semaphore(f"d_{k}")) for k in self.dma_cnt}
            block = st.enter_context(nc.Block())

            def run(e, eng):
                for o in self.ops[e]:
                    for key, val, d in o.waits:
                        if key[0] == 'dma':
                            eng.wait_ge(dsem[key[1]], val)
                        else:
                            eng.wait_ge(esem[key[1]], d.count)
                    if o.fn is None:
                        continue
                    ins = o.fn(eng)
                    if o.is_dma:
                        ins.then_inc(dsem[o.sem], 16)
                    elif o.signal:
                        ins.then_inc(esem[e], 1)

            @block.tensor
            def _(eng):
                run('pe', eng)

            @block.scalar
            def _(eng):
                run('act', eng)

            @block.vector
            def _(eng):
                run('dve', eng)

            @block.gpsimd
            def _(eng):
                run('pool', eng)

            @block.sync
            def _(eng):
                run('sp', eng)


def build_nc(dbg=False, stop_after=None):
    nc = bass.Bass("TRN2", target_bir_lowering=False)
    S = Sched(nc)

    def din(name, shape):
        return nc.dram_tensor(name, shape, F32, kind="ExternalInput").ap()

    def dscr(name, shape, dt):
        kind = "ExternalOutput" if (dbg and name in dbg) else "Internal"
        return nc.dram_tensor(name, shape, dt, kind=kind).ap()

    xs = din("xs", [S_FULL, D])
    w_in = din("w_in", [D, IN_W])
    w_attn = din("w_attn", [V_W, D])
    w_four = din("w_four", [F_W, D])
    w_out = din("w_out", [D, D])
    w_gate = din("w_gate", [D, DFF])
    w_up = din("w_up", [D, DFF])
    w_down = din("w_down", [DFF, D])
    g_mix = din("g_mix", [1, D])
    g_ffn = din("g_ffn", [1, D])
    g_final = din("g_final", [1, D])
    g_sub = din("g_sub", [128, 2])
    lams = din("lams", [4, 128])
    rope_c = din("rope_c", [32, S_FULL])
    rope_s = din("rope_s", [32, S_FULL])
    perm = din("perm", [32, 32])
    ident = din("ident", [128, 128])
    dft_c = nc.dram_tensor("dft_c", [S_FULL, T_OWN], BF16, kind="ExternalInput").ap()
    dft_s = nc.dram_tensor("dft_s", [S_FULL, T_OWN], BF16, kind="ExternalInput").ap()
    dft_cd = din("dft_cd", [128, 128])
    dft_nsd = din("dft_nsd", [128, 128])
    out = nc.dram_tensor("out", [T_OWN, D], F32, kind="ExternalOutput").ap()

    hT_scr = dscr("hT_scr", [8, 128, 16, 512], BF16)
    qT_scr = dscr("qT_scr", [12, 128, T_OWN], BF16)
    kT_scr = dscr("kT_scr", [12, 128, S_FULL], BF16)
    v_scr = dscr("v_scr", [S_FULL, V_W], BF16)
    f_scr = dscr("f_scr", [S_FULL, F_W], BF16)
    ga_scr = dscr("ga_scr", [4, 128, 16, 512], BF16)
    gf_scr = dscr("gf_scr", [4, 128, 16, 512], BF16)
    oT_scr = dscr("oT_scr", [4, 128, 12, 512], BF16)
    yT_scr = dscr("yT_scr", [4, 128, 4, 512], BF16)
    mT_scr = dscr("mT_scr", [4, 128, 16, 512], BF16)
    x2_scr = dscr("x2_scr", [T_OWN, D], F32)
    h2T_scr = dscr("h2T_scr", [4, 128, 16, 512], BF16)
    aT_scr = dscr("aT_scr", [4, 128, 44, 512], BF16)
    x3_scr = dscr("x3_scr", [T_OWN, D], F32)

    cv_src = {'wa': w_attn, 'wf': w_four, 'wo': w_out, 'wg': w_gate, 'wu': w_up, 'wd': w_down}
    cv_dst = {k: dscr("cvb_" + k, list(v.shape), BF16) for k, v in cv_src.items()}
    cv_tiles = {k: S.tiles(v.shape[0] // 128, "cv_" + k) for k, v in cv_src.items()}

    cv_sem = {'wa': 'cv_e', 'wf': 'cv_e', 'wo': 'cv_e', 'wg': 'cv_f', 'wu': 'cv_f', 'wd': 'cv_g'}

    cv_list = [(k, r) for k in ['wa', 'wf', 'wo', 'wg', 'wu'] for r in range(cv_src[k].shape[0] // 128)]
    cv_list2 = [('wd', r) for r in range(cv_src['wd'].shape[0] // 128)]
    cv_pos = [0, 0]

    def issue_conversions(n, which=0):
        lst = cv_list if which == 0 else cv_list2
        for k, r in lst[cv_pos[which]:cv_pos[which] + n]:
            dma('pool', cv_dst[k][r * 128:(r + 1) * 128, :], cv_src[k][r * 128:(r + 1) * 128, :], [], [cv_tiles[k][r]], cv_sem[k])
        cv_pos[which] += n

    def load_wb(tile3, key, r0, k_tot, c0, cn, writes, sem, ksplit=4):
        wb = cv_dst[key]
        for k0 in range(0, k_tot, ksplit):
            k1 = min(k_tot, k0 + ksplit)
            dma('sp', tile3[:, k0:k1, :],
                wb[r0 + k0 * 128:r0 + k1 * 128, c0:c0 + cn].rearrange("(k p) n -> p k n", p=128),
                cv_tiles[key][(r0 // 128) + k0:(r0 // 128) + k1], writes, sem)

    def dtiles(n):
        return S.tiles(n, "dr")

    d_hT = dtiles(8); d_qT = dtiles(4); d_kT = dtiles(8); d_v = dtiles(8); d_f = dtiles(8)
    d_ga = dtiles(4); d_gf = dtiles(4); d_oT = dtiles(4); d_yT = dtiles(4); d_mT = dtiles(4)
    d_x2 = dtiles(4); d_h2T = dtiles(4); d_aT = dtiles(4); d_x3 = dtiles(4); d_out = dtiles(4)

    def mm(o, l, r, st, sp, reads, writes):
        S.op('pe', lambda e: e.matmul(o, lhsT=l, rhs=r, start=st, stop=sp), reads, writes)

    def tr(o, i, idn, reads, writes):
        S.op('pe', lambda e: e.transpose(out=o, in_=i, identity=idn), reads, writes)

    def act(o, i, func, reads, writes, scale=1.0, bias=None, accum=None):
        kw = {}
        if bias is not None:
            kw['bias'] = bias
        if accum is not None:
            kw['accum_out'] = accum
        S.op('act', lambda e: e.activation(out=o, in_=i, func=func, scale=scale, **kw), reads, writes)

    def tt(eng, o, a, b, op, reads, writes):
        S.op(eng, lambda e: e.tensor_tensor(out=o, in0=a, in1=b, op=op), reads, writes)

    def ts(eng, o, a, s1, s2, op0, op1, reads, writes):
        if s2 is None:
            S.op(eng, lambda e: e.tensor_scalar(out=o, in0=a, scalar1=s1, scalar2=None, op0=op0), reads, writes)
        else:
            S.op(eng, lambda e: e.tensor_scalar(out=o, in0=a, scalar1=s1, scalar2=s2, op0=op0, op1=op1), reads, writes)

    def stt(eng, o, a, s, b, op0, op1, reads, writes):
        S.op(eng, lambda e: e.scalar_tensor_tensor(out=o, in0=a, scalar=s, in1=b, op0=op0, op1=op1), reads, writes)

    def cp(eng, o, i, reads, writes):
        if eng == 'act':
            S.op('act', lambda e: e.copy(out=o, in_=i), reads, writes)
        else:
            S.op(eng, lambda e: e.tensor_copy(out=o, in_=i), reads, writes)

    def recip(o, i, reads, writes):
        S.op('dve', lambda e: e.reciprocal(out=o, in_=i), reads, writes)

    def dma(q, o, i, reads, writes, sem):
        S.dma(q, lambda e: e.dma_start(out=o, in_=i), reads, writes, sem)

    def bcast_rows(ap_row, n):
        return bass.AP(ap_row.tensor, ap_row.offset, [[0, 128], [1, n]])

    def load_kblock(q, tile3, src3, k_tot, t0, tn, reads, writes, sem, ksplit=4):
        for k0 in range(0, k_tot, ksplit):
            k1 = min(k_tot, k0 + ksplit)
            dma(q, tile3[:, k0:k1, :], src3[k0:k1, :, t0:t0 + tn].rearrange("k p t -> p k t"), reads, writes, sem)

    def store_kblock(q, dst3, tile3, k_tot, t0, tn, reads, writes, sem, ksplit=4):
        for k0 in range(0, k_tot, ksplit):
            k1 = min(k_tot, k0 + ksplit)
            dma(q, dst3[k0:k1, :, t0:t0 + tn].rearrange("k p t -> p k t"), tile3[:, k0:k1, :], reads, writes, sem)

    def load_blk(q, tile3, scr4, blk, k_tot, reads, writes, sem, ksplit=8):
        for k0 in range(0, k_tot, ksplit):
            k1 = min(k_tot, k0 + ksplit)
            dma(q, tile3[:, k0:k1, :], scr4[blk, :, k0:k1, :], reads, writes, sem)

    def store_blk(q, scr4, tile3, blk, k_tot, reads, writes, sem, ksplit=8):
        for k0 in range(0, k_tot, ksplit):
            k1 = min(k_tot, k0 + ksplit)
            dma(q, scr4[blk, :, k0:k1, :], tile3[:, k0:k1, :], reads, writes, sem)

    def load_w(tile3, w2, r0, k_tot, c0, cn, writes, sem, ksplit=4):
        for k0 in range(0, k_tot, ksplit):
            k1 = min(k_tot, k0 + ksplit)
            dma('pool', tile3[:, k0:k1, :],
                w2[r0 + k0 * 128:r0 + k1 * 128, c0:c0 + cn].rearrange("(k p) n -> p k n", p=128), [], writes, sem)

    es = contextlib.ExitStack()

    def sb(name, shape, dt):
        return es.enter_context(nc.sbuf_tensor(name, shape, dt))

    def ps(name, shape, dt=F32):
        return es.enter_context(nc.psum_tensor(name, shape, dt))

    phase_tiles = []

    def T(name="t"):
        t = S.tile(name)
        phase_tiles.append(t)
        return t

    def Ts(n, name="t"):
        return [T(name) for _ in range(n)]

    def end_phase():
        nonlocal es
        S.barrier(phase_tiles)
        phase_tiles.clear()
        es.close()
        es = contextlib.ExitStack()

    done = [False]

    def stop(name):
        if stop_after == name:
            done[0] = True
        return done[0]

    def norm_sq(xt_ap, ssq, junk, t_x, t_ss, t_junk):
        act(junk[:], xt_ap, AF.Square, [t_x], [t_junk, t_ss], accum=ssq[:, 0:1])

    def norm_rest(xt_ap, gb, idb, ssq, hb, pT, t_x, t_g, t_id, t_ss, t_hb, t_pT):
        ts('dve', ssq[:, 1:2], ssq[:, 0:1], 1.0 / D, EPS, ALU.mult, ALU.add, [t_ss], [t_ss])
        act(ssq[:, 2:3], ssq[:, 1:2], AF.Sqrt, [t_ss], [t_ss])
        recip(ssq[:, 3:4], ssq[:, 2:3], [t_ss], [t_ss])
        stt('dve', hb[:], xt_ap, ssq[:, 3:4], gb[:], ALU.mult, ALU.mult, [t_x, t_ss, t_g], [t_hb])
        for k in range(16):
            tr(pT[:, k, :], hb[:, k * 128:(k + 1) * 128], idb[:], [t_hb, t_id], [t_pT])

    es_ab = contextlib.ExitStack()
    hT_own = es_ab.enter_context(nc.sbuf_tensor("hT_own", [128, 16, T_OWN], BF16))
    t_hown = S.tiles(4, "hown")
    if True:
        gb = sb("a_gb", [128, D], F32); t_g = T()
        idb = sb("a_id", [128, 128], BF16); t_id = T()
        xts = [sb(f"a_x{i}", [128, D], F32) for i in range(3)]; t_xs = Ts(3)
        junk = sb("a_junk", [128, D], F32); t_junk = T()
        hb = [sb(f"a_hb{i}", [128, D], BF16) for i in range(2)]; t_hb = Ts(2)
        ssq = [sb(f"a_ss{i}", [128, 4], F32) for i in range(2)]; t_ss = Ts(2)
        hTb = [sb(f"a_hT{i}", [128, 16, 512], BF16) for i in range(2)]; t_hT = Ts(2)
        pT = [ps(f"a_pT{i}", [128, 16, 128], BF16) for i in range(2)]; t_pT = Ts(2)
        dma('sp', gb[:], bcast_rows(g_mix, D), [], [t_g], 'a_g')
        dma('pool', idb[:], ident, [], [t_id], 'a_id')
        NT = S_FULL // 128

        def a_load(i):
            dma('sp', xts[i % 3][:], xs[i * 128:(i + 1) * 128, :], [], [t_xs[i % 3]], f'a_x{i % 3}')

        def a_part2(i):
            b = i % 2; blk = i // 4; sub = i % 4
            eng = 'act' if i % 2 else 'dve'
            if blk < 4:
                cp(eng, hT_own[:, :, i * 128:(i + 1) * 128], pT[b][:], [t_pT[b]], [t_hown[blk]])
            else:
                cp(eng, hTb[blk % 2][:, :, sub * 128:(sub + 1) * 128], pT[b][:], [t_pT[b]], [t_hT[blk % 2]])
                if sub == 3:
                    store_blk('sp', hT_scr, hTb[blk % 2], blk, 16, [t_hT[blk % 2]], [d_hT[blk]], f'a_st{blk % 2}')

        a_load(0)
        a_load(1)
        norm_sq(xts[0][:], ssq[0], junk, t_xs[0], t_ss[0], t_junk)
        for i in range(NT):
            if i + 2 < NT:
                a_load(i + 2)
            if i + 1 < NT:
                j = i + 1
                norm_sq(xts[j % 3][:], ssq[j % 2], junk, t_xs[j % 3], t_ss[j % 2], t_junk)
            b = i % 2
            norm_rest(xts[i % 3][:], gb, idb, ssq[b], hb[b], pT[b], t_xs[i % 3], t_g, t_id, t_ss[b], t_hb[b], t_pT[b])
            if i > 0:
                a_part2(i - 1)
        a_part2(NT - 1)
        end_phase()

    if not stop('A'):
        wsl = [sb(f"b_w{i}", [128, 16, 512], BF16) for i in range(2)]; t_w = Ts(2)
        hbk = [sb(f"b_h{i}", [128, 16, 512], BF16) for i in range(2)]; t_h = Ts(2)
        stg = [sb(f"b_s{i}", [128, 4, 512], BF16) for i in range(2)]; t_stg = [Ts(4) for _ in range(2)]
        rc = sb("b_rc", [32, S_FULL], F32); rs = sb("b_rs", [32, S_FULL], F32); t_rope = T()
        pmb = sb("b_pm", [32, 32], BF16); t_pm = T()
        r1 = [sb(f"b_r1{i}", [32, 512], F32) for i in range(2)]; t_r1 = Ts(2)
        r2 = [sb(f"b_r2{i}", [32, 512], F32) for i in range(2)]; t_r2 = Ts(2)
        pz = [ps(f"b_pz{i}", [128, 512]) for i in range(6)]; t_pz = Ts(6)
        psw = [ps(f"b_psw{i}", [32, 512]) for i in range(2)]; t_psw = Ts(2)
        dma('sp', rc[:], rope_c, [], [t_rope], 'b_rope')
        dma('sp', rs[:], rope_s, [], [t_rope], 'b_rope')
        dma('pool', pmb[:], perm, [], [t_pm], 'b_pm')

        kinds = ['q'] * 3 + ['k'] * 3 + ['v'] * 3 + ['f'] + ['ga'] * 4 + ['gf'] * 4
        steps = []
        for sl in range(18):
            nblk = 8 if kinds[sl] in ('k', 'v', 'f') else 4
            for tb in range(nblk):
                steps.append((sl, tb))
        lsteps = [si for si, (sl, tb) in enumerate(steps) if tb >= 4]
        lslot = {si: n % 2 for n, si in enumerate(lsteps)}
        l_issued = 0
        l_done = 0

        def b_load_w(sl):
            load_w(wsl[sl % 2], w_in, 0, 16, sl * 512, 512, [t_w[sl % 2]], f'b_w{sl % 2}')

        def b_issue_loads():
            nonlocal l_issued
            while l_issued < len(lsteps) and l_issued < l_done + 2:
                si_ = lsteps[l_issued]
                j = lslot[si_]
                tb_ = steps[si_][1]
                load_blk('sp', hbk[j], hT_scr, tb_, 16, [d_hT[tb_]], [t_h[j]], f'b_h{j}')
                l_issued += 1

        b_load_w(0)
        pzi = 0
        swi = 0
        for si, (sl, tb) in enumerate(steps):
            if tb == 0 and sl + 1 < 18:
                b_load_w(sl + 1)
            b_issue_loads()
            kind = kinds[sl]
            W = wsl[sl % 2]; tw = t_w[sl % 2]
            if tb < 4:
                th = t_hown[tb]

                def Hs(k, a_, b_, tb=tb):
                    return hT_own[:, k, tb * 512 + a_:tb * 512 + b_]
            else:
                th = t_h[lslot[si]]

                def Hs(k, a_, b_, j=lslot[si]):
                    return hbk[j][:, k, a_:b_]
            sg = stg[si % 2]; tsg = t_stg[si % 2]
            pending = None
            for c in range(4):
                P = pz[pzi % 6]; tp = t_pz[pzi % 6]; pzi += 1
                if kind in ('v', 'f'):
                    for k in range(16):
                        mm(P[:], Hs(k, c * 128, (c + 1) * 128), W[:, k, :], k == 0, k == 15, [th, tw], [tp])
                    cp('act' if c % 2 else 'dve', sg[:, c, :], P[:], [tp], [tsg[c]])
                else:
                    for k in range(16):
                        mm(P[:], W[:, k, c * 128:(c + 1) * 128], Hs(k, 0, 512), k == 0, k == 15, [th, tw], [tp])
                    if pending is not None:
                        pending(); pending = None
                    if kind in ('ga', 'gf'):
                        act(sg[:, c, :], P[:], AF.Sigmoid, [tp], [tsg[c]])
                    else:
                        cp('act', sg[:, c, :], P[:], [tp], [tsg[c]])

                        def rope(c=c, P=P, tp=tp):
                            nonlocal swi
                            SW = psw[swi % 2]; tsw = t_psw[swi % 2]
                            R1 = r1[swi % 2]; tr1 = t_r1[swi % 2]
                            R2 = r2[swi % 2]; tr2 = t_r2[swi % 2]
                            swi += 1
                            mm(SW[:], pmb[:], sg[0:32, c, :], True, True, [tsg[c], t_pm], [tsw])
                            tt('dve', R1[:], SW[:], rs[:, tb * 512:(tb + 1) * 512], ALU.mult, [tsw, t_rope], [tr1])
                            tt('dve', R2[:], P[0:32, :], rc[:, tb * 512:(tb + 1) * 512], ALU.mult, [tp, t_rope], [tr2])
                            tt('dve', sg[0:32, c, :], R1[:], R2[:], ALU.add, [tr1, tr2], [tsg[c]])
                        pending = rope
            if pending is not None:
                pending(); pending = None
            if tb >= 4:
                l_done += 1
            tsg = list(tsg)
            if kind == 'q':
                dma('sp', qT_scr[sl * 4:(sl + 1) * 4, :, tb * 512:(tb + 1) * 512].rearrange("k p t -> p k t"), sg[:], tsg, [d_qT[tb]], f'b_st{si % 2}')
            elif kind == 'k':
                c0 = (sl - 3) * 4
                dma('sp', kT_scr[c0:c0 + 4, :, tb * 512:(tb + 1) * 512].rearrange("k p t -> p k t"), sg[:], tsg, [d_kT[tb]], f'b_st{si % 2}')
            elif kind == 'v':
                c0 = (sl - 6) * 512
                dma('sp', v_scr[tb * 512:(tb + 1) * 512, c0:c0 + 512].rearrange("(c p) n -> p c n", p=128), sg[:], tsg, [d_v[tb]], f'b_st{si % 2}')
            elif kind == 'f':
                dma('sp', f_scr[tb * 512:(tb + 1) * 512, :].rearrange("(c p) n -> p c n", p=128), sg[:], tsg, [d_f[tb]], f'b_st{si % 2}')
            elif kind == 'ga':
                c0 = (sl - 10) * 4
                dma('sp', ga_scr[tb, :, c0:c0 + 4, :], sg[:], tsg, [d_ga[tb]], f'b_st{si % 2}')
            else:
                c0 = (sl - 14) * 4
                dma('sp', gf_scr[tb, :, c0:c0 + 4, :], sg[:], tsg, [d_gf[tb]], f'b_st{si % 2}')
        phase_tiles.extend(t_hown)
        end_phase()
    es_ab.close()

    if not stop('A') and not stop('B'):
        kT = [sb(f"c_k{i}", [128, 2, S_FULL], BF16) for i in range(2)]; t_k = Ts(2)
        vh = [sb(f"c_v{i}", [128, 32, 256], BF16) for i in range(2)]; t_v = Ts(2)
        qh = [sb(f"c_q{i}", [128, 2, 512], BF16) for i in range(2)]; t_q = Ts(2)
        ones = sb("c_ones", [128, 128], BF16); t_ones = T()
        pT4 = sb("c_p4", [128, 8, 512], BF16); t_p = Ts(8)
        gs2 = [sb(f"c_gs2{i}", [128, 2, 512], BF16) for i in range(2)]; t_gs2 = Ts(2)
        gs1 = [sb(f"c_gs1{i}", [128, 512], BF16) for i in range(2)]; t_gs1 = Ts(2)
        lam_in = sb("c_lam", [128, 4, 128], F32); t_lam = T()
        lam_w = sb("c_lamw", [128, 2, 128], F32)
        lam_s = sb("c_lams", [128, 8], F32)
        gsb = sb("c_gs", [128, 2], F32); gss = sb("c_gss", [128, 2], F32); t_gs = T()
        epsb = sb("c_eps", [128, 1], F32)
        rr = sb("c_rr", [128, 512], F32); t_rr = T()
        oa = sb("c_oa", [128, 2, 512], F32); t_oa = T()
        ob = sb("c_ob", [128, 2, 512], F32); t_ob = T()
        oo = sb("c_oo", [128, 2, 512], F32); t_oo = T()
        sq = sb("c_sq", [128, 2, 512], BF16); t_sq = T()
        rstd = sb("c_rstd", [128, 512], F32); t_rstd = T()
        ssb = sb("c_ssb", [128, 512], F32); t_ssb = T()
        on = [sb(f"c_on{i}", [128, 2, 512], BF16) for i in range(2)]; t_on = Ts(2)
        psc = [ps(f"c_sc{i}", [128, 512]) for i in range(3)]; t_sc = Ts(3)
        pacc = [[ps(f"c_acc{s}{j}", [128, 512]) for j in range(2)] for s in range(2)]
        t_acc = [Ts(2) for _ in range(2)]
        psm = ps("c_psm", [128, 512]); t_psm = T()

        S.op('dve', lambda e: e.memset(ones[:], 1.0), [], [t_ones])
        S.op('dve', lambda e: e.memset(epsb[:], EPS), [], [t_gs])
        dma('sp', lam_in[:].rearrange("p a b -> p (a b)"), bass.AP(lams.tensor, 0, [[0, 128], [1, 512]]), [], [t_lam], 'c_lam')
        dma('sp', gsb[:], g_sub, [], [t_gs], 'c_gs')
        tt('dve', lam_w[:, 0, :], lam_in[:, 0, :], lam_in[:, 1, :], ALU.mult, [t_lam], [t_lam])
        tt('dve', lam_w[:, 1, :], lam_in[:, 2, :], lam_in[:, 3, :], ALU.mult, [t_lam], [t_lam])
        S.op('dve', lambda e: e.reduce_sum(out=lam_s[:, 0:1], in_=lam_w[:, 0, :], axis=AX.X), [t_lam], [t_lam])
        S.op('dve', lambda e: e.reduce_sum(out=lam_s[:, 1:2], in_=lam_w[:, 1, :], axis=AX.X), [t_lam], [t_lam])
        act(lam_s[:, 2:4], lam_s[:, 0:2], AF.Exp, [t_lam], [t_lam])
        tt('dve', lam_s[:, 4:5], lam_s[:, 3:4], lam_s[:, 2:3], ALU.subtract, [t_lam], [t_lam])
        ts('dve', lam_s[:, 5:6], lam_s[:, 4:5], -LAMBDA_INIT, None, ALU.add, None, [t_lam], [t_lam])
        ts('dve', gss[:], gsb[:], 1.0 - LAMBDA_INIT, None, ALU.mult, None, [t_gs], [t_gs])
        neg_lam = lam_s[:, 5:6]
        scale = 1.0 / math.sqrt(128.0)

        csteps = [(h, qb) for h in range(NH) for qb in range(4)]

        def c_load_kv(h):
            b = h % 2
            for c in range(2):
                for t0 in range(0, S_FULL, 1024):
                    dma('sp', kT[b][:, c, t0:t0 + 1024], kT_scr[2 * h + c, :, t0:t0 + 1024], d_kT, [t_k[b]], f'c_k{b}')
            for c0 in range(0, 32, 8):
                dma('sp', vh[b][:, c0:c0 + 8, :],
                    v_scr[c0 * 128:(c0 + 8) * 128, h * 256:(h + 1) * 256].rearrange("(c p) e -> p c e", p=128),
                    d_v, [t_v[b]], f'c_v{b}')

        def c_load_q(si):
            h, qb = csteps[si]
            dma('sp', qh[si % 2][:], qT_scr[2 * h:2 * h + 2, :, qb * 512:(qb + 1) * 512].rearrange("k p t -> p k t"),
                [d_qT[qb]], [t_q[si % 2]], f'c_q{si % 2}')

        c_load_kv(0)
        c_load_q(0)
        items = [(si, c, kc) for si in range(len(csteps)) for c in range(2) for kc in range(32)]

        def rec_S(i):
            si, c, kc = items[i]
            h, qb = csteps[si]
            mm(psc[i % 3][:], kT[h % 2][:, c, kc * 128:(kc + 1) * 128], qh[si % 2][:, c, :], True, True,
               [t_k[h % 2], t_q[si % 2]], [t_sc[i % 3]])

        def epi1a(si):
            for j in range(2):
                stt('dve', oo[:, j, :], ob[:, j, :], neg_lam, oa[:, j, :], ALU.mult, ALU.add, [t_ob, t_oa, t_lam], [t_oo])

        def epi1b(si):
            act(sq[:].rearrange("p a b -> p (a b)"), oo[:].rearrange("p a b -> p (a b)"), AF.Square, [t_oo], [t_sq])

        def epi2a(si, pb, tpb):
            for j in range(2):
                mm(pb[:], ones[:], sq[:, j, :], j == 0, j == 1, [t_ones, t_sq], [tpb])
            cp('dve', ssb[:], pb[:], [tpb], [t_ssb])

        def epi2b(si):
            act(rstd[:], ssb[:], AF.Ln, [t_ssb, t_gs], [t_rstd], scale=1.0 / 256.0, bias=epsb[:, 0:1])
            act(rstd[:], rstd[:], AF.Exp, [t_rstd], [t_rstd], scale=-0.5)

        def epi2c(si):
            h, qb = csteps[si]
            ON = on[si % 2]; ton = t_on[si % 2]
            for j in range(2):
                stt('dve', ON[:, j, :], oo[:, j, :], gss[:, j:j + 1], rstd[:], ALU.mult, ALU.mult, [t_oo, t_gs, t_rstd], [ton])
            dma('sp', oT_scr[qb, :, 2 * h:2 * h + 2, :], ON[:], [ton], [d_oT[qb]], f'c_st{si % 2}')

        def evac_a(si, c):
            act(rr[:], psm[:], AF.Ln, [t_psm], [t_rr])
            act(rr[:], rr[:], AF.Exp, [t_rr], [t_rr], scale=-1.0)

        def evac_b(si, c):
            dst = oa if c == 0 else ob
            tdst = t_oa if c == 0 else t_ob
            for j in range(2):
                tt('dve', dst[:, j, :], pacc[c][j][:], rr[:], ALU.mult, [t_acc[c][j], t_rr], [tdst])

        rec_S(0)
        rec_S(1)
        n_items = len(items)
        pending_evac = None
        pending_ones = None
        for i, (si, c, kc) in enumerate(items):
            h, qb = csteps[si]
            if i + 2 < n_items:
                rec_S(i + 2)
            slot = i % 8
            PT = pT4[:, slot, :]; tpp = t_p[slot]
            V = vh[h % 2]; tv = t_v[h % 2]
            ACC = pacc[c]; tacc = t_acc[c]
            act(PT, psc[i % 3][:], AF.Exp, [t_sc[i % 3]], [tpp], scale=scale)
            pace_op = S.ops['act'][-1]
            mm(ACC[0][:], V[:, kc, 0:128], PT, kc == 0, kc == 31, [tv, tpp], [tacc[0]])
            mm(ACC[1][:], V[:, kc, 128:256], PT, kc == 0, kc == 31, [tv, tpp], [tacc[1]])
            if c == 0 and kc == 0:
                if qb == 0 and h + 1 < NH:
                    c_load_kv(h + 1)
                if si + 1 < len(csteps):
                    c_load_q(si + 1)
            if c == 0 and kc == 16:
                S.wait_on('pool', pace_op)
                issue_conversions((len(cv_list) + len(csteps) - 1) // len(csteps))
            if kc % 4 == 1 and pending_ones is not None:
                pending_ones(); pending_ones = None
            if kc == 4 and pending_evac is not None:
                evac_a(*pending_evac)
            if kc == 6 and pending_evac is not None:
                evac_b(*pending_evac); pending_evac = None
            if kc % 4 == 3:
                base = slot - 3
                gp = (kc // 4) % 2
                tgrp = [t_p[base + u] for u in range(4)]
                tt('dve', gs2[gp][:], pT4[:, base:base + 2, :], pT4[:, base + 2:base + 4, :], ALU.add, tgrp, [t_gs2[gp]])
                tt('dve', gs1[gp][:], gs2[gp][:, 0, :], gs2[gp][:, 1, :], ALU.add, [t_gs2[gp]], [t_gs1[gp]])

                def ones_mm(gp=gp, kc=kc):
                    mm(psm[:], ones[:], gs1[gp][:], kc == 3, kc == 31, [t_ones, t_gs1[gp]], [t_psm])
                pending_ones = ones_mm
            if kc == 31:
                pending_evac = (si, c)
            if si > 0 and c == 0:
                if kc == 9:
                    epi1a(si - 1)
                elif kc == 14:
                    epi1b(si - 1)
                elif kc == 20:
                    epi2a(si - 1, psc[i % 3], t_sc[i % 3])
                elif kc == 26:
                    epi2b(si - 1)
                elif kc == 30:
                    epi2c(si - 1)
        pending_ones()
        evac_a(*pending_evac)
        evac_b(*pending_evac)
        if dbg and 'dbg_small' in dbg:
            dsm = nc.dram_tensor("dbg_small", [128, 10], F32, kind="ExternalOutput").ap()
            doa = nc.dram_tensor("dbg_oa", [128, 2048], F32, kind="ExternalOutput").ap()
            drr = nc.dram_tensor("dbg_rr", [128, 512], F32, kind="ExternalOutput").ap()
            t_dbg = S.tile("dbg")
            dma('sp', dsm[:, 0:8], lam_s[:], [t_lam], [t_dbg], 'dbg')
            dma('sp', dsm[:, 8:10], gss[:], [t_gs], [t_dbg], 'dbg')
            dma('sp', doa[:, 0:1024], oa[:].rearrange("p a b -> p (a b)"), [t_oa], [t_dbg], 'dbg')
            dma('sp', doa[:, 1024:2048], ob[:].rearrange("p a b -> p (a b)"), [t_ob], [t_dbg], 'dbg')
            dma('sp', drr[:], rr[:], [t_rr], [t_dbg], 'dbg')
            dk = nc.dram_tensor("dbg_k", [128, 2 * S_FULL], BF16, kind="ExternalOutput").ap()
            dv = nc.dram_tensor("dbg_v", [128, 32 * 256], BF16, kind="ExternalOutput").ap()
            dq = nc.dram_tensor("dbg_q", [128, 2 * 512], BF16, kind="ExternalOutput").ap()
            dma('sp', dk[:], kT[1][:].rearrange("p a b -> p (a b)"), [t_k[1]], [t_dbg], 'dbg')
            dma('sp', dv[:], vh[1][:].rearrange("p a b -> p (a b)"), [t_v[1]], [t_dbg], 'dbg')
            dma('sp', dq[:], qh[1][:].rearrange("p a b -> p (a b)"), [t_q[1]], [t_dbg], 'dbg')
            phase_tiles.append(t_dbg)
        epi1a(len(csteps) - 1)
        epi1b(len(csteps) - 1)
        epi2a(len(csteps) - 1, psc[0], t_sc[0])
        epi2b(len(csteps) - 1)
        epi2c(len(csteps) - 1)
        end_phase()

    if not stop('A') and not stop('B') and not stop('C'):
        fsb = sb("d_f", [128, 32, 512], BF16); t_f = T()
        tab = [sb(f"d_t{i}", [128, 16, 512], BF16) for i in range(8)]; t_tab = Ts(8)
        cd = sb("d_cd", [128, 128], BF16); nsd = sb("d_nsd", [128, 128], BF16); t_cd = T()
        ab = [sb(f"d_ab{i}", [128, 2, 512], BF16) for i in range(2)]; t_ab = Ts(2)
        yst = [sb(f"d_y{i}", [128, 512], BF16) for i in range(2)]; t_y = Ts(2)
        pab = [ps(f"d_pab{i}", [128, 512]) for i in range(4)]; t_pab = Ts(4)
        py = [ps(f"d_py{i}", [128, 512]) for i in range(2)]; t_py = Ts(2)
        for c0 in range(0, 32, 8):
            dma('sp', fsb[:, c0:c0 + 8, :], f_scr[c0 * 128:(c0 + 8) * 128, :].rearrange("(c p) n -> p c n", p=128), d_f, [t_f], 'd_f')
        cdf = sb("d_cdf", [128, 2, 128], F32); t_cdf = T()
        dma('sp', cdf[:, 0, :], dft_cd, [], [t_cdf], 'd_cd')
        dma('sp', cdf[:, 1, :], dft_nsd, [], [t_cdf], 'd_cd')
        cp('dve', cd[:], cdf[:, 0, :], [t_cdf], [t_cd])
        cp('dve', nsd[:], cdf[:, 1, :], [t_cdf], [t_cd])
        pieces = [(ob_, w, hf) for ob_ in range(4) for w in range(2) for hf in range(2)]

        def d_load(pidx):
            ob_, w, hf = pieces[pidx]
            src = dft_c if w == 0 else dft_s
            for k0 in range(0, 16, 8):
                r0 = hf * 2048 + k0 * 128
                dma('sp', tab[pidx % 8][:, k0:k0 + 8, :],
                    src[r0:r0 + 1024, ob_ * 512:(ob_ + 1) * 512].rearrange("(k p) n -> p k n", p=128),
                    [], [t_tab[pidx % 8]], f'd_t{pidx % 8}')

        for p_ in range(4):
            d_load(p_)
        gi = 0
        for ob_ in range(4):
            if ob_ + 1 < 4:
                for p_ in range(4):
                    d_load((ob_ + 1) * 4 + p_)
            for g in range(4):
                for w in range(2):
                    P = pab[(gi * 2 + w) % 4]; tp = t_pab[(gi * 2 + w) % 4]
                    for hf in range(2):
                        pidx = ob_ * 4 + w * 2 + hf
                        TB = tab[pidx % 8]; ttb = t_tab[pidx % 8]
                        for k in range(16):
                            sc_ = hf * 16 + k
                            mm(P[:], fsb[:, sc_, g * 128:(g + 1) * 128], TB[:, k, :], sc_ == 0, sc_ == 31, [t_f, ttb], [tp])
                    cp('act' if w else 'dve', ab[gi % 2][:, w, :], P[:], [tp], [t_ab[gi % 2]])
                PY = py[gi % 2]; tpy = t_py[gi % 2]
                mm(PY[:], cd[:], ab[gi % 2][:, 0, :], True, False, [t_cd, t_ab[gi % 2]], [tpy])
                mm(PY[:], nsd[:], ab[gi % 2][:, 1, :], False, True, [t_cd, t_ab[gi % 2]], [tpy])
                cp('act', yst[gi % 2][:], PY[:], [tpy], [t_y[gi % 2]])
                dma('sp', yT_scr[ob_, :, g, :], yst[gi % 2][:], [t_y[gi % 2]], [d_yT[ob_]], f'd_st{gi % 2}')
                gi += 1
        end_phase()

    if not any(stop(x) for x in 'ABCD'):
        wa = sb("e_wa", [128, 12, D], BF16); t_wa = Ts(4)
        wf = sb("e_wf", [128, 4, D], BF16); t_wf = Ts(4)
        onb = [sb(f"e_on{i}", [128, 12, 512], BF16) for i in range(2)]; t_onb = Ts(2)
        yb = [sb(f"e_y{i}", [128, 4, 512], BF16) for i in range(2)]; t_yb = Ts(2)
        gab = [sb(f"e_ga{i}", [128, 512], BF16) for i in range(4)]; t_gab = Ts(4)
        gfb = [sb(f"e_gf{i}", [128, 512], BF16) for i in range(4)]; t_gfb = Ts(4)
        m1 = [sb(f"e_m1{i}", [128, 512], F32) for i in range(2)]; t_m1 = Ts(2)
        m2 = [sb(f"e_m2{i}", [128, 512], F32) for i in range(2)]; t_m2 = Ts(2)
        mst = [sb(f"e_ms{i}", [128, 16, 512], BF16) for i in range(2)]; t_mst = Ts(2)
        pa = [ps(f"e_pa{i}", [128, 512]) for i in range(3)]; t_pa = Ts(3)
        pf = [ps(f"e_pf{i}", [128, 512]) for i in range(3)]; t_pf = Ts(3)

        def e_load(tb):
            b = tb % 2
            load_blk('sp', onb[b], oT_scr, tb, 12, [d_oT[tb]], [t_onb[b]], f'e_on{b}', ksplit=12)
            load_blk('sp', yb[b], yT_scr, tb, 4, [d_yT[tb]], [t_yb[b]], f'e_y{b}')

        e_load(0)
        for cq in range(4):
            load_wb(wf[:, :, cq * 512:(cq + 1) * 512], 'wf', 0, 4, cq * 512, 512, [t_wf[cq]], 'e_wf', ksplit=4)
            load_wb(wa[:, :, cq * 512:(cq + 1) * 512], 'wa', 0, 12, cq * 512, 512, [t_wa[cq]], 'e_wa', ksplit=12)
        ci = 0
        for tb in range(4):
            if tb + 1 < 4:
                e_load(tb + 1)
            b = tb % 2
            for dc in range(16):
                PA = pa[ci % 3]; tpa = t_pa[ci % 3]
                PF = pf[ci % 3]; tpf = t_pf[ci % 3]
                M1 = m1[ci % 2]; tm1 = t_m1[ci % 2]
                M2 = m2[ci % 2]; tm2 = t_m2[ci % 2]
                GA = gab[ci % 4]; tga = t_gab[ci % 4]
                GF = gfb[ci % 4]; tgf = t_gfb[ci % 4]
                dma('sp', GA[:], ga_scr[tb, :, dc, :], [d_ga[tb]], [tga], f'e_ga{ci % 4}')
                dma('sp', GF[:], gf_scr[tb, :, dc, :], [d_gf[tb]], [tgf], f'e_gf{ci % 4}')
                ci += 1
                for k in range(12):
                    mm(PA[:], wa[:, k, dc * 128:(dc + 1) * 128], onb[b][:, k, :], k == 0, k == 11, [t_wa[dc // 4], t_onb[b]], [tpa])
                for k in range(4):
                    mm(PF[:], wf[:, k, dc * 128:(dc + 1) * 128], yb[b][:, k, :], k == 0, k == 3, [t_wf[dc // 4], t_yb[b]], [tpf])
                tt('dve', M1[:], PA[:], GA[:], ALU.mult, [tpa, tga], [tm1])
                tt('dve', M2[:], PF[:], GF[:], ALU.mult, [tpf, tgf], [tm2])
                tt('pool', mst[b][:, dc, :], M1[:], M2[:], ALU.add, [tm1, tm2], [t_mst[b]])
            store_blk('sp', mT_scr, mst[b], tb, 16, [t_mst[b]], [d_mT[tb]], f'e_st{b}')
        end_phase()

    if not any(stop(x) for x in ['A', 'B', 'C', 'D', 'E1']):
        wo = sb("f_wo", [128, 16, D], BF16); t_wo = Ts(4)
        gb = sb("f_gb", [128, D], F32); t_g = T()
        idb = sb("f_id", [128, 128], BF16); t_id = T()
        mb = [sb(f"f_m{i}", [128, 16, 512], BF16) for i in range(2)]; t_mb = Ts(2)
        xts = [sb(f"f_x{i}", [128, D], F32) for i in range(2)]; t_xs = Ts(2)
        x2t = [sb(f"f_x2{i}", [128, D], F32) for i in range(3)]; t_x2 = Ts(3)
        junk = sb("f_junk", [128, D], F32); t_junk = T()
        hb = [sb(f"f_hb{i}", [128, D], BF16) for i in range(2)]; t_hb = Ts(2)
        ssq = [sb(f"f_ss{i}", [128, 4], F32) for i in range(2)]; t_ss = Ts(2)
        hTb = [sb(f"f_hT{i}", [128, 16, 512], BF16) for i in range(2)]; t_hT = Ts(2)
        po = [ps(f"f_po{i}", [128, 512]) for i in range(4)]; t_po = Ts(4)
        pT = [ps(f"f_pT{i}", [128, 16, 128], BF16) for i in range(2)]; t_pT = Ts(2)
        dma('sp', gb[:], bcast_rows(g_ffn, D), [], [t_g], 'f_g')
        idf = sb("f_idf", [128, 128], F32); t_idf = T()
        dma('sp', idf[:], ident, [], [t_idf], 'f_id')
        cp('dve', idb[:], idf[:], [t_idf], [t_id])

        def f_load_m(tb):
            load_blk('sp', mb[tb % 2], mT_scr, tb, 16, [d_mT[tb]], [t_mb[tb % 2]], f'f_m{tb % 2}')

        def f_load_x(i):
            dma('sp', xts[i % 2][:], xs[i * 128:(i + 1) * 128, :], [], [t_xs[i % 2]], f'f_x{i % 2}')

        def f_part2(i):
            tb = i // 4; sub = i % 4; b = i % 2
            cp('act', hTb[tb % 2][:, :, sub * 128:(sub + 1) * 128], pT[b][:], [t_pT[b]], [t_hT[tb % 2]])
            if sub == 3:
                store_blk('sp', h2T_scr, hTb[tb % 2], tb, 16, [t_hT[tb % 2]], [d_h2T[tb]], f'f_st{tb % 2}')

        def f_stageA(i):
            tb = i // 4; sub = i % 4
            if sub == 0 and tb + 1 < 4:
                f_load_m(tb + 1)
            if i + 1 < 16:
                f_load_x(i + 1)
            b = i % 2; b3 = i % 3
            M = mb[tb % 2]; tm = t_mb[tb % 2]
            for cb in range(4):
                for k in range(16):
                    mm(po[cb][:], M[:, k, sub * 128:(sub + 1) * 128], wo[:, k, cb * 512:(cb + 1) * 512], k == 0, k == 15, [tm, t_wo[cb]], [t_po[cb]])
                tt('dve', x2t[b3][:, cb * 512:(cb + 1) * 512], po[cb][:], xts[b][:, cb * 512:(cb + 1) * 512], ALU.add, [t_po[cb], t_xs[b]], [t_x2[b3]])
            dma('sp', x2_scr[i * 128:(i + 1) * 128, :], x2t[b3][:], [t_x2[b3]], [d_x2[tb]], f'f_sx{b3}')
            norm_sq(x2t[b3][:], ssq[b], junk, t_x2[b3], t_ss[b], t_junk)

        f_load_m(0)
        f_load_x(0)
        for cb in range(4):
            load_wb(wo[:, :, cb * 512:(cb + 1) * 512], 'wo', 0, 16, cb * 512, 512, [t_wo[cb]], 'f_wo', ksplit=8)
        f_stageA(0)
        for i in range(16):
            if i + 1 < 16:
                f_stageA(i + 1)
            b = i % 2; b3 = i % 3
            norm_rest(x2t[b3][:], gb, idb, ssq[b], hb[b], pT[b], t_x2[b3], t_g, t_id, t_ss[b], t_hb[b], t_pT[b])
            if i > 0:
                f_part2(i - 1)
        f_part2(15)
        end_phase()

    if not any(stop(x) for x in ['A', 'B', 'C', 'D', 'E1', 'E2']):
        wg = [sb(f"g_wg{i}", [128, 16, 512], BF16) for i in range(2)]; t_wg = Ts(2)
        wu = [sb(f"g_wu{i}", [128, 16, 512], BF16) for i in range(2)]; t_wu = Ts(2)
        hbk = [sb(f"g_h{i}", [128, 16, 512], BF16) for i in range(2)]; t_h = Ts(2)
        sl_ = [sb(f"g_s{i}", [128, 512], F32) for i in range(2)]; t_sl = Ts(2)
        ast = [sb(f"g_a{i}", [128, 4, 512], BF16) for i in range(2)]; t_ast = Ts(2)
        pg = [ps(f"g_pg{i}", [128, 512]) for i in range(3)]; t_pg = Ts(3)
        pu = [ps(f"g_pu{i}", [128, 512]) for i in range(3)]; t_pu = Ts(3)
        steps = [(sl, tb) for sl in range(11) for tb in range(4)]

        def g_load_w(sl):
            load_wb(wg[sl % 2], 'wg', 0, 16, sl * 512, 512, [t_wg[sl % 2]], f'g_wg{sl % 2}', ksplit=8)
            load_wb(wu[sl % 2], 'wu', 0, 16, sl * 512, 512, [t_wu[sl % 2]], f'g_wu{sl % 2}', ksplit=8)

        def g_load_h(si):
            sl, tb = steps[si]
            load_blk('sp', hbk[si % 2], h2T_scr, tb, 16, [d_h2T[tb]], [t_h[si % 2]], f'g_h{si % 2}')

        g_load_w(0)
        g_load_h(0)
        ci = 0
        for si, (sl, tb) in enumerate(steps):
            if si + 1 < len(steps):
                g_load_h(si + 1)
            if tb == 0 and sl + 1 < 11:
                g_load_w(sl + 1)
            if tb >= 2:
                S.wait_on('pool', S.ops['act'][-1])
                issue_conversions(2, which=1)
            H = hbk[si % 2]; th = t_h[si % 2]
            A = ast[si % 2]; ta = t_ast[si % 2]
            for c in range(4):
                PG = pg[ci % 3]; tpg = t_pg[ci % 3]
                PU = pu[ci % 3]; tpu = t_pu[ci % 3]
                SL = sl_[ci % 2]; tsl = t_sl[ci % 2]
                ci += 1
                for k in range(16):
                    mm(PG[:], wg[sl % 2][:, k, c * 128:(c + 1) * 128], H[:, k, :], k == 0, k == 15, [t_wg[sl % 2], th], [tpg])
                for k in range(16):
                    mm(PU[:], wu[sl % 2][:, k, c * 128:(c + 1) * 128], H[:, k, :], k == 0, k == 15, [t_wu[sl % 2], th], [tpu])
                act(SL[:], PG[:], AF.Silu, [tpg], [tsl])
                tt('dve', A[:, c, :], SL[:], PU[:], ALU.mult, [tsl, tpu], [ta])
            dma('sp', aT_scr[tb, :, sl * 4:(sl + 1) * 4, :], A[:], [ta], [d_aT[tb]], f'g_st{si % 2}')
        end_phase()

    if not any(stop(x) for x in ['A', 'B', 'C', 'D', 'E1', 'E2', 'F']):
        KH = 22
        wdh = sb("h_wd", [128, KH, D], BF16); t_wdh = Ts(4)
        ab_ = [sb(f"h_a{i}", [128, KH, 512], BF16) for i in range(2)]; t_ab = Ts(2)
        xin = [sb(f"h_xi{i}", [128, D], F32) for i in range(2)]; t_xin = Ts(2)
        xo = [sb(f"h_xo{i}", [128, D], F32) for i in range(2)]; t_xo = Ts(2)
        ot = [sb(f"h_ot{i}", [128, D], F32) for i in range(2)]; t_ot = Ts(2)
        gb = sb("h_gb", [128, D], F32); t_g = T()
        ssq = [sb(f"h_ss{i}", [128, 4], F32) for i in range(2)]; t_ss = Ts(2)
        pd = [[ps(f"h_pd{s_}{i}", [128, 512]) for i in range(4)] for s_ in range(2)]
        t_pd = [Ts(4) for _ in range(2)]
        dma('sp', gb[:], bcast_rows(g_final, D), [], [t_g], 'h_g')
        gsteps = [(p_, tt_) for p_ in range(2) for tt_ in range(16)]

        def h_load_w(p_):
            for cb in range(4):
                load_wb(wdh[:, :, cb * 512:(cb + 1) * 512], 'wd', p_ * KH * 128, KH, cb * 512, 512, [t_wdh[cb]], 'h_wd', ksplit=11)

        def h_load_a(p_, tb):
            j = (p_ * 4 + tb) % 2
            dma('sp', ab_[j][:, 0:11, :], aT_scr[tb, :, p_ * KH:p_ * KH + 11, :], [d_aT[tb]], [t_ab[j]], f'h_a{j}')
            dma('sp', ab_[j][:, 11:22, :], aT_scr[tb, :, p_ * KH + 11:(p_ + 1) * KH, :], [d_aT[tb]], [t_ab[j]], f'h_a{j}')

        def h_load_x(gi_):
            p_, tt_ = gsteps[gi_]
            src = x2_scr if p_ == 0 else x3_scr
            dep = d_x2[tt_ // 4] if p_ == 0 else d_x3[tt_ // 4]
            dma('sp', xin[gi_ % 2][:], src[tt_ * 128:(tt_ + 1) * 128, :], [dep], [t_xin[gi_ % 2]], f'h_xi{gi_ % 2}')

        def h_norm(gi_):
            p_, tt_ = gsteps[gi_]
            b_ = gi_ % 2
            ts('dve', ssq[b_][:, 1:2], ssq[b_][:, 0:1], 1.0 / D, EPS, ALU.mult, ALU.add, [t_ss[b_]], [t_ss[b_]])
            act(ssq[b_][:, 2:3], ssq[b_][:, 1:2], AF.Sqrt, [t_ss[b_]], [t_ss[b_]])
            recip(ssq[b_][:, 3:4], ssq[b_][:, 2:3], [t_ss[b_]], [t_ss[b_]])
            stt('dve', ot[b_][:], xo[b_][:], ssq[b_][:, 3:4], gb[:], ALU.mult, ALU.mult, [t_xo[b_], t_ss[b_], t_g], [t_ot[b_]])
            dma('sp', out[tt_ * 128:(tt_ + 1) * 128, :], ot[b_][:], [t_ot[b_]], [d_out[tt_ // 4]], f'h_so{b_}')

        h_load_a(0, 0)
        h_load_x(0)
        h_load_w(0)
        for gi_, (p_, tt_) in enumerate(gsteps):
            tb = tt_ // 4; sub = tt_ % 4
            if p_ == 1 and tt_ == 0:
                h_load_w(1)
            if sub == 0:
                nxt = p_ * 4 + tb + 1
                if nxt < 8:
                    h_load_a(nxt // 4, nxt % 4)
            if gi_ + 1 < len(gsteps):
                h_load_x(gi_ + 1)
            A = ab_[(p_ * 4 + tb) % 2]; ta = t_ab[(p_ * 4 + tb) % 2]
            b_ = gi_ % 2
            P = pd[gi_ % 2]; tp = t_pd[gi_ % 2]
            for cb in range(4):
                for k in range(KH):
                    mm(P[cb][:], A[:, k, sub * 128:(sub + 1) * 128], wdh[:, k, cb * 512:(cb + 1) * 512], k == 0, k == KH - 1, [ta, t_wdh[cb]], [tp[cb]])
            for cb in range(4):
                tt('dve', xo[b_][:, cb * 512:(cb + 1) * 512], P[cb][:], xin[b_][:, cb * 512:(cb + 1) * 512], ALU.add, [tp[cb], t_xin[b_]], [t_xo[b_]])
            if p_ == 0:
                dma('sp', x3_scr[tt_ * 128:(tt_ + 1) * 128, :], xo[b_][:], [t_xo[b_]], [d_x3[tb]], f'h_sx{b_}')
            else:
                act(ot[b_][:], xo[b_][:], AF.Square, [t_xo[b_]], [t_ot[b_], t_ss[b_]], accum=ssq[b_][:, 0:1])
                if tt_ > 0:
                    h_norm(gi_ - 1)
        h_norm(len(gsteps) - 1)
        end_phase()

    alld = d_hT + d_qT + d_kT + d_v + d_f + d_ga + d_gf + d_oT + d_yT + d_mT + d_x2 + d_h2T + d_aT + d_x3 + d_out
    S.finish('sp', alld)
    es.close()
    S.emit()
    return nc


def host_tables(half):
    pos = np.concatenate([np.arange(half * T_OWN, (half + 1) * T_OWN), np.arange((1 - half) * T_OWN, (2 - half) * T_OWN)])
    inv_freq = (np.float32(ROPE_THETA) ** (-np.arange(0, 32, 2, dtype=np.float32) / np.float32(32))).astype(np.float32)
    ang = pos.astype(np.float32)[:, None] * inv_freq[None, :]
    c = np.cos(ang).astype(np.float32).T
    s = np.sin(ang).astype(np.float32).T
    rope_c = np.ascontiguousarray(np.concatenate([c, c], 0))
    rope_s = np.ascontiguousarray(np.concatenate([-s, s], 0))
    prod = (pos[:, None].astype(np.int64) * pos[None, :T_OWN].astype(np.int64)) % S_FULL
    a = (2.0 * np.pi / S_FULL) * prod.astype(np.float64)
    dft_c = np.cos(a).astype(np.float32).astype(ml_dtypes.bfloat16)
    dft_s = np.sin(a).astype(np.float32).astype(ml_dtypes.bfloat16)
    return rope_c, rope_s, dft_c, dft_s


def host_consts():
    perm = np.zeros((32, 32), np.float32)
    for m in range(32):
        perm[(m + 16) % 32, m] = 1.0
    ident = np.eye(128, dtype=np.float32)
    k = np.arange(128)
    a = (2.0 * np.pi / 128) * ((k[:, None] * k[None, :]) % 128).astype(np.float64)
    nrm = 1.0 / math.sqrt(S_FULL * 128.0)
    cd = (np.cos(a) * nrm).astype(np.float32)
    nsd = (-np.sin(a) * nrm).astype(np.float32)
    return perm, ident, cd, nsd


def make_in_maps(x, g_mix, w_in, lambda_q1, lambda_k1, lambda_q2, lambda_k2, g_subln,
                 w_attn_branch, w_four_branch, w_out, g_ffn, w_gate, w_up, w_down, g_final):
    f = lambda a: np.ascontiguousarray(np.asarray(a, dtype=np.float32))
    x = f(x)
    perm, ident, cd, nsd = host_consts()
    tabs = [host_tables(0), host_tables(1)]
    common = {
        "w_in": f(w_in), "w_attn": f(w_attn_branch), "w_four": f(w_four_branch), "w_out": f(w_out),
        "w_gate": f(w_gate), "w_up": f(w_up), "w_down": f(w_down),
        "g_mix": f(g_mix).reshape(1, D), "g_ffn": f(g_ffn).reshape(1, D), "g_final": f(g_final).reshape(1, D),
        "g_sub": np.ascontiguousarray(f(g_subln).reshape(2, 128).T),
        "lams": np.ascontiguousarray(np.stack([f(lambda_q1), f(lambda_k1), f(lambda_q2), f(lambda_k2)], 0)),
        "perm": perm, "ident": ident, "dft_cd": cd, "dft_nsd": nsd,
    }
    in_maps = []
    for c in range(8):
        b, half = c // 2, c % 2
        xs = np.ascontiguousarray(np.concatenate(
            [x[b, half * T_OWN:(half + 1) * T_OWN], x[b, (1 - half) * T_OWN:(2 - half) * T_OWN]], 0))
        rope_c, rope_s, dft_c, dft_s = tabs[half]
        m = dict(common)
        m.update({"xs": xs, "rope_c": rope_c, "rope_s": rope_s, "dft_c": dft_c, "dft_s": dft_s})
        in_maps.append(m)
    return in_maps


_NC_CACHE = {}


def kernel(**inputs):
    in_maps = make_in_maps(**inputs)
    if 'nc' not in _NC_CACHE:
        _NC_CACHE['nc'] = build_nc()
    nc = _NC_CACHE['nc']
    res = run_bass_kernel_spmd(nc, in_maps, core_ids=list(range(8)))
    outp = np.empty((4, S_FULL, D), np.float32)
    for c in range(8):
        b, half = c // 2, c % 2
        outp[b, half * T_OWN:(half + 1) * T_OWN] = res.results[c]["out"]
    return outp
```

```python
import contextlib
import math
import numpy as np
import ml_dtypes
import concourse.bass as bass
import concourse.mybir as mybir
from concourse.bass_utils import run_bass_kernel_spmd

F32 = mybir.dt.float32
BF16 = mybir.dt.bfloat16
AF = mybir.ActivationFunctionType
ALU = mybir.AluOpType
AX = mybir.AxisListType

D = 2048
S_FULL = 4096
T_OWN = 2048
NH = 6
QK_W = 1536
V_W = 1536
F_W = 512
IN_W = 9216
DFF = 5632
LAMBDA_INIT = 0.8 - 0.6 * math.exp(-0.3 * 0)
EPS = 1e-5
ROPE_THETA = 500000.0
ENGS = ['pe', 'act', 'dve', 'pool', 'sp']


class Tile:
    __slots__ = ('name', 'last_w', 'readers')

    def __init__(self, name):
        self.name = name
        self.last_w = None
        self.readers = []


class Op:
    __slots__ = ('eng', 'idx', 'fn', 'waits', 'signal', 'count', 'is_dma', 'sem')


class Sched:
    def __init__(self, nc):
        self.nc = nc
        self.ops = {e: [] for e in ENGS}
        self.waited = {e: {} for e in ENGS}
        self.dma_cnt = {}

    def tile(self, name="t"):
        return Tile(name)

    def tiles(self, n, name="t"):
        return [Tile(f"{name}{i}") for i in range(n)]

    def _add_wait(self, op, d, kind, force=False):
        w = self.waited[op.eng]
        if d.is_dma:
            key = ('dma', d.sem)
            cnt = self.dma_cnt[d.sem]
            if op.is_dma and op.sem == d.sem:
                cnt -= 1
            val = cnt * 16
            if w.get(key, 0) >= val:
                return
            w[key] = val
            op.waits.append((key, val, None))
        else:
            if d.eng == op.eng and not op.is_dma and kind != 'raw' and not force:
                return
            key = ('eng', d.eng)
            if w.get(key, -1) >= d.idx:
                return
            w[key] = d.idx
            d.signal = True
            op.waits.append((key, None, d))

    def _deps(self, op, reads, writes):
        deps = []
        for t in reads:
            if t.last_w is not None:
                deps.append((t.last_w, 'raw'))
        for t in writes:
            if t.last_w is not None:
                deps.append((t.last_w, 'waw'))
            for r in t.readers:
                deps.append((r, 'war'))
        for t in reads:
            t.readers.append(op)
        for t in writes:
            t.last_w = op
            t.readers = []
        for d, kind in deps:
            if d is not op:
                self._add_wait(op, d, kind)

    def _new(self, eng, fn, is_dma, sem):
        o = Op()
        o.eng = eng; o.fn = fn; o.waits = []; o.signal = False; o.is_dma = is_dma; o.sem = sem
        o.count = 0
        o.idx = len(self.ops[eng])
        self.ops[eng].append(o)
        return o

    def op(self, eng, fn, reads=(), writes=()):
        o = self._new(eng, fn, False, None)
        self._deps(o, reads, writes)
        return o

    def dma(self, eng, fn, reads=(), writes=(), sem=None):
        self.dma_cnt[sem] = self.dma_cnt.get(sem, 0) + 1
        o = self._new(eng, fn, True, sem)
        self._deps(o, reads, writes)
        return o

    def barrier(self, tiles):
        deps = []
        for t in tiles:
            if t.last_w is not None:
                deps.append(t.last_w)
            deps.extend(t.readers)
        deps.sort(key=lambda d: -d.idx)
        for e in ENGS:
            o = self._new(e, None, False, None)
            for d in deps:
                self._add_wait(o, d, 'raw', force=True)

    def wait_on(self, eng, target):
        o = self._new(eng, None, False, None)
        self._add_wait(o, target, 'raw', force=True)

    def finish(self, eng, tiles):
        self.op(eng, None, reads=tiles)

    def emit(self):
        nc = self.nc
        for e in ENGS:
            c = 0
            for o in self.ops[e]:
                if o.signal and not o.is_dma:
                    c += 1
                    o.count = c
        with contextlib.ExitStack() as st:
            esem = {e: st.enter_context(nc.semaphore(f"s_{e}")) for e in ENGS}
            dsem = {k: st.enter_context(nc.semaphore(f"d_{k}")) for k in self.dma_cnt}
            block = st.enter_context(nc.Block())

            def run(e, eng):
                for o in self.ops[e]:
                    for key, val, d in o.waits:
                        if key[0] == 'dma':
                            eng.wait_ge(dsem[key[1]], val)
                        else:
                            eng.wait_ge(esem[key[1]], d.count)
                    if o.fn is None:
                        continue
                    ins = o.fn(eng)
                    if o.is_dma:
                        ins.then_inc(dsem[o.sem], 16)
                    elif o.signal:
                        ins.then_inc(esem[e], 1)

            @block.tensor
            def _(eng):
                run('pe', eng)

            @block.scalar
            def _(eng):
                run('act', eng)

            @block.vector
            def _(eng):
                run('dve', eng)

            @block.gpsimd
            def _(eng):
                run('pool', eng)

            @block.sync
            def _(eng):
                run('sp', eng)


def build_nc(dbg=False, stop_after=None):
    nc = bass.Bass("TRN2", target_bir_lowering=False)
    S = Sched(nc)

    def din(name, shape):
        return nc.dram_tensor(name, shape, F32, kind="ExternalInput").ap()

    def dscr(name, shape, dt):
        kind = "ExternalOutput" if (dbg and name in dbg) else "Internal"
        return nc.dram_tensor(name, shape, dt, kind=kind).ap()

    xs = din("xs", [S_FULL, D])
    w_in = din("w_in", [D, IN_W])
    w_attn = din("w_attn", [V_W, D])
    w_four = din("w_four", [F_W, D])
    w_out = din("w_out", [D, D])
    w_gate = din("w_gate", [D, DFF])
    w_up = din("w_up", [D, DFF])
    w_down = din("w_down", [DFF, D])
    g_mix = din("g_mix", [1, D])
    g_ffn = din("g_ffn", [1, D])
    g_final = din("g_final", [1, D])
    g_sub = din("g_sub", [128, 2])
    lams = din("lams", [4, 128])
    rope_c = din("rope_c", [32, S_FULL])
    rope_s = din("rope_s", [32, S_FULL])
    perm = din("perm", [32, 32])
    ident = din("ident", [128, 128])
    dft_c = nc.dram_tensor("dft_c", [S_FULL, T_OWN], BF16, kind="ExternalInput").ap()
    dft_s = nc.dram_tensor("dft_s", [S_FULL, T_OWN], BF16, kind="ExternalInput").ap()
    dft_cd = din("dft_cd", [128, 128])
    dft_nsd = din("dft_nsd", [128, 128])
    out = nc.dram_tensor("out", [T_OWN, D], F32, kind="ExternalOutput").ap()

    hT_scr = dscr("hT_scr", [8, 128, 16, 512], BF16)
    qT_scr = dscr("qT_scr", [12, 128, T_OWN], BF16)
    kT_scr = dscr("kT_scr", [12, 128, S_FULL], BF16)
    v_scr = dscr("v_scr", [S_FULL, V_W], BF16)
    f_scr = dscr("f_scr", [S_FULL, F_W], BF16)
    ga_scr = dscr("ga_scr", [4, 128, 16, 512], BF16)
    gf_scr = dscr("gf_scr", [4, 128, 16, 512], BF16)
    oT_scr = dscr("oT_scr", [4, 128, 12, 512], BF16)
    yT_scr = dscr("yT_scr", [4, 128, 4, 512], BF16)
    mT_scr = dscr("mT_scr", [4, 128, 16, 512], BF16)
    x2_scr = dscr("x2_scr", [T_OWN, D], F32)
    h2T_scr = dscr("h2T_scr", [4, 128, 16, 512], BF16)
    aT_scr = dscr("aT_scr", [4, 128, 44, 512], BF16)
    x3_scr = dscr("x3_scr", [T_OWN, D], F32)

    cv_src = {'wa': w_attn, 'wf': w_four, 'wo': w_out, 'wg': w_gate, 'wu': w_up, 'wd': w_down}
    cv_dst = {k: dscr("cvb_" + k, list(v.shape), BF16) for k, v in cv_src.items()}
    cv_tiles = {k: S.tiles(v.shape[0] // 128, "cv_" + k) for k, v in cv_src.items()}

    cv_sem = {'wa': 'cv_e', 'wf': 'cv_e', 'wo': 'cv_e', 'wg': 'cv_f', 'wu': 'cv_f', 'wd': 'cv_g'}

    cv_list = [(k, r) for k in ['wa', 'wf', 'wo', 'wg', 'wu'] for r in range(cv_src[k].shape[0] // 128)]
    cv_list2 = [('wd', r) for r in range(cv_src['wd'].shape[0] // 128)]
    cv_pos = [0, 0]

    def issue_conversions(n, which=0):
        lst = cv_list if which == 0 else cv_list2
        for k, r in lst[cv_pos[which]:cv_pos[which] + n]:
            dma('pool', cv_dst[k][r * 128:(r + 1) * 128, :], cv_src[k][r * 128:(r + 1) * 128, :], [], [cv_tiles[k][r]], cv_sem[k])
        cv_pos[which] += n

    def load_wb(tile3, key, r0, k_tot, c0, cn, writes, sem, ksplit=4):
        wb = cv_dst[key]
        for k0 in range(0, k_tot, ksplit):
            k1 = min(k_tot, k0 + ksplit)
            dma('sp', tile3[:, k0:k1, :],
                wb[r0 + k0 * 128:r0 + k1 * 128, c0:c0 + cn].rearrange("(k p) n -> p k n", p=128),
                cv_tiles[key][(r0 // 128) + k0:(r0 // 128) + k1], writes, sem)

    def dtiles(n):
        return S.tiles(n, "dr")

    d_hT = dtiles(8); d_qT = dtiles(4); d_kT = dtiles(8); d_v = dtiles(8); d_f = dtiles(8)
    d_ga = dtiles(4); d_gf = dtiles(4); d_oT = dtiles(4); d_yT = dtiles(4); d_mT = dtiles(4)
    d_x2 = dtiles(4); d_h2T = dtiles(4); d_aT = dtiles(4); d_x3 = dtiles(4); d_out = dtiles(4)

    def mm(o, l, r, st, sp, reads, writes):
        S.op('pe', lambda e: e.matmul(o, lhsT=l, rhs=r, start=st, stop=sp), reads, writes)

    def tr(o, i, idn, reads, writes):
        S.op('pe', lambda e: e.transpose(out=o, in_=i, identity=idn), reads, writes)

    def act(o, i, func, reads, writes, scale=1.0, bias=None, accum=None):
        kw = {}
        if bias is not None:
            kw['bias'] = bias
        if accum is not None:
            kw['accum_out'] = accum
        S.op('act', lambda e: e.activation(out=o, in_=i, func=func, scale=scale, **kw), reads, writes)

    def tt(eng, o, a, b, op, reads, writes):
        S.op(eng, lambda e: e.tensor_tensor(out=o, in0=a, in1=b, op=op), reads, writes)

    def ts(eng, o, a, s1, s2, op0, op1, reads, writes):
        if s2 is None:
            S.op(eng, lambda e: e.tensor_scalar(out=o, in0=a, scalar1=s1, scalar2=None, op0=op0), reads, writes)
        else:
            S.op(eng, lambda e: e.tensor_scalar(out=o, in0=a, scalar1=s1, scalar2=s2, op0=op0, op1=op1), reads, writes)

    def stt(eng, o, a, s, b, op0, op1, reads, writes):
        S.op(eng, lambda e: e.scalar_tensor_tensor(out=o, in0=a, scalar=s, in1=b, op0=op0, op1=op1), reads, writes)

    def cp(eng, o, i, reads, writes):
        if eng == 'act':
            S.op('act', lambda e: e.copy(out=o, in_=i), reads, writes)
        else:
            S.op(eng, lambda e: e.tensor_copy(out=o, in_=i), reads, writes)

    def recip(o, i, reads, writes):
        S.op('dve', lambda e: e.reciprocal(out=o, in_=i), reads, writes)

    def dma(q, o, i, reads, writes, sem):
        S.dma(q, lambda e: e.dma_start(out=o, in_=i), reads, writes, sem)

    def bcast_rows(ap_row, n):
        return bass.AP(ap_row.tensor, ap_row.offset, [[0, 128], [1, n]])

    def load_kblock(q, tile3, src3, k_tot, t0, tn, reads, writes, sem, ksplit=4):
        for k0 in range(0, k_tot, ksplit):
            k1 = min(k_tot, k0 + ksplit)
            dma(q, tile3[:, k0:k1, :], src3[k0:k1, :, t0:t0 + tn].rearrange("k p t -> p k t"), reads, writes, sem)

    def store_kblock(q, dst3, tile3, k_tot, t0, tn, reads, writes, sem, ksplit=4):
        for k0 in range(0, k_tot, ksplit):
            k1 = min(k_tot, k0 + ksplit)
            dma(q, dst3[k0:k1, :, t0:t0 + tn].rearrange("k p t -> p k t"), tile3[:, k0:k1, :], reads, writes, sem)

    def load_blk(q, tile3, scr4, blk, k_tot, reads, writes, sem, ksplit=8):
        for k0 in range(0, k_tot, ksplit):
            k1 = min(k_tot, k0 + ksplit)
            dma(q, tile3[:, k0:k1, :], scr4[blk, :, k0:k1, :], reads, writes, sem)

    def store_blk(q, scr4, tile3, blk, k_tot, reads, writes, sem, ksplit=8):
        for k0 in range(0, k_tot, ksplit):
            k1 = min(k_tot, k0 + ksplit)
            dma(q, scr4[blk, :, k0:k1, :], tile3[:, k0:k1, :], reads, writes, sem)

    def load_w(tile3, w2, r0, k_tot, c0, cn, writes, sem, ksplit=4):
        for k0 in range(0, k_tot, ksplit):
            k1 = min(k_tot, k0 + ksplit)
            dma('pool', tile3[:, k0:k1, :],
                w2[r0 + k0 * 128:r0 + k1 * 128, c0:c0 + cn].rearrange("(k p) n -> p k n", p=128), [], writes, sem)

    es = contextlib.ExitStack()

    def sb(name, shape, dt):
        return es.enter_context(nc.sbuf_tensor(name, shape, dt))

    def ps(name, shape, dt=F32):
        return es.enter_context(nc.psum_tensor(name, shape, dt))

    phase_tiles = []

    def T(name="t"):
        t = S.tile(name)
        phase_tiles.append(t)
        return t

    def Ts(n, name="t"):
        return [T(name) for _ in range(n)]

    def end_phase():
        nonlocal es
        S.barrier(phase_tiles)
        phase_tiles.clear()
        es.close()
        es = contextlib.ExitStack()

    done = [False]

    def stop(name):
        if stop_after == name:
            done[0] = True
        return done[0]

    def norm_sq(xt_ap, ssq, junk, t_x, t_ss, t_junk):
        act(junk[:], xt_ap, AF.Square, [t_x], [t_junk, t_ss], accum=ssq[:, 0:1])

    def norm_rest(xt_ap, gb, idb, ssq, hb, pT, t_x, t_g, t_id, t_ss, t_hb, t_pT):
        ts('dve', ssq[:, 1:2], ssq[:, 0:1], 1.0 / D, EPS, ALU.mult, ALU.add, [t_ss], [t_ss])
        act(ssq[:, 2:3], ssq[:, 1:2], AF.Sqrt, [t_ss], [t_ss])
        recip(ssq[:, 3:4], ssq[:, 2:3], [t_ss], [t_ss])
        stt('dve', hb[:], xt_ap, ssq[:, 3:4], gb[:], ALU.mult, ALU.mult, [t_x, t_ss, t_g], [t_hb])
        for k in range(16):
            tr(pT[:, k, :], hb[:, k * 128:(k + 1) * 128], idb[:], [t_hb, t_id], [t_pT])

    es_ab = contextlib.ExitStack()
    hT_own = es_ab.enter_context(nc.sbuf_tensor("hT_own", [128, 16, T_OWN], BF16))
    t_hown = S.tiles(4, "hown")
    if True:
        gb = sb("a_gb", [128, D], F32); t_g = T()
        idb = sb("a_id", [128, 128], BF16); t_id = T()
        xts = [sb(f"a_x{i}", [128, D], F32) for i in range(3)]; t_xs = Ts(3)
        junk = sb("a_junk", [128, D], F32); t_junk = T()
        hb = [sb(f"a_hb{i}", [128, D], BF16) for i in range(2)]; t_hb = Ts(2)
        ssq = [sb(f"a_ss{i}", [128, 4], F32) for i in range(2)]; t_ss = Ts(2)
        hTb = [sb(f"a_hT{i}", [128, 16, 512], BF16) for i in range(2)]; t_hT = Ts(2)
        pT = [ps(f"a_pT{i}", [128, 16, 128], BF16) for i in range(2)]; t_pT = Ts(2)
        dma('sp', gb[:], bcast_rows(g_mix, D), [], [t_g], 'a_g')
        dma('pool', idb[:], ident, [], [t_id], 'a_id')
        NT = S_FULL // 128

        def a_load(i):
            dma('sp', xts[i % 3][:], xs[i * 128:(i + 1) * 128, :], [], [t_xs[i % 3]], f'a_x{i % 3}')

        def a_part2(i):
            b = i % 2; blk = i // 4; sub = i % 4
            eng = 'act' if i % 2 else 'dve'
            if blk < 4:
                cp(eng, hT_own[:, :, i * 128:(i + 1) * 128], pT[b][:], [t_pT[b]], [t_hown[blk]])
            else:
                cp(eng, hTb[blk % 2][:, :, sub * 128:(sub + 1) * 128], pT[b][:], [t_pT[b]], [t_hT[blk % 2]])
                if sub == 3:
                    store_blk('sp', hT_scr, hTb[blk % 2], blk, 16, [t_hT[blk % 2]], [d_hT[blk]], f'a_st{blk % 2}')

        a_load(0)
        a_load(1)
        norm_sq(xts[0][:], ssq[0], junk, t_xs[0], t_ss[0], t_junk)
        for i in range(NT):
            if i + 2 < NT:
                a_load(i + 2)
            if i + 1 < NT:
                j = i + 1
                norm_sq(xts[j % 3][:], ssq[j % 2], junk, t_xs[j % 3], t_ss[j % 2], t_junk)
            b = i % 2
            norm_rest(xts[i % 3][:], gb, idb, ssq[b], hb[b], pT[b], t_xs[i % 3], t_g, t_id, t_ss[b], t_hb[b], t_pT[b])
            if i > 0:
                a_part2(i - 1)
        a_part2(NT - 1)
        end_phase()

    if not stop('A'):
        wsl = [sb(f"b_w{i}", [128, 16, 512], BF16) for i in range(2)]; t_w = Ts(2)
        hbk = [sb(f"b_h{i}", [128, 16, 512], BF16) for i in range(2)]; t_h = Ts(2)
        stg = [sb(f"b_s{i}", [128, 4, 512], BF16) for i in range(2)]; t_stg = [Ts(4) for _ in range(2)]
        rc = sb("b_rc", [32, S_FULL], F32); rs = sb("b_rs", [32, S_FULL], F32); t_rope = T()
        pmb = sb("b_pm", [32, 32], BF16); t_pm = T()
        r1 = [sb(f"b_r1{i}", [32, 512], F32) for i in range(2)]; t_r1 = Ts(2)
        r2 = [sb(f"b_r2{i}", [32, 512], F32) for i in range(2)]; t_r2 = Ts(2)
        pz = [ps(f"b_pz{i}", [128, 512]) for i in range(6)]; t_pz = Ts(6)
        psw = [ps(f"b_psw{i}", [32, 512]) for i in range(2)]; t_psw = Ts(2)
        dma('sp', rc[:], rope_c, [], [t_rope], 'b_rope')
        dma('sp', rs[:], rope_s, [], [t_rope], 'b_rope')
        dma('pool', pmb[:], perm, [], [t_pm], 'b_pm')

        kinds = ['q'] * 3 + ['k'] * 3 + ['v'] * 3 + ['f'] + ['ga'] * 4 + ['gf'] * 4
        steps = []
        for sl in range(18):
            nblk = 8 if kinds[sl] in ('k', 'v', 'f') else 4
            for tb in range(nblk):
                steps.append((sl, tb))
        lsteps = [si for si, (sl, tb) in enumerate(steps) if tb >= 4]
        lslot = {si: n % 2 for n, si in enumerate(lsteps)}
        l_issued = 0
        l_done = 0

        def b_load_w(sl):
            load_w(wsl[sl % 2], w_in, 0, 16, sl * 512, 512, [t_w[sl % 2]], f'b_w{sl % 2}')

        def b_issue_loads():
            nonlocal l_issued
            while l_issued < len(lsteps) and l_issued < l_done + 2:
                si_ = lsteps[l_issued]
                j = lslot[si_]
                tb_ = steps[si_][1]
                load_blk('sp', hbk[j], hT_scr, tb_, 16, [d_hT[tb_]], [t_h[j]], f'b_h{j}')
                l_issued += 1

        b_load_w(0)
        pzi = 0
        swi = 0
        for si, (sl, tb) in enumerate(steps):
            if tb == 0 and sl + 1 < 18:
                b_load_w(sl + 1)
            b_issue_loads()
            kind = kinds[sl]
            W = wsl[sl % 2]; tw = t_w[sl % 2]
            if tb < 4:
                th = t_hown[tb]

                def Hs(k, a_, b_, tb=tb):
                    return hT_own[:, k, tb * 512 + a_:tb * 512 + b_]
            else:
                th = t_h[lslot[si]]

                def Hs(k, a_, b_, j=lslot[si]):
                    return hbk[j][:, k, a_:b_]
            sg = stg[si % 2]; tsg = t_stg[si % 2]
            pending = None
            for c in range(4):
                P = pz[pzi % 6]; tp = t_pz[pzi % 6]; pzi += 1
                if kind in ('v', 'f'):
                    for k in range(16):
                        mm(P[:], Hs(k, c * 128, (c + 1) * 128), W[:, k, :], k == 0, k == 15, [th, tw], [tp])
                    cp('act' if c % 2 else 'dve', sg[:, c, :], P[:], [tp], [tsg[c]])
                else:
                    for k in range(16):
                        mm(P[:], W[:, k, c * 128:(c + 1) * 128], Hs(k, 0, 512), k == 0, k == 15, [th, tw], [tp])
                    if pending is not None:
                        pending(); pending = None
                    if kind in ('ga', 'gf'):
                        act(sg[:, c, :], P[:], AF.Sigmoid, [tp], [tsg[c]])
                    else:
                        cp('act', sg[:, c, :], P[:], [tp], [tsg[c]])

                        def rope(c=c, P=P, tp=tp):
                            nonlocal swi
                            SW = psw[swi % 2]; tsw = t_psw[swi % 2]
                            R1 = r1[swi % 2]; tr1 = t_r1[swi % 2]
                            R2 = r2[swi % 2]; tr2 = t_r2[swi % 2]
                            swi += 1
                            mm(SW[:], pmb[:], sg[0:32, c, :], True, True, [tsg[c], t_pm], [tsw])
                            tt('dve', R1[:], SW[:], rs[:, tb * 512:(tb + 1) * 512], ALU.mult, [tsw, t_rope], [tr1])
                            tt('dve', R2[:], P[0:32, :], rc[:, tb * 512:(tb + 1) * 512], ALU.mult, [tp, t_rope], [tr2])
                            tt('dve', sg[0:32, c, :], R1[:], R2[:], ALU.add, [tr1, tr2], [tsg[c]])
                        pending = rope
            if pending is not None:
                pending(); pending = None
            if tb >= 4:
                l_done += 1
            tsg = list(tsg)
            if kind == 'q':
                dma('sp', qT_scr[sl * 4:(sl + 1) * 4, :, tb * 512:(tb + 1) * 512].rearrange("k p t -> p k t"), sg[:], tsg, [d_qT[tb]], f'b_st{si % 2}')
            elif kind == 'k':
                c0 = (sl - 3) * 4
                dma('sp', kT_scr[c0:c0 + 4, :, tb * 512:(tb + 1) * 512].rearrange("k p t -> p k t"), sg[:], tsg, [d_kT[tb]], f'b_st{si % 2}')
            elif kind == 'v':
                c0 = (sl - 6) * 512
                dma('sp', v_scr[tb * 512:(tb + 1) * 512, c0:c0 + 512].rearrange("(c p) n -> p c n", p=128), sg[:], tsg, [d_v[tb]], f'b_st{si % 2}')
            elif kind == 'f':
                dma('sp', f_scr[tb * 512:(tb + 1) * 512, :].rearrange("(c p) n -> p c n", p=128), sg[:], tsg, [d_f[tb]], f'b_st{si % 2}')
            elif kind == 'ga':
                c0 = (sl - 10) * 4
                dma('sp', ga_scr[tb, :, c0:c0 + 4, :], sg[:], tsg, [d_ga[tb]], f'b_st{si % 2}')
            else:
                c0 = (sl - 14) * 4
                dma('sp', gf_scr[tb, :, c0:c0 + 4, :], sg[:], tsg, [d_gf[tb]], f'b_st{si % 2}')
        phase_tiles.extend(t_hown)
        end_phase()
    es_ab.close()

    if not stop('A') and not stop('B'):
        kT = [sb(f"c_k{i}", [128, 2, S_FULL], BF16) for i in range(2)]; t_k = Ts(2)
        vh = [sb(f"c_v{i}", [128, 32, 256], BF16) for i in range(2)]; t_v = Ts(2)
        qh = [sb(f"c_q{i}", [128, 2, 512], BF16) for i in range(2)]; t_q = Ts(2)
        ones = sb("c_ones", [128, 128], BF16); t_ones = T()
        pT4 = sb("c_p4", [128, 8, 512], BF16); t_p = Ts(8)
        gs2 = [sb(f"c_gs2{i}", [128, 2, 512], BF16) for i in range(2)]; t_gs2 = Ts(2)
        gs1 = [sb(f"c_gs1{i}", [128, 512], BF16) for i in range(2)]; t_gs1 = Ts(2)
        lam_in = sb("c_lam", [128, 4, 128], F32); t_lam = T()
        lam_w = sb("c_lamw", [128, 2, 128], F32)
        lam_s = sb("c_lams", [128, 8], F32)
        gsb = sb("c_gs", [128, 2], F32); gss = sb("c_gss", [128, 2], F32); t_gs = T()
        epsb = sb("c_eps", [128, 1], F32)
        rr = sb("c_rr", [128, 512], F32); t_rr = T()
        oa = sb("c_oa", [128, 2, 512], F32); t_oa = T()
        ob = sb("c_ob", [128, 2, 512], F32); t_ob = T()
        oo = sb("c_oo", [128, 2, 512], F32); t_oo = T()
        sq = sb("c_sq", [128, 2, 512], BF16); t_sq = T()
        rstd = sb("c_rstd", [128, 512], F32); t_rstd = T()
        ssb = sb("c_ssb", [128, 512], F32); t_ssb = T()
        on = [sb(f"c_on{i}", [128, 2, 512], BF16) for i in range(2)]; t_on = Ts(2)
        psc = [ps(f"c_sc{i}", [128, 512]) for i in range(3)]; t_sc = Ts(3)
        pacc = [[ps(f"c_acc{s}{j}", [128, 512]) for j in range(2)] for s in range(2)]
        t_acc = [Ts(2) for _ in range(2)]
        psm = ps("c_psm", [128, 512]); t_psm = T()

        S.op('dve', lambda e: e.memset(ones[:], 1.0), [], [t_ones])
        S.op('dve', lambda e: e.memset(epsb[:], EPS), [], [t_gs])
        dma('sp', lam_in[:].rearrange("p a b -> p (a b)"), bass.AP(lams.tensor, 0, [[0, 128], [1, 512]]), [], [t_lam], 'c_lam')
        dma('sp', gsb[:], g_sub, [], [t_gs], 'c_gs')
        tt('dve', lam_w[:, 0, :], lam_in[:, 0, :], lam_in[:, 1, :], ALU.mult, [t_lam], [t_lam])
        tt('dve', lam_w[:, 1, :], lam_in[:, 2, :], lam_in[:, 3, :], ALU.mult, [t_lam], [t_lam])
        S.op('dve', lambda e: e.reduce_sum(out=lam_s[:, 0:1], in_=lam_w[:, 0, :], axis=AX.X), [t_lam], [t_lam])
        S.op('dve', lambda e: e.reduce_sum(out=lam_s[:, 1:2], in_=lam_w[:, 1, :], axis=AX.X), [t_lam], [t_lam])
        act(lam_s[:, 2:4], lam_s[:, 0:2], AF.Exp, [t_lam], [t_lam])
        tt('dve', lam_s[:, 4:5], lam_s[:, 3:4], lam_s[:, 2:3], ALU.subtract, [t_lam], [t_lam])
        ts('dve', lam_s[:, 5:6], lam_s[:, 4:5], -LAMBDA_INIT, None, ALU.add, None, [t_lam], [t_lam])
        ts('dve', gss[:], gsb[:], 1.0 - LAMBDA_INIT, None, ALU.mult, None, [t_gs], [t_gs])
        neg_lam = lam_s[:, 5:6]
        scale = 1.0 / math.sqrt(128.0)

        csteps = [(h, qb) for h in range(NH) for qb in range(4)]

        def c_load_kv(h):
            b = h % 2
            for c in range(2):
                for t0 in range(0, S_FULL, 1024):
                    dma('sp', kT[b][:, c, t0:t0 + 1024], kT_scr[2 * h + c, :, t0:t0 + 1024], d_kT, [t_k[b]], f'c_k{b}')
            for c0 in range(0, 32, 8):
                dma('sp', vh[b][:, c0:c0 + 8, :],
                    v_scr[c0 * 128:(c0 + 8) * 128, h * 256:(h + 1) * 256].rearrange("(c p) e -> p c e", p=128),
                    d_v, [t_v[b]], f'c_v{b}')

        def c_load_q(si):
            h, qb = csteps[si]
            dma('sp', qh[si % 2][:], qT_scr[2 * h:2 * h + 2, :, qb * 512:(qb + 1) * 512].rearrange("k p t -> p k t"),
                [d_qT[qb]], [t_q[si % 2]], f'c_q{si % 2}')

        c_load_kv(0)
        c_load_q(0)
        items = [(si, c, kc) for si in range(len(csteps)) for c in range(2) for kc in range(32)]

        def rec_S(i):
            si, c, kc = items[i]
            h, qb = csteps[si]
            mm(psc[i % 3][:], kT[h % 2][:, c, kc * 128:(kc + 1) * 128], qh[si % 2][:, c, :], True, True,
               [t_k[h % 2], t_q[si % 2]], [t_sc[i % 3]])

        def epi1a(si):
            for j in range(2):
                stt('dve', oo[:, j, :], ob[:, j, :], neg_lam, oa[:, j, :], ALU.mult, ALU.add, [t_ob, t_oa, t_lam], [t_oo])

        def epi1b(si):
            act(sq[:].rearrange("p a b -> p (a b)"), oo[:].rearrange("p a b -> p (a b)"), AF.Square, [t_oo], [t_sq])

        def epi2a(si, pb, tpb):
            for j in range(2):
                mm(pb[:], ones[:], sq[:, j, :], j == 0, j == 1, [t_ones, t_sq], [tpb])
            cp('dve', ssb[:], pb[:], [tpb], [t_ssb])

        def epi2b(si):
            act(rstd[:], ssb[:], AF.Ln, [t_ssb, t_gs], [t_rstd], scale=1.0 / 256.0, bias=epsb[:, 0:1])
            act(rstd[:], rstd[:], AF.Exp, [t_rstd], [t_rstd], scale=-0.5)

        def epi2c(si):
            h, qb = csteps[si]
            ON = on[si % 2]; ton = t_on[si % 2]
            for j in range(2):
                stt('dve', ON[:, j, :], oo[:, j, :], gss[:, j:j + 1], rstd[:], ALU.mult, ALU.mult, [t_oo, t_gs, t_rstd], [ton])
            dma('sp', oT_scr[qb, :, 2 * h:2 * h + 2, :], ON[:], [ton], [d_oT[qb]], f'c_st{si % 2}')

        def evac_a(si, c):
            act(rr[:], psm[:], AF.Ln, [t_psm], [t_rr])
            act(rr[:], rr[:], AF.Exp, [t_rr], [t_rr], scale=-1.0)

        def evac_b(si, c):
            dst = oa if c == 0 else ob
            tdst = t_oa if c == 0 else t_ob
            for j in range(2):
                tt('dve', dst[:, j, :], pacc[c][j][:], rr[:], ALU.mult, [t_acc[c][j], t_rr], [tdst])

        rec_S(0)
        rec_S(1)
        n_items = len(items)
        pending_evac = None
        pending_ones = None
        for i, (si, c, kc) in enumerate(items):
            h, qb = csteps[si]
            if i + 2 < n_items:
                rec_S(i + 2)
            slot = i % 8
            PT = pT4[:, slot, :]; tpp = t_p[slot]
            V = vh[h % 2]; tv = t_v[h % 2]
            ACC = pacc[c]; tacc = t_acc[c]
            act(PT, psc[i % 3][:], AF.Exp, [t_sc[i % 3]], [tpp], scale=scale)
            pace_op = S.ops['act'][-1]
            mm(ACC[0][:], V[:, kc, 0:128], PT, kc == 0, kc == 31, [tv, tpp], [tacc[0]])
            mm(ACC[1][:], V[:, kc, 128:256], PT, kc == 0, kc == 31, [tv, tpp], [tacc[1]])
            if c == 0 and kc == 0:
                if qb == 0 and h + 1 < NH:
                    c_load_kv(h + 1)
                if si + 1 < len(csteps):
                    c_load_q(si + 1)
            if c == 0 and kc == 16:
                S.wait_on('pool', pace_op)
                issue_conversions((len(cv_list) + len(csteps) - 1) // len(csteps))
            if kc % 4 == 1 and pending_ones is not None:
                pending_ones(); pending_ones = None
            if kc == 4 and pending_evac is not None:
                evac_a(*pending_evac)
            if kc == 6 and pending_evac is not None:
                evac_b(*pending_evac); pending_evac = None
            if kc % 4 == 3:
                base = slot - 3
                gp = (kc // 4) % 2
                tgrp = [t_p[base + u] for u in range(4)]
                tt('dve', gs2[gp][:], pT4[:, base:base + 2, :], pT4[:, base + 2:base + 4, :], ALU.add, tgrp, [t_gs2[gp]])
                tt('dve', gs1[gp][:], gs2[gp][:, 0, :], gs2[gp][:, 1, :], ALU.add, [t_gs2[gp]], [t_gs1[gp]])

                def ones_mm(gp=gp, kc=kc):
                    mm(psm[:], ones[:], gs1[gp][:], kc == 3, kc == 31, [t_ones, t_gs1[gp]], [t_psm])
                pending_ones = ones_mm
            if kc == 31:
                pending_evac = (si, c)
            if si > 0 and c == 0:
                if kc == 9:
                    epi1a(si - 1)
                elif kc == 14:
                    epi1b(si - 1)
                elif kc == 20:
                    epi2a(si - 1, psc[i % 3], t_sc[i % 3])
                elif kc == 26:
                    epi2b(si - 1)
                elif kc == 30:
                    epi2c(si - 1)
        pending_ones()
        evac_a(*pending_evac)
        evac_b(*pending_evac)
        if dbg and 'dbg_small' in dbg:
            dsm = nc.dram_tensor("dbg_small", [128, 10], F32, kind="ExternalOutput").ap()
            doa = nc.dram_tensor("dbg_oa", [128, 2048], F32, kind="ExternalOutput").ap()
            drr = nc.dram_tensor("dbg_rr", [128, 512], F32, kind="ExternalOutput").ap()
            t_dbg = S.tile("dbg")
            dma('sp', dsm[:, 0:8], lam_s[:], [t_lam], [t_dbg], 'dbg')
            dma('sp', dsm[:, 8:10], gss[:], [t_gs], [t_dbg], 'dbg')
            dma('sp', doa[:, 0:1024], oa[:].rearrange("p a b -> p (a b)"), [t_oa], [t_dbg], 'dbg')
            dma('sp', doa[:, 1024:2048], ob[:].rearrange("p a b -> p (a b)"), [t_ob], [t_dbg], 'dbg')
            dma('sp', drr[:], rr[:], [t_rr], [t_dbg], 'dbg')
            dk = nc.dram_tensor("dbg_k", [128, 2 * S_FULL], BF16, kind="ExternalOutput").ap()
            dv = nc.dram_tensor("dbg_v", [128, 32 * 256], BF16, kind="ExternalOutput").ap()
            dq = nc.dram_tensor("dbg_q", [128, 2 * 512], BF16, kind="ExternalOutput").ap()
            dma('sp', dk[:], kT[1][:].rearrange("p a b -> p (a b)"), [t_k[1]], [t_dbg], 'dbg')
            dma('sp', dv[:], vh[1][:].rearrange("p a b -> p (a b)"), [t_v[1]], [t_dbg], 'dbg')
            dma('sp', dq[:], qh[1][:].rearrange("p a b -> p (a b)"), [t_q[1]], [t_dbg], 'dbg')
            phase_tiles.append(t_dbg)
        epi1a(len(csteps) - 1)
        epi1b(len(csteps) - 1)
        epi2a(len(csteps) - 1, psc[0], t_sc[0])
        epi2b(len(csteps) - 1)
        epi2c(len(csteps) - 1)
        end_phase()

    if not stop('A') and not stop('B') and not stop('C'):
        fsb = sb("d_f", [128, 32, 512], BF16); t_f = T()
        tab = [sb(f"d_t{i}", [128, 16, 512], BF16) for i in range(8)]; t_tab = Ts(8)
        cd = sb("d_cd", [128, 128], BF16); nsd = sb("d_nsd", [128, 128], BF16); t_cd = T()
        ab = [sb(f"d_ab{i}", [128, 2, 512], BF16) for i in range(2)]; t_ab = Ts(2)
        yst = [sb(f"d_y{i}", [128, 512], BF16) for i in range(2)]; t_y = Ts(2)
        pab = [ps(f"d_pab{i}", [128, 512]) for i in range(4)]; t_pab = Ts(4)
        py = [ps(f"d_py{i}", [128, 512]) for i in range(2)]; t_py = Ts(2)
        for c0 in range(0, 32, 8):
            dma('sp', fsb[:, c0:c0 + 8, :], f_scr[c0 * 128:(c0 + 8) * 128, :].rearrange("(c p) n -> p c n", p=128), d_f, [t_f], 'd_f')
        cdf = sb("d_cdf", [128, 2, 128], F32); t_cdf = T()
        dma('sp', cdf[:, 0, :], dft_cd, [], [t_cdf], 'd_cd')
        dma('sp', cdf[:, 1, :], dft_nsd, [], [t_cdf], 'd_cd')
        cp('dve', cd[:], cdf[:, 0, :], [t_cdf], [t_cd])
        cp('dve', nsd[:], cdf[:, 1, :], [t_cdf], [t_cd])
        pieces = [(ob_, w, hf) for ob_ in range(4) for w in range(2) for hf in range(2)]

        def d_load(pidx):
            ob_, w, hf = pieces[pidx]
            src = dft_c if w == 0 else dft_s
            for k0 in range(0, 16, 8):
                r0 = hf * 2048 + k0 * 128
                dma('sp', tab[pidx % 8][:, k0:k0 + 8, :],
                    src[r0:r0 + 1024, ob_ * 512:(ob_ + 1) * 512].rearrange("(k p) n -> p k n", p=128),
                    [], [t_tab[pidx % 8]], f'd_t{pidx % 8}')

        for p_ in range(4):
            d_load(p_)
        gi = 0
        for ob_ in range(4):
            if ob_ + 1 < 4:
                for p_ in range(4):
                    d_load((ob_ + 1) * 4 + p_)
            for g in range(4):
                for w in range(2):
                    P = pab[(gi * 2 + w) % 4]; tp = t_pab[(gi * 2 + w) % 4]
                    for hf in range(2):
                        pidx = ob_ * 4 + w * 2 + hf
                        TB = tab[pidx % 8]; ttb = t_tab[pidx % 8]
                        for k in range(16):
                            sc_ = hf * 16 + k
                            mm(P[:], fsb[:, sc_, g * 128:(g + 1) * 128], TB[:, k, :], sc_ == 0, sc_ == 31, [t_f, ttb], [tp])
                    cp('act' if w else 'dve', ab[gi % 2][:, w, :], P[:], [tp], [t_ab[gi % 2]])
                PY = py[gi % 2]; tpy = t_py[gi % 2]
                mm(PY[:], cd[:], ab[gi % 2][:, 0, :], True, False, [t_cd, t_ab[gi % 2]], [tpy])
                mm(PY[:], nsd[:], ab[gi % 2][:, 1, :], False, True, [t_cd, t_ab[gi % 2]], [tpy])
                cp('act', yst[gi % 2][:], PY[:], [tpy], [t_y[gi % 2]])
                dma('sp', yT_scr[ob_, :, g, :], yst[gi % 2][:], [t_y[gi % 2]], [d_yT[ob_]], f'd_st{gi % 2}')
                gi += 1
        end_phase()

    if not any(stop(x) for x in 'ABCD'):
        wa = sb("e_wa", [128, 12, D], BF16); t_wa = Ts(4)
        wf = sb("e_wf", [128, 4, D], BF16); t_wf = Ts(4)
        onb = [sb(f"e_on{i}", [128, 12, 512], BF16) for i in range(2)]; t_onb = Ts(2)
        yb = [sb(f"e_y{i}", [128, 4, 512], BF16) for i in range(2)]; t_yb = Ts(2)
        gab = [sb(f"e_ga{i}", [128, 512], BF16) for i in range(4)]; t_gab = Ts(4)
        gfb = [sb(f"e_gf{i}", [128, 512], BF16) for i in range(4)]; t_gfb = Ts(4)
        m1 = [sb(f"e_m1{i}", [128, 512], F32) for i in range(2)]; t_m1 = Ts(2)
        m2 = [sb(f"e_m2{i}", [128, 512], F32) for i in range(2)]; t_m2 = Ts(2)
        mst = [sb(f"e_ms{i}", [128, 16, 512], BF16) for i in range(2)]; t_mst = Ts(2)
        pa = [ps(f"e_pa{i}", [128, 512]) for i in range(3)]; t_pa = Ts(3)
        pf = [ps(f"e_pf{i}", [128, 512]) for i in range(3)]; t_pf = Ts(3)

        def e_load(tb):
            b = tb % 2
            load_blk('sp', onb[b], oT_scr, tb, 12, [d_oT[tb]], [t_onb[b]], f'e_on{b}', ksplit=12)
            load_blk('sp', yb[b], yT_scr, tb, 4, [d_yT[tb]], [t_yb[b]], f'e_y{b}')

        e_load(0)
        for cq in range(4):
            load_wb(wf[:, :, cq * 512:(cq + 1) * 512], 'wf', 0, 4, cq * 512, 512, [t_wf[cq]], 'e_wf', ksplit=4)
            load_wb(wa[:, :, cq * 512:(cq + 1) * 512], 'wa', 0, 12, cq * 512, 512, [t_wa[cq]], 'e_wa', ksplit=12)
        ci = 0
        for tb in range(4):
            if tb + 1 < 4:
                e_load(tb + 1)
            b = tb % 2
            for dc in range(16):
                PA = pa[ci % 3]; tpa = t_pa[ci % 3]
                PF = pf[ci % 3]; tpf = t_pf[ci % 3]
                M1 = m1[ci % 2]; tm1 = t_m1[ci % 2]
                M2 = m2[ci % 2]; tm2 = t_m2[ci % 2]
                GA = gab[ci % 4]; tga = t_gab[ci % 4]
                GF = gfb[ci % 4]; tgf = t_gfb[ci % 4]
                dma('sp', GA[:], ga_scr[tb, :, dc, :], [d_ga[tb]], [tga], f'e_ga{ci % 4}')
                dma('sp', GF[:], gf_scr[tb, :, dc, :], [d_gf[tb]], [tgf], f'e_gf{ci % 4}')
                ci += 1
                for k in range(12):
                    mm(PA[:], wa[:, k, dc * 128:(dc + 1) * 128], onb[b][:, k, :], k == 0, k == 11, [t_wa[dc // 4], t_onb[b]], [tpa])
                for k in range(4):
                    mm(PF[:], wf[:, k, dc * 128:(dc + 1) * 128], yb[b][:, k, :], k == 0, k == 3, [t_wf[dc // 4], t_yb[b]], [tpf])
                tt('dve', M1[:], PA[:], GA[:], ALU.mult, [tpa, tga], [tm1])
                tt('dve', M2[:], PF[:], GF[:], ALU.mult, [tpf, tgf], [tm2])
                tt('dve', mst[b][:, dc, :], M1[:], M2[:], ALU.add, [tm1, tm2], [t_mst[b]])
            store_blk('sp', mT_scr, mst[b], tb, 16, [t_mst[b]], [d_mT[tb]], f'e_st{b}')
        end_phase()

    if not any(stop(x) for x in ['A', 'B', 'C', 'D', 'E1']):
        wo = sb("f_wo", [128, 16, D], BF16); t_wo = Ts(4)
        gb = sb("f_gb", [128, D], F32); t_g = T()
        idb = sb("f_id", [128, 128], BF16); t_id = T()
        mb = [sb(f"f_m{i}", [128, 16, 512], BF16) for i in range(2)]; t_mb = Ts(2)
        xts = [sb(f"f_x{i}", [128, D], F32) for i in range(2)]; t_xs = Ts(2)
        x2t = [sb(f"f_x2{i}", [128, D], F32) for i in range(3)]; t_x2 = Ts(3)
        junk = sb("f_junk", [128, D], F32); t_junk = T()
        hb = [sb(f"f_hb{i}", [128, D], BF16) for i in range(2)]; t_hb = Ts(2)
        ssq = [sb(f"f_ss{i}", [128, 4], F32) for i in range(2)]; t_ss = Ts(2)
        hTb = [sb(f"f_hT{i}", [128, 16, 512], BF16) for i in range(2)]; t_hT = Ts(2)
        po = [ps(f"f_po{i}", [128, 512]) for i in range(4)]; t_po = Ts(4)
        pT = [ps(f"f_pT{i}", [128, 16, 128], BF16) for i in range(2)]; t_pT = Ts(2)
        dma('sp', gb[:], bcast_rows(g_ffn, D), [], [t_g], 'f_g')
        idf = sb("f_idf", [128, 128], F32); t_idf = T()
        dma('sp', idf[:], ident, [], [t_idf], 'f_id')
        cp('dve', idb[:], idf[:], [t_idf], [t_id])

        def f_load_m(tb):
            load_blk('sp', mb[tb % 2], mT_scr, tb, 16, [d_mT[tb]], [t_mb[tb % 2]], f'f_m{tb % 2}')

        def f_load_x(i):
            dma('sp', xts[i % 2][:], xs[i * 128:(i + 1) * 128, :], [], [t_xs[i % 2]], f'f_x{i % 2}')

        def f_part2(i):
            tb = i // 4; sub = i % 4; b = i % 2
            cp('act', hTb[tb % 2][:, :, sub * 128:(sub + 1) * 128], pT[b][:], [t_pT[b]], [t_hT[tb % 2]])
            if sub == 3:
                store_blk('sp', h2T_scr, hTb[tb % 2], tb, 16, [t_hT[tb % 2]], [d_h2T[tb]], f'f_st{tb % 2}')

        def f_mm(i):
            tb = i // 4; sub = i % 4
            if sub == 0 and tb + 1 < 4:
                f_load_m(tb + 1)
            if i + 1 < 16:
                f_load_x(i + 1)
            M = mb[tb % 2]; tm = t_mb[tb % 2]
            for cb in range(4):
                for k in range(16):
                    mm(po[cb][:], M[:, k, sub * 128:(sub + 1) * 128], wo[:, k, cb * 512:(cb + 1) * 512], k == 0, k == 15, [tm, t_wo[cb]], [t_po[cb]])

        def f_post(i):
            tb = i // 4
            b = i % 2; b3 = i % 3
            for cb in range(4):
                tt('dve', x2t[b3][:, cb * 512:(cb + 1) * 512], po[cb][:], xts[b][:, cb * 512:(cb + 1) * 512], ALU.add, [t_po[cb], t_xs[b]], [t_x2[b3]])
            dma('sp', x2_scr[i * 128:(i + 1) * 128, :], x2t[b3][:], [t_x2[b3]], [d_x2[tb]], f'f_sx{b3}')
            norm_sq(x2t[b3][:], ssq[b], junk, t_x2[b3], t_ss[b], t_junk)

        f_load_m(0)
        f_load_x(0)
        for cb in range(4):
            load_wb(wo[:, :, cb * 512:(cb + 1) * 512], 'wo', 0, 16, cb * 512, 512, [t_wo[cb]], 'f_wo', ksplit=8)
        f_mm(0)
        f_post(0)
        for i in range(16):
            if i + 1 < 16:
                f_mm(i + 1)
            b = i % 2; b3 = i % 3
            norm_rest(x2t[b3][:], gb, idb, ssq[b], hb[b], pT[b], t_x2[b3], t_g, t_id, t_ss[b], t_hb[b], t_pT[b])
            if i + 1 < 16:
                f_post(i + 1)
            if i > 0:
                f_part2(i - 1)
        f_part2(15)
        end_phase()

    if not any(stop(x) for x in ['A', 'B', 'C', 'D', 'E1', 'E2']):
        wg = [sb(f"g_wg{i}", [128, 16, 512], BF16) for i in range(2)]; t_wg = Ts(2)
        wu = [sb(f"g_wu{i}", [128, 16, 512], BF16) for i in range(2)]; t_wu = Ts(2)
        hbk = [sb(f"g_h{i}", [128, 16, 512], BF16) for i in range(2)]; t_h = Ts(2)
        sl_ = [sb(f"g_s{i}", [128, 512], F32) for i in range(2)]; t_sl = Ts(2)
        ast = [sb(f"g_a{i}", [128, 4, 512], BF16) for i in range(2)]; t_ast = Ts(2)
        pg = [ps(f"g_pg{i}", [128, 512]) for i in range(3)]; t_pg = Ts(3)
        pu = [ps(f"g_pu{i}", [128, 512]) for i in range(3)]; t_pu = Ts(3)
        steps = [(sl, tb) for sl in range(11) for tb in range(4)]

        def g_load_w(sl):
            load_wb(wg[sl % 2], 'wg', 0, 16, sl * 512, 512, [t_wg[sl % 2]], f'g_wg{sl % 2}', ksplit=8)
            load_wb(wu[sl % 2], 'wu', 0, 16, sl * 512, 512, [t_wu[sl % 2]], f'g_wu{sl % 2}', ksplit=8)

        def g_load_h(si):
            sl, tb = steps[si]
            load_blk('sp', hbk[si % 2], h2T_scr, tb, 16, [d_h2T[tb]], [t_h[si % 2]], f'g_h{si % 2}')

        g_load_w(0)
        g_load_h(0)
        ci = 0
        for si, (sl, tb) in enumerate(steps):
            if si + 1 < len(steps):
                g_load_h(si + 1)
            if tb == 0 and sl + 1 < 11:
                g_load_w(sl + 1)
            if tb >= 2:
                S.wait_on('pool', S.ops['act'][-1])
                issue_conversions(2, which=1)
            H = hbk[si % 2]; th = t_h[si % 2]
            A = ast[si % 2]; ta = t_ast[si % 2]
            for c in range(4):
                PG = pg[ci % 3]; tpg = t_pg[ci % 3]
                PU = pu[ci % 3]; tpu = t_pu[ci % 3]
                SL = sl_[ci % 2]; tsl = t_sl[ci % 2]
                ci += 1
                for k in range(16):
                    mm(PG[:], wg[sl % 2][:, k, c * 128:(c + 1) * 128], H[:, k, :], k == 0, k == 15, [t_wg[sl % 2], th], [tpg])
                for k in range(16):
                    mm(PU[:], wu[sl % 2][:, k, c * 128:(c + 1) * 128], H[:, k, :], k == 0, k == 15, [t_wu[sl % 2], th], [tpu])
                act(SL[:], PG[:], AF.Silu, [tpg], [tsl])
                tt('dve', A[:, c, :], SL[:], PU[:], ALU.mult, [tsl, tpu], [ta])
            dma('sp', aT_scr[tb, :, sl * 4:(sl + 1) * 4, :], A[:], [ta], [d_aT[tb]], f'g_st{si % 2}')
        end_phase()

    if not any(stop(x) for x in ['A', 'B', 'C', 'D', 'E1', 'E2', 'F']):
        KH = 22
        wdh = sb("h_wd", [128, KH, D], BF16); t_wdh = Ts(4)
        ab_ = [sb(f"h_a{i}", [128, KH, 512], BF16) for i in range(2)]; t_ab = Ts(2)
        xin = [sb(f"h_xi{i}", [128, D], F32) for i in range(2)]; t_xin = Ts(2)
        xo = [sb(f"h_xo{i}", [128, D], F32) for i in range(2)]; t_xo = Ts(2)
        ot = [sb(f"h_ot{i}", [128, D], F32) for i in range(2)]; t_ot = Ts(2)
        gb = sb("h_gb", [128, D], F32); t_g = T()
        ssq = [sb(f"h_ss{i}", [128, 4], F32) for i in range(2)]; t_ss = Ts(2)
        pd = [[ps(f"h_pd{s_}{i}", [128, 512]) for i in range(4)] for s_ in range(2)]
        t_pd = [Ts(4) for _ in range(2)]
        dma('sp', gb[:], bcast_rows(g_final, D), [], [t_g], 'h_g')
        gsteps = [(p_, tt_) for p_ in range(2) for tt_ in range(16)]

        def h_load_w(p_):
            for cb in range(4):
                load_wb(wdh[:, :, cb * 512:(cb + 1) * 512], 'wd', p_ * KH * 128, KH, cb * 512, 512, [t_wdh[cb]], 'h_wd', ksplit=11)

        def h_load_a(p_, tb):
            j = (p_ * 4 + tb) % 2
            dma('sp', ab_[j][:, 0:11, :], aT_scr[tb, :, p_ * KH:p_ * KH + 11, :], [d_aT[tb]], [t_ab[j]], f'h_a{j}')
            dma('sp', ab_[j][:, 11:22, :], aT_scr[tb, :, p_ * KH + 11:(p_ + 1) * KH, :], [d_aT[tb]], [t_ab[j]], f'h_a{j}')

        def h_load_x(gi_):
            p_, tt_ = gsteps[gi_]
            src = x2_scr if p_ == 0 else x3_scr
            dep = d_x2[tt_ // 4] if p_ == 0 else d_x3[tt_ // 4]
            dma('sp', xin[gi_ % 2][:], src[tt_ * 128:(tt_ + 1) * 128, :], [dep], [t_xin[gi_ % 2]], f'h_xi{gi_ % 2}')

        def h_norm(gi_):
            p_, tt_ = gsteps[gi_]
            b_ = gi_ % 2
            ts('dve', ssq[b_][:, 1:2], ssq[b_][:, 0:1], 1.0 / D, EPS, ALU.mult, ALU.add, [t_ss[b_]], [t_ss[b_]])
            act(ssq[b_][:, 2:3], ssq[b_][:, 1:2], AF.Sqrt, [t_ss[b_]], [t_ss[b_]])
            recip(ssq[b_][:, 3:4], ssq[b_][:, 2:3], [t_ss[b_]], [t_ss[b_]])
            stt('dve', ot[b_][:], xo[b_][:], ssq[b_][:, 3:4], gb[:], ALU.mult, ALU.mult, [t_xo[b_], t_ss[b_], t_g], [t_ot[b_]])
            dma('sp', out[tt_ * 128:(tt_ + 1) * 128, :], ot[b_][:], [t_ot[b_]], [d_out[tt_ // 4]], f'h_so{b_}')

        h_load_a(0, 0)
        h_load_x(0)
        h_load_w(0)
        for gi_, (p_, tt_) in enumerate(gsteps):
            tb = tt_ // 4; sub = tt_ % 4
            if sub == 0:
                nxt = p_ * 4 + tb + 1
                if nxt < 8:
                    h_load_a(nxt // 4, nxt % 4)
            if gi_ + 1 < len(gsteps):
                h_load_x(gi_ + 1)
            A = ab_[(p_ * 4 + tb) % 2]; ta = t_ab[(p_ * 4 + tb) % 2]
            b_ = gi_ % 2
            P = pd[gi_ % 2]; tp = t_pd[gi_ % 2]
            for cb in range(4):
                for k in range(KH):
                    mm(P[cb][:], A[:, k, sub * 128:(sub + 1) * 128], wdh[:, k, cb * 512:(cb + 1) * 512], k == 0, k == KH - 1, [ta, t_wdh[cb]], [tp[cb]])
                if p_ == 0 and tt_ == 15:
                    load_wb(wdh[:, :, cb * 512:(cb + 1) * 512], 'wd', KH * 128, KH, cb * 512, 512, [t_wdh[cb]], 'h_wd', ksplit=11)
            for cb in range(4):
                tt('dve', xo[b_][:, cb * 512:(cb + 1) * 512], P[cb][:], xin[b_][:, cb * 512:(cb + 1) * 512], ALU.add, [tp[cb], t_xin[b_]], [t_xo[b_]])
            if p_ == 0:
                dma('sp', x3_scr[tt_ * 128:(tt_ + 1) * 128, :], xo[b_][:], [t_xo[b_]], [d_x3[tb]], f'h_sx{b_}')
            else:
                act(ot[b_][:], xo[b_][:], AF.Square, [t_xo[b_]], [t_ot[b_], t_ss[b_]], accum=ssq[b_][:, 0:1])
                if tt_ > 0:
                    h_norm(gi_ - 1)
        h_norm(len(gsteps) - 1)
        end_phase()

    alld = d_hT + d_qT + d_kT + d_v + d_f + d_ga + d_gf + d_oT + d_yT + d_mT + d_x2 + d_h2T + d_aT + d_x3 + d_out
    S.finish('sp', alld)
    es.close()
    S.emit()
    return nc


def host_tables(half):
    pos = np.concatenate([np.arange(half * T_OWN, (half + 1) * T_OWN), np.arange((1 - half) * T_OWN, (2 - half) * T_OWN)])
    inv_freq = (np.float32(ROPE_THETA) ** (-np.arange(0, 32, 2, dtype=np.float32) / np.float32(32))).astype(np.float32)
    ang = pos.astype(np.float32)[:, None] * inv_freq[None, :]
    c = np.cos(ang).astype(np.float32).T
    s = np.sin(ang).astype(np.float32).T
    rope_c = np.ascontiguousarray(np.concatenate([c, c], 0))
    rope_s = np.ascontiguousarray(np.concatenate([-s, s], 0))
    prod = (pos[:, None].astype(np.int64) * pos[None, :T_OWN].astype(np.int64)) % S_FULL
    a = (2.0 * np.pi / S_FULL) * prod.astype(np.float64)
    dft_c = np.cos(a).astype(np.float32).astype(ml_dtypes.bfloat16)
    dft_s = np.sin(a).astype(np.float32).astype(ml_dtypes.bfloat16)
    return rope_c, rope_s, dft_c, dft_s


def host_consts():
    perm = np.zeros((32, 32), np.float32)
    for m in range(32):
        perm[(m + 16) % 32, m] = 1.0
    ident = np.eye(128, dtype=np.float32)
    k = np.arange(128)
    a = (2.0 * np.pi / 128) * ((k[:, None] * k[None, :]) % 128).astype(np.float64)
    nrm = 1.0 / math.sqrt(S_FULL * 128.0)
    cd = (np.cos(a) * nrm).astype(np.float32)
    nsd = (-np.sin(a) * nrm).astype(np.float32)
    return perm, ident, cd, nsd


def make_in_maps(x, g_mix, w_in, lambda_q1, lambda_k1, lambda_q2, lambda_k2, g_subln,
                 w_attn_branch, w_four_branch, w_out, g_ffn, w_gate, w_up, w_down, g_final):
    f = lambda a: np.ascontiguousarray(np.asarray(a, dtype=np.float32))
    x = f(x)
    perm, ident, cd, nsd = host_consts()
    tabs = [host_tables(0), host_tables(1)]
    common = {
        "w_in": f(w_in), "w_attn": f(w_attn_branch), "w_four": f(w_four_branch), "w_out": f(w_out),
        "w_gate": f(w_gate), "w_up": f(w_up), "w_down": f(w_down),
        "g_mix": f(g_mix).reshape(1, D), "g_ffn": f(g_ffn).reshape(1, D), "g_final": f(g_final).reshape(1, D),
        "g_sub": np.ascontiguousarray(f(g_subln).reshape(2, 128).T),
        "lams": np.ascontiguousarray(np.stack([f(lambda_q1), f(lambda_k1), f(lambda_q2), f(lambda_k2)], 0)),
        "perm": perm, "ident": ident, "dft_cd": cd, "dft_nsd": nsd,
    }
    in_maps = []
    for c in range(8):
        b, half = c // 2, c % 2
        xs = np.ascontiguousarray(np.concatenate(
            [x[b, half * T_OWN:(half + 1) * T_OWN], x[b, (1 - half) * T_OWN:(2 - half) * T_OWN]], 0))
        rope_c, rope_s, dft_c, dft_s = tabs[half]
        m = dict(common)
        m.update({"xs": xs, "rope_c": rope_c, "rope_s": rope_s, "dft_c": dft_c, "dft_s": dft_s})
        in_maps.append(m)
    return in_maps


_NC_CACHE = {}


def kernel(**inputs):
    in_maps = make_in_maps(**inputs)
    if 'nc' not in _NC_CACHE:
        _NC_CACHE['nc'] = build_nc()
    nc = _NC_CACHE['nc']
    res = run_bass_kernel_spmd(nc, in_maps, core_ids=list(range(8)))
    outp = np.empty((4, S_FULL, D), np.float32)
    for c in range(8):
        b, half = c // 2, c % 2
        outp[b, half * T_OWN:(half + 1) * T_OWN] = res.results[c]["out"]
    return outp
```

```python
import contextlib
import math
import numpy as np
import ml_dtypes
import concourse.bass as bass
import concourse.mybir as mybir
from concourse.bass_utils import run_bass_kernel_spmd

F32 = mybir.dt.float32
BF16 = mybir.dt.bfloat16
AF = mybir.ActivationFunctionType
ALU = mybir.AluOpType
AX = mybir.AxisListType

D = 2048
S_FULL = 4096
T_OWN = 2048
NH = 6
QK_W = 1536
V_W = 1536
F_W = 512
IN_W = 9216
DFF = 5632
LAMBDA_INIT = 0.8 - 0.6 * math.exp(-0.3 * 0)
EPS = 1e-5
ROPE_THETA = 500000.0
ENGS = ['pe', 'act', 'dve', 'pool', 'sp']


class Tile:
    __slots__ = ('name', 'last_w', 'readers')

    def __init__(self, name):
        self.name = name
        self.last_w = None
        self.readers = []


class Op:
    __slots__ = ('eng', 'idx', 'fn', 'waits', 'signal', 'count', 'is_dma', 'sem')


class Sched:
    def __init__(self, nc):
        self.nc = nc
        self.ops = {e: [] for e in ENGS}
        self.waited = {e: {} for e in ENGS}
        self.dma_cnt = {}

    def tile(self, name="t"):
        return Tile(name)

    def tiles(self, n, name="t"):
        return [Tile(f"{name}{i}") for i in range(n)]

    def _add_wait(self, op, d, kind, force=False):
        w = self.waited[op.eng]
        if d.is_dma:
            key = ('dma', d.sem)
            cnt = self.dma_cnt[d.sem]
            if op.is_dma and op.sem == d.sem:
                cnt -= 1
            val = cnt * 16
            if w.get(key, 0) >= val:
                return
            w[key] = val
            op.waits.append((key, val, None))
        else:
            if d.eng == op.eng and not op.is_dma and kind != 'raw' and not force:
                return
            key = ('eng', d.eng)
            if w.get(key, -1) >= d.idx:
                return
            w[key] = d.idx
            d.signal = True
            op.waits.append((key, None, d))

    def _deps(self, op, reads, writes):
        deps = []
        for t in reads:
            if t.last_w is not None:
                deps.append((t.last_w, 'raw'))
        for t in writes:
            if t.last_w is not None:
                deps.append((t.last_w, 'waw'))
            for r in t.readers:
                deps.append((r, 'war'))
        for t in reads:
            t.readers.append(op)
        for t in writes:
            t.last_w = op
            t.readers = []
        for d, kind in deps:
            if d is not op:
                self._add_wait(op, d, kind)

    def _new(self, eng, fn, is_dma, sem):
        o = Op()
        o.eng = eng; o.fn = fn; o.waits = []; o.signal = False; o.is_dma = is_dma; o.sem = sem
        o.count = 0
        o.idx = len(self.ops[eng])
        self.ops[eng].append(o)
        return o

    def op(self, eng, fn, reads=(), writes=()):
        o = self._new(eng, fn, False, None)
        self._deps(o, reads, writes)
        return o

    def dma(self, eng, fn, reads=(), writes=(), sem=None):
        self.dma_cnt[sem] = self.dma_cnt.get(sem, 0) + 1
        o = self._new(eng, fn, True, sem)
        self._deps(o, reads, writes)
        return o

    def barrier(self, tiles):
        deps = []
        for t in tiles:
            if t.last_w is not None:
                deps.append(t.last_w)
            deps.extend(t.readers)
        deps.sort(key=lambda d: -d.idx)
        for e in ENGS:
            o = self._new(e, None, False, None)
            for d in deps:
                self._add_wait(o, d, 'raw', force=True)

    def wait_on(self, eng, target):
        o = self._new(eng, None, False, None)
        self._add_wait(o, target, 'raw', force=True)

    def finish(self, eng, tiles):
        self.op(eng, None, reads=tiles)

    def emit(self):
        nc = self.nc
        for e in ENGS:
            c = 0
            for o in self.ops[e]:
                if o.signal and not o.is_dma:
                    c += 1
                    o.count = c
        with contextlib.ExitStack() as st:
            esem = {e: st.enter_context(nc.semaphore(f"s_{e}")) for e in ENGS}
            dsem = {k: st.enter_context(nc.semaphore(f"d_{k}")) for k in self.dma_cnt}
            block = st.enter_context(nc.Block())

            def run(e, eng):
                for o in self.ops[e]:
                    for key, val, d in o.waits:
                        if key[0] == 'dma':
                            eng.wait_ge(dsem[key[1]], val)
                        else:
                            eng.wait_ge(esem[key[1]], d.count)
                    if o.fn is None:
                        continue
                    ins = o.fn(eng)
                    if o.is_dma:
                        ins.then_inc(dsem[o.sem], 16)
                    elif o.signal:
                        ins.then_inc(esem[e], 1)

            @block.tensor
            def _(eng):
                run('pe', eng)

            @block.scalar
            def _(eng):
                run('act', eng)

            @block.vector
            def _(eng):
                run('dve', eng)

            @block.gpsimd
            def _(eng):
                run('pool', eng)

            @block.sync
            def _(eng):
                run('sp', eng)


def build_nc(dbg=False, stop_after=None):
    nc = bass.Bass("TRN2", target_bir_lowering=False)
    S = Sched(nc)

    def din(name, shape):
        return nc.dram_tensor(name, shape, F32, kind="ExternalInput").ap()

    def dscr(name, shape, dt):
        kind = "ExternalOutput" if (dbg and name in dbg) else "Internal"
        return nc.dram_tensor(name, shape, dt, kind=kind).ap()

    xs = din("xs", [S_FULL, D])
    w_in = din("w_in", [D, IN_W])
    w_attn = din("w_attn", [V_W, D])
    w_four = din("w_four", [F_W, D])
    w_out = din("w_out", [D, D])
    w_gate = din("w_gate", [D, DFF])
    w_up = din("w_up", [D, DFF])
    w_down = din("w_down", [DFF, D])
    g_mix = din("g_mix", [1, D])
    g_ffn = din("g_ffn", [1, D])
    g_final = din("g_final", [1, D])
    g_sub = din("g_sub", [128, 2])
    lams = din("lams", [4, 128])
    rope_c = din("rope_c", [32, S_FULL])
    rope_s = din("rope_s", [32, S_FULL])
    perm = din("perm", [32, 32])
    ident = din("ident", [128, 128])
    dft_c = nc.dram_tensor("dft_c", [S_FULL, T_OWN], BF16, kind="ExternalInput").ap()
    dft_s = nc.dram_tensor("dft_s", [S_FULL, T_OWN], BF16, kind="ExternalInput").ap()
    dft_cd = din("dft_cd", [128, 128])
    dft_nsd = din("dft_nsd", [128, 128])
    out = nc.dram_tensor("out", [T_OWN, D], F32, kind="ExternalOutput").ap()

    hT_scr = dscr("hT_scr", [8, 128, 16, 512], BF16)
    qT_scr = dscr("qT_scr", [12, 128, T_OWN], BF16)
    kT_scr = dscr("kT_scr", [12, 128, S_FULL], BF16)
    v_scr = dscr("v_scr", [S_FULL, V_W], BF16)
    f_scr = dscr("f_scr", [S_FULL, F_W], BF16)
    ga_scr = dscr("ga_scr", [4, 128, 16, 512], BF16)
    gf_scr = dscr("gf_scr", [4, 128, 16, 512], BF16)
    oT_scr = dscr("oT_scr", [4, 128, 12, 512], BF16)
    yT_scr = dscr("yT_scr", [4, 128, 4, 512], BF16)
    mT_scr = dscr("mT_scr", [4, 128, 16, 512], BF16)
    x2_scr = dscr("x2_scr", [T_OWN, D], F32)
    h2T_scr = dscr("h2T_scr", [4, 128, 16, 512], BF16)
    aT_scr = dscr("aT_scr", [4, 128, 44, 512], BF16)
    x3_scr = dscr("x3_scr", [T_OWN, D], F32)

    cv_src = {'wa': w_attn, 'wf': w_four, 'wo': w_out, 'wg': w_gate, 'wu': w_up, 'wd': w_down}
    cv_dst = {k: dscr("cvb_" + k, list(v.shape), BF16) for k, v in cv_src.items()}
    cv_tiles = {k: S.tiles(v.shape[0] // 128, "cv_" + k) for k, v in cv_src.items()}

    cv_sem = {'wa': 'cv_e', 'wf': 'cv_e', 'wo': 'cv_e', 'wg': 'cv_f', 'wu': 'cv_f', 'wd': 'cv_g'}

    cv_list = [(k, r) for k in ['wa', 'wf', 'wo', 'wg', 'wu'] for r in range(cv_src[k].shape[0] // 128)]
    cv_list2 = [('wd', r) for r in range(cv_src['wd'].shape[0] // 128)]
    cv_pos = [0, 0]

    def issue_conversions(n, which=0):
        lst = cv_list if which == 0 else cv_list2
        for k, r in lst[cv_pos[which]:cv_pos[which] + n]:
            dma('pool', cv_dst[k][r * 128:(r + 1) * 128, :], cv_src[k][r * 128:(r + 1) * 128, :], [], [cv_tiles[k][r]], cv_sem[k])
        cv_pos[which] += n

    def load_wb(tile3, key, r0, k_tot, c0, cn, writes, sem, ksplit=4):
        wb = cv_dst[key]
        for k0 in range(0, k_tot, ksplit):
            k1 = min(k_tot, k0 + ksplit)
            dma('sp', tile3[:, k0:k1, :],
                wb[r0 + k0 * 128:r0 + k1 * 128, c0:c0 + cn].rearrange("(k p) n -> p k n", p=128),
                cv_tiles[key][(r0 // 128) + k0:(r0 // 128) + k1], writes, sem)

    def dtiles(n):
        return S.tiles(n, "dr")

    d_hT = dtiles(8); d_qT = dtiles(4); d_kT = dtiles(8); d_v = dtiles(8); d_f = dtiles(8)
    d_ga = dtiles(4); d_gf = dtiles(4); d_oT = dtiles(4); d_yT = dtiles(4); d_mT = dtiles(4)
    d_x2 = dtiles(4); d_h2T = dtiles(4); d_aT = dtiles(4); d_x3 = dtiles(4); d_out = dtiles(4)

    def mm(o, l, r, st, sp, reads, writes):
        S.op('pe', lambda e: e.matmul(o, lhsT=l, rhs=r, start=st, stop=sp), reads, writes)

    def tr(o, i, idn, reads, writes):
        S.op('pe', lambda e: e.transpose(out=o, in_=i, identity=idn), reads, writes)

    def act(o, i, func, reads, writes, scale=1.0, bias=None, accum=None):
        kw = {}
        if bias is not None:
            kw['bias'] = bias
        if accum is not None:
            kw['accum_out'] = accum
        S.op('act', lambda e: e.activation(out=o, in_=i, func=func, scale=scale, **kw), reads, writes)

    def tt(eng, o, a, b, op, reads, writes):
        S.op(eng, lambda e: e.tensor_tensor(out=o, in0=a, in1=b, op=op), reads, writes)

    def ts(eng, o, a, s1, s2, op0, op1, reads, writes):
        if s2 is None:
            S.op(eng, lambda e: e.tensor_scalar(out=o, in0=a, scalar1=s1, scalar2=None, op0=op0), reads, writes)
        else:
            S.op(eng, lambda e: e.tensor_scalar(out=o, in0=a, scalar1=s1, scalar2=s2, op0=op0, op1=op1), reads, writes)

    def stt(eng, o, a, s, b, op0, op1, reads, writes):
        S.op(eng, lambda e: e.scalar_tensor_tensor(out=o, in0=a, scalar=s, in1=b, op0=op0, op1=op1), reads, writes)

    def cp(eng, o, i, reads, writes):
        if eng == 'act':
            S.op('act', lambda e: e.copy(out=o, in_=i), reads, writes)
        else:
            S.op(eng, lambda e: e.tensor_copy(out=o, in_=i), reads, writes)

    def recip(o, i, reads, writes):
        S.op('dve', lambda e: e.reciprocal(out=o, in_=i), reads, writes)

    def dma(q, o, i, reads, writes, sem):
        S.dma(q, lambda e: e.dma_start(out=o, in_=i), reads, writes, sem)

    def bcast_rows(ap_row, n):
        return bass.AP(ap_row.tensor, ap_row.offset, [[0, 128], [1, n]])

    def load_kblock(q, tile3, src3, k_tot, t0, tn, reads, writes, sem, ksplit=4):
        for k0 in range(0, k_tot, ksplit):
            k1 = min(k_tot, k0 + ksplit)
            dma(q, tile3[:, k0:k1, :], src3[k0:k1, :, t0:t0 + tn].rearrange("k p t -> p k t"), reads, writes, sem)

    def store_kblock(q, dst3, tile3, k_tot, t0, tn, reads, writes, sem, ksplit=4):
        for k0 in range(0, k_tot, ksplit):
            k1 = min(k_tot, k0 + ksplit)
            dma(q, dst3[k0:k1, :, t0:t0 + tn].rearrange("k p t -> p k t"), tile3[:, k0:k1, :], reads, writes, sem)

    def load_blk(q, tile3, scr4, blk, k_tot, reads, writes, sem, ksplit=8):
        for k0 in range(0, k_tot, ksplit):
            k1 = min(k_tot, k0 + ksplit)
            dma(q, tile3[:, k0:k1, :], scr4[blk, :, k0:k1, :], reads, writes, sem)

    def store_blk(q, scr4, tile3, blk, k_tot, reads, writes, sem, ksplit=8):
        for k0 in range(0, k_tot, ksplit):
            k1 = min(k_tot, k0 + ksplit)
            dma(q, scr4[blk, :, k0:k1, :], tile3[:, k0:k1, :], reads, writes, sem)

    def load_w(tile3, w2, r0, k_tot, c0, cn, writes, sem, ksplit=4):
        for k0 in range(0, k_tot, ksplit):
            k1 = min(k_tot, k0 + ksplit)
            dma('pool', tile3[:, k0:k1, :],
                w2[r0 + k0 * 128:r0 + k1 * 128, c0:c0 + cn].rearrange("(k p) n -> p k n", p=128), [], writes, sem)

    es = contextlib.ExitStack()

    def sb(name, shape, dt):
        return es.enter_context(nc.sbuf_tensor(name, shape, dt))

    def ps(name, shape, dt=F32):
        return es.enter_context(nc.psum_tensor(name, shape, dt))

    phase_tiles = []

    def T(name="t"):
        t = S.tile(name)
        phase_tiles.append(t)
        return t

    def Ts(n, name="t"):
        return [T(name) for _ in range(n)]

    def end_phase():
        nonlocal es
        S.barrier(phase_tiles)
        phase_tiles.clear()
        es.close()
        es = contextlib.ExitStack()

    done = [False]

    def stop(name):
        if stop_after == name:
            done[0] = True
        return done[0]

    def norm_sq(xt_ap, ssq, junk, t_x, t_ss, t_junk):
        act(junk[:], xt_ap, AF.Square, [t_x], [t_junk, t_ss], accum=ssq[:, 0:1])

    def norm_rest(xt_ap, gb, idb, ssq, hb, pT, t_x, t_g, t_id, t_ss, t_hb, t_pT):
        ts('dve', ssq[:, 1:2], ssq[:, 0:1], 1.0 / D, EPS, ALU.mult, ALU.add, [t_ss], [t_ss])
        act(ssq[:, 2:3], ssq[:, 1:2], AF.Sqrt, [t_ss], [t_ss])
        recip(ssq[:, 3:4], ssq[:, 2:3], [t_ss], [t_ss])
        stt('dve', hb[:], xt_ap, ssq[:, 3:4], gb[:], ALU.mult, ALU.mult, [t_x, t_ss, t_g], [t_hb])
        for k in range(16):
            tr(pT[:, k, :], hb[:, k * 128:(k + 1) * 128], idb[:], [t_hb, t_id], [t_pT])

    def norm_rest_a(ssq, t_ss):
        ts('dve', ssq[:, 1:2], ssq[:, 0:1], 1.0 / D, EPS, ALU.mult, ALU.add, [t_ss], [t_ss])
        act(ssq[:, 2:3], ssq[:, 1:2], AF.Sqrt, [t_ss], [t_ss])

    def norm_rest_b(xt_ap, gb, idb, ssq, hb, pT, t_x, t_g, t_id, t_ss, t_hb, t_pT):
        recip(ssq[:, 3:4], ssq[:, 2:3], [t_ss], [t_ss])
        stt('dve', hb[:], xt_ap, ssq[:, 3:4], gb[:], ALU.mult, ALU.mult, [t_x, t_ss, t_g], [t_hb])
        for k in range(16):
            tr(pT[:, k, :], hb[:, k * 128:(k + 1) * 128], idb[:], [t_hb, t_id], [t_pT])

    es_ab = contextlib.ExitStack()
    hT_own = es_ab.enter_context(nc.sbuf_tensor("hT_own", [128, 16, T_OWN], BF16))
    t_hown = S.tiles(4, "hown")
    if True:
        gb = sb("a_gb", [128, D], F32); t_g = T()
        idb = sb("a_id", [128, 128], BF16); t_id = T()
        xts = [sb(f"a_x{i}", [128, D], F32) for i in range(3)]; t_xs = Ts(3)
        junk = sb("a_junk", [128, D], F32); t_junk = T()
        hb = [sb(f"a_hb{i}", [128, D], BF16) for i in range(2)]; t_hb = Ts(2)
        ssq = [sb(f"a_ss{i}", [128, 4], F32) for i in range(2)]; t_ss = Ts(2)
        hTb = [sb(f"a_hT{i}", [128, 16, 512], BF16) for i in range(2)]; t_hT = Ts(2)
        pT = [ps(f"a_pT{i}", [128, 16, 128], BF16) for i in range(2)]; t_pT = Ts(2)
        dma('sp', gb[:], bcast_rows(g_mix, D), [], [t_g], 'a_g')
        dma('pool', idb[:], ident, [], [t_id], 'a_id')
        NT = S_FULL // 128

        def a_load(i):
            dma('sp', xts[i % 3][:], xs[i * 128:(i + 1) * 128, :], [], [t_xs[i % 3]], f'a_x{i % 3}')

        def a_part2(i):
            b = i % 2; blk = i // 4; sub = i % 4
            eng = 'act' if i % 2 else 'dve'
            if blk < 4:
                cp(eng, hT_own[:, :, i * 128:(i + 1) * 128], pT[b][:], [t_pT[b]], [t_hown[blk]])
            else:
                cp(eng, hTb[blk % 2][:, :, sub * 128:(sub + 1) * 128], pT[b][:], [t_pT[b]], [t_hT[blk % 2]])
                if sub == 3:
                    store_blk('sp', hT_scr, hTb[blk % 2], blk, 16, [t_hT[blk % 2]], [d_hT[blk]], f'a_st{blk % 2}')

        a_load(0)
        a_load(1)
        norm_sq(xts[0][:], ssq[0], junk, t_xs[0], t_ss[0], t_junk)
        for i in range(NT):
            if i + 2 < NT:
                a_load(i + 2)
            b = i % 2
            norm_rest_a(ssq[b], t_ss[b])
            if i + 1 < NT:
                j = i + 1
                norm_sq(xts[j % 3][:], ssq[j % 2], junk, t_xs[j % 3], t_ss[j % 2], t_junk)
            norm_rest_b(xts[i % 3][:], gb, idb, ssq[b], hb[b], pT[b], t_xs[i % 3], t_g, t_id, t_ss[b], t_hb[b], t_pT[b])
            if i > 0:
                a_part2(i - 1)
        a_part2(NT - 1)
        end_phase()

    if not stop('A'):
        wsl = [sb(f"b_w{i}", [128, 16, 512], BF16) for i in range(2)]; t_w = Ts(2)
        hbk = [sb(f"b_h{i}", [128, 16, 512], BF16) for i in range(2)]; t_h = Ts(2)
        stg = [sb(f"b_s{i}", [128, 4, 512], BF16) for i in range(2)]; t_stg = [Ts(4) for _ in range(2)]
        rc = sb("b_rc", [32, S_FULL], F32); rs = sb("b_rs", [32, S_FULL], F32); t_rope = T()
        pmb = sb("b_pm", [32, 32], BF16); t_pm = T()
        r1 = [sb(f"b_r1{i}", [32, 512], F32) for i in range(2)]; t_r1 = Ts(2)
        r2 = [sb(f"b_r2{i}", [32, 512], F32) for i in range(2)]; t_r2 = Ts(2)
        pz = [ps(f"b_pz{i}", [128, 512]) for i in range(6)]; t_pz = Ts(6)
        psw = [ps(f"b_psw{i}", [32, 512]) for i in range(2)]; t_psw = Ts(2)
        dma('sp', rc[:], rope_c, [], [t_rope], 'b_rope')
        dma('sp', rs[:], rope_s, [], [t_rope], 'b_rope')
        dma('pool', pmb[:], perm, [], [t_pm], 'b_pm')

        kinds = ['q'] * 3 + ['k'] * 3 + ['v'] * 3 + ['f'] + ['ga'] * 4 + ['gf'] * 4
        steps = []
        for sl in range(18):
            nblk = 8 if kinds[sl] in ('k', 'v', 'f') else 4
            for tb in range(nblk):
                steps.append((sl, tb))
        lsteps = [si for si, (sl, tb) in enumerate(steps) if tb >= 4]
        lslot = {si: n % 2 for n, si in enumerate(lsteps)}
        l_issued = 0
        l_done = 0

        def b_load_w(sl):
            load_w(wsl[sl % 2], w_in, 0, 16, sl * 512, 512, [t_w[sl % 2]], f'b_w{sl % 2}')

        def b_issue_loads():
            nonlocal l_issued
            while l_issued < len(lsteps) and l_issued < l_done + 2:
                si_ = lsteps[l_issued]
                j = lslot[si_]
                tb_ = steps[si_][1]
                load_blk('sp', hbk[j], hT_scr, tb_, 16, [d_hT[tb_]], [t_h[j]], f'b_h{j}')
                l_issued += 1

        b_load_w(0)
        pzi = 0
        swi = 0
        for si, (sl, tb) in enumerate(steps):
            if tb == 0 and sl + 1 < 18:
                b_load_w(sl + 1)
            b_issue_loads()
            kind = kinds[sl]
            W = wsl[sl % 2]; tw = t_w[sl % 2]
            if tb < 4:
                th = t_hown[tb]

                def Hs(k, a_, b_, tb=tb):
                    return hT_own[:, k, tb * 512 + a_:tb * 512 + b_]
            else:
                th = t_h[lslot[si]]

                def Hs(k, a_, b_, j=lslot[si]):
                    return hbk[j][:, k, a_:b_]
            sg = stg[si % 2]; tsg = t_stg[si % 2]
            pending = None
            for c in range(4):
                P = pz[pzi % 6]; tp = t_pz[pzi % 6]; pzi += 1
                if kind in ('v', 'f'):
                    for k in range(16):
                        mm(P[:], Hs(k, c * 128, (c + 1) * 128), W[:, k, :], k == 0, k == 15, [th, tw], [tp])
                    cp('act' if c % 2 else 'dve', sg[:, c, :], P[:], [tp], [tsg[c]])
                else:
                    for k in range(16):
                        mm(P[:], W[:, k, c * 128:(c + 1) * 128], Hs(k, 0, 512), k == 0, k == 15, [th, tw], [tp])
                    if pending is not None:
                        pending(); pending = None
                    if kind in ('ga', 'gf'):
                        act(sg[:, c, :], P[:], AF.Sigmoid, [tp], [tsg[c]])
                    else:
                        cp('act', sg[:, c, :], P[:], [tp], [tsg[c]])

                        def rope(c=c, P=P, tp=tp):
                            nonlocal swi
                            SW = psw[swi % 2]; tsw = t_psw[swi % 2]
                            R1 = r1[swi % 2]; tr1 = t_r1[swi % 2]
                            R2 = r2[swi % 2]; tr2 = t_r2[swi % 2]
                            swi += 1
                            mm(SW[:], pmb[:], sg[0:32, c, :], True, True, [tsg[c], t_pm], [tsw])
                            tt('dve', R1[:], SW[:], rs[:, tb * 512:(tb + 1) * 512], ALU.mult, [tsw, t_rope], [tr1])
                            tt('dve', R2[:], P[0:32, :], rc[:, tb * 512:(tb + 1) * 512], ALU.mult, [tp, t_rope], [tr2])
                            tt('dve', sg[0:32, c, :], R1[:], R2[:], ALU.add, [tr1, tr2], [tsg[c]])
                        pending = rope
            if pending is not None:
                pending(); pending = None
            if tb >= 4:
                l_done += 1
            tsg = list(tsg)
            if kind == 'q':
                dma('sp', qT_scr[sl * 4:(sl + 1) * 4, :, tb * 512:(tb + 1) * 512].rearrange("k p t -> p k t"), sg[:], tsg, [d_qT[tb]], f'b_st{si % 2}')
            elif kind == 'k':
                c0 = (sl - 3) * 4
                dma('sp', kT_scr[c0:c0 + 4, :, tb * 512:(tb + 1) * 512].rearrange("k p t -> p k t"), sg[:], tsg, [d_kT[tb]], f'b_st{si % 2}')
            elif kind == 'v':
                c0 = (sl - 6) * 512
                dma('sp', v_scr[tb * 512:(tb + 1) * 512, c0:c0 + 512].rearrange("(c p) n -> p c n", p=128), sg[:], tsg, [d_v[tb]], f'b_st{si % 2}')
            elif kind == 'f':
                dma('sp', f_scr[tb * 512:(tb + 1) * 512, :].rearrange("(c p) n -> p c n", p=128), sg[:], tsg, [d_f[tb]], f'b_st{si % 2}')
            elif kind == 'ga':
                c0 = (sl - 10) * 4
                dma('sp', ga_scr[tb, :, c0:c0 + 4, :], sg[:], tsg, [d_ga[tb]], f'b_st{si % 2}')
            else:
                c0 = (sl - 14) * 4
                dma('sp', gf_scr[tb, :, c0:c0 + 4, :], sg[:], tsg, [d_gf[tb]], f'b_st{si % 2}')
        phase_tiles.extend(t_hown)
        end_phase()
    es_ab.close()

    if not stop('A') and not stop('B'):
        kT = [sb(f"c_k{i}", [128, 2, S_FULL], BF16) for i in range(2)]; t_k = Ts(2)
        vh = [sb(f"c_v{i}", [128, 32, 256], BF16) for i in range(2)]; t_v = Ts(2)
        qh = [sb(f"c_q{i}", [128, 2, 512], BF16) for i in range(2)]; t_q = Ts(2)
        ones = sb("c_ones", [128, 128], BF16); t_ones = T()
        pT4 = sb("c_p4", [128, 8, 512], BF16); t_p = Ts(8)
        gs2 = [sb(f"c_gs2{i}", [128, 2, 512], BF16) for i in range(2)]; t_gs2 = Ts(2)
        gs1 = [sb(f"c_gs1{i}", [128, 512], BF16) for i in range(2)]; t_gs1 = Ts(2)
        lam_in = sb("c_lam", [128, 4, 128], F32); t_lam = T()
        lam_w = sb("c_lamw", [128, 2, 128], F32)
        lam_s = sb("c_lams", [128, 8], F32)
        gsb = sb("c_gs", [128, 2], F32); gss = sb("c_gss", [128, 2], F32); t_gs = T()
        epsb = sb("c_eps", [128, 1], F32)
        rr = sb("c_rr", [128, 512], F32); t_rr = T()
        oa = sb("c_oa", [128, 2, 512], F32); t_oa = T()
        ob = sb("c_ob", [128, 2, 512], F32); t_ob = T()
        oo = sb("c_oo", [128, 2, 512], F32); t_oo = T()
        sq = sb("c_sq", [128, 2, 512], BF16); t_sq = T()
        rstd = sb("c_rstd", [128, 512], F32); t_rstd = T()
        ssb = sb("c_ssb", [128, 512], F32); t_ssb = T()
        on = [sb(f"c_on{i}", [128, 2, 512], BF16) for i in range(2)]; t_on = Ts(2)
        psc = [ps(f"c_sc{i}", [128, 512]) for i in range(3)]; t_sc = Ts(3)
        pacc = [[ps(f"c_acc{s}{j}", [128, 512]) for j in range(2)] for s in range(2)]
        t_acc = [Ts(2) for _ in range(2)]
        psm = ps("c_psm", [128, 512]); t_psm = T()

        S.op('dve', lambda e: e.memset(ones[:], 1.0), [], [t_ones])
        S.op('dve', lambda e: e.memset(epsb[:], EPS), [], [t_gs])
        dma('sp', lam_in[:].rearrange("p a b -> p (a b)"), bass.AP(lams.tensor, 0, [[0, 128], [1, 512]]), [], [t_lam], 'c_lam')
        dma('sp', gsb[:], g_sub, [], [t_gs], 'c_gs')
        tt('dve', lam_w[:, 0, :], lam_in[:, 0, :], lam_in[:, 1, :], ALU.mult, [t_lam], [t_lam])
        tt('dve', lam_w[:, 1, :], lam_in[:, 2, :], lam_in[:, 3, :], ALU.mult, [t_lam], [t_lam])
        S.op('dve', lambda e: e.reduce_sum(out=lam_s[:, 0:1], in_=lam_w[:, 0, :], axis=AX.X), [t_lam], [t_lam])
        S.op('dve', lambda e: e.reduce_sum(out=lam_s[:, 1:2], in_=lam_w[:, 1, :], axis=AX.X), [t_lam], [t_lam])
        act(lam_s[:, 2:4], lam_s[:, 0:2], AF.Exp, [t_lam], [t_lam])
        tt('dve', lam_s[:, 4:5], lam_s[:, 3:4], lam_s[:, 2:3], ALU.subtract, [t_lam], [t_lam])
        ts('dve', lam_s[:, 5:6], lam_s[:, 4:5], -LAMBDA_INIT, None, ALU.add, None, [t_lam], [t_lam])
        ts('dve', gss[:], gsb[:], 1.0 - LAMBDA_INIT, None, ALU.mult, None, [t_gs], [t_gs])
        neg_lam = lam_s[:, 5:6]
        scale = 1.0 / math.sqrt(128.0)

        csteps = [(h, qb) for h in range(NH) for qb in range(4)]

        def c_load_kv(h):
            b = h % 2
            for c in range(2):
                for t0 in range(0, S_FULL, 1024):
                    dma('sp', kT[b][:, c, t0:t0 + 1024], kT_scr[2 * h + c, :, t0:t0 + 1024], d_kT, [t_k[b]], f'c_k{b}')
            for c0 in range(0, 32, 8):
                dma('sp', vh[b][:, c0:c0 + 8, :],
                    v_scr[c0 * 128:(c0 + 8) * 128, h * 256:(h + 1) * 256].rearrange("(c p) e -> p c e", p=128),
                    d_v, [t_v[b]], f'c_v{b}')

        def c_load_q(si):
            h, qb = csteps[si]
            dma('sp', qh[si % 2][:], qT_scr[2 * h:2 * h + 2, :, qb * 512:(qb + 1) * 512].rearrange("k p t -> p k t"),
                [d_qT[qb]], [t_q[si % 2]], f'c_q{si % 2}')

        c_load_kv(0)
        c_load_q(0)
        items = [(si, c, kc) for si in range(len(csteps)) for c in range(2) for kc in range(32)]

        def rec_S(i):
            si, c, kc = items[i]
            h, qb = csteps[si]
            mm(psc[i % 3][:], kT[h % 2][:, c, kc * 128:(kc + 1) * 128], qh[si % 2][:, c, :], True, True,
               [t_k[h % 2], t_q[si % 2]], [t_sc[i % 3]])

        def epi1a(si):
            for j in range(2):
                stt('dve', oo[:, j, :], ob[:, j, :], neg_lam, oa[:, j, :], ALU.mult, ALU.add, [t_ob, t_oa, t_lam], [t_oo])

        def epi1b(si):
            act(sq[:].rearrange("p a b -> p (a b)"), oo[:].rearrange("p a b -> p (a b)"), AF.Square, [t_oo], [t_sq])

        def epi2a(si, pb, tpb):
            for j in range(2):
                mm(pb[:], ones[:], sq[:, j, :], j == 0, j == 1, [t_ones, t_sq], [tpb])
            cp('dve', ssb[:], pb[:], [tpb], [t_ssb])

        def epi2b(si):
            act(rstd[:], ssb[:], AF.Ln, [t_ssb, t_gs], [t_rstd], scale=1.0 / 256.0, bias=epsb[:, 0:1])
            act(rstd[:], rstd[:], AF.Exp, [t_rstd], [t_rstd], scale=-0.5)

        def epi2c(si):
            h, qb = csteps[si]
            ON = on[si % 2]; ton = t_on[si % 2]
            for j in range(2):
                stt('dve', ON[:, j, :], oo[:, j, :], gss[:, j:j + 1], rstd[:], ALU.mult, ALU.mult, [t_oo, t_gs, t_rstd], [ton])
            dma('sp', oT_scr[qb, :, 2 * h:2 * h + 2, :], ON[:], [ton], [d_oT[qb]], f'c_st{si % 2}')

        def evac_a(si, c):
            act(rr[:], psm[:], AF.Ln, [t_psm], [t_rr])
            act(rr[:], rr[:], AF.Exp, [t_rr], [t_rr], scale=-1.0)

        def evac_b(si, c):
            dst = oa if c == 0 else ob
            tdst = t_oa if c == 0 else t_ob
            for j in range(2):
                tt('dve', dst[:, j, :], pacc[c][j][:], rr[:], ALU.mult, [t_acc[c][j], t_rr], [tdst])

        rec_S(0)
        rec_S(1)
        n_items = len(items)
        pending_evac = None
        pending_ones = None
        for i, (si, c, kc) in enumerate(items):
            h, qb = csteps[si]
            if i + 2 < n_items:
                rec_S(i + 2)
            slot = i % 8
            PT = pT4[:, slot, :]; tpp = t_p[slot]
            V = vh[h % 2]; tv = t_v[h % 2]
            ACC = pacc[c]; tacc = t_acc[c]
            act(PT, psc[i % 3][:], AF.Exp, [t_sc[i % 3]], [tpp], scale=scale)
            pace_op = S.ops['act'][-1]
            mm(ACC[0][:], V[:, kc, 0:128], PT, kc == 0, kc == 31, [tv, tpp], [tacc[0]])
            mm(ACC[1][:], V[:, kc, 128:256], PT, kc == 0, kc == 31, [tv, tpp], [tacc[1]])
            if c == 0 and kc == 0:
                if qb == 0 and h + 1 < NH:
                    c_load_kv(h + 1)
                if si + 1 < len(csteps):
                    c_load_q(si + 1)
            if c == 0 and kc == 16:
                S.wait_on('pool', pace_op)
                issue_conversions((len(cv_list) + len(csteps) - 1) // len(csteps))
            if kc % 4 == 1 and pending_ones is not None:
                pending_ones(); pending_ones = None
            if kc == 4 and pending_evac is not None:
                evac_a(*pending_evac)
            if kc == 6 and pending_evac is not None:
                evac_b(*pending_evac); pending_evac = None
            if kc % 4 == 3:
                base = slot - 3
                gp = (kc // 4) % 2
                tgrp = [t_p[base + u] for u in range(4)]
                tt('dve', gs2[gp][:], pT4[:, base:base + 2, :], pT4[:, base + 2:base + 4, :], ALU.add, tgrp, [t_gs2[gp]])
                tt('dve', gs1[gp][:], gs2[gp][:, 0, :], gs2[gp][:, 1, :], ALU.add, [t_gs2[gp]], [t_gs1[gp]])

                def ones_mm(gp=gp, kc=kc):
                    mm(psm[:], ones[:], gs1[gp][:], kc == 3, kc == 31, [t_ones, t_gs1[gp]], [t_psm])
                pending_ones = ones_mm
            if kc == 31:
                pending_evac = (si, c)
            if si > 0 and c == 0:
                if kc == 9:
                    epi1a(si - 1)
                elif kc == 14:
                    epi1b(si - 1)
                elif kc == 20:
                    epi2a(si - 1, psc[i % 3], t_sc[i % 3])
                elif kc == 26:
                    epi2b(si - 1)
                elif kc == 30:
                    epi2c(si - 1)
        pending_ones()
        evac_a(*pending_evac)
        evac_b(*pending_evac)
        if dbg and 'dbg_small' in dbg:
            dsm = nc.dram_tensor("dbg_small", [128, 10], F32, kind="ExternalOutput").ap()
            doa = nc.dram_tensor("dbg_oa", [128, 2048], F32, kind="ExternalOutput").ap()
            drr = nc.dram_tensor("dbg_rr", [128, 512], F32, kind="ExternalOutput").ap()
            t_dbg = S.tile("dbg")
            dma('sp', dsm[:, 0:8], lam_s[:], [t_lam], [t_dbg], 'dbg')
            dma('sp', dsm[:, 8:10], gss[:], [t_gs], [t_dbg], 'dbg')
            dma('sp', doa[:, 0:1024], oa[:].rearrange("p a b -> p (a b)"), [t_oa], [t_dbg], 'dbg')
            dma('sp', doa[:, 1024:2048], ob[:].rearrange("p a b -> p (a b)"), [t_ob], [t_dbg], 'dbg')
            dma('sp', drr[:], rr[:], [t_rr], [t_dbg], 'dbg')
            dk = nc.dram_tensor("dbg_k", [128, 2 * S_FULL], BF16, kind="ExternalOutput").ap()
            dv = nc.dram_tensor("dbg_v", [128, 32 * 256], BF16, kind="ExternalOutput").ap()
            dq = nc.dram_tensor("dbg_q", [128, 2 * 512], BF16, kind="ExternalOutput").ap()
            dma('sp', dk[:], kT[1][:].rearrange("p a b -> p (a b)"), [t_k[1]], [t_dbg], 'dbg')
            dma('sp', dv[:], vh[1][:].rearrange("p a b -> p (a b)"), [t_v[1]], [t_dbg], 'dbg')
            dma('sp', dq[:], qh[1][:].rearrange("p a b -> p (a b)"), [t_q[1]], [t_dbg], 'dbg')
            phase_tiles.append(t_dbg)
        epi1a(len(csteps) - 1)
        epi1b(len(csteps) - 1)
        epi2a(len(csteps) - 1, psc[0], t_sc[0])
        epi2b(len(csteps) - 1)
        epi2c(len(csteps) - 1)
        end_phase()

    if not stop('A') and not stop('B') and not stop('C'):
        fsb = sb("d_f", [128, 32, 512], BF16); t_f = T()
        tab = [sb(f"d_t{i}", [128, 16, 512], BF16) for i in range(8)]; t_tab = Ts(8)
        cd = sb("d_cd", [128, 128], BF16); nsd = sb("d_nsd", [128, 128], BF16); t_cd = T()
        ab = [sb(f"d_ab{i}", [128, 2, 512], BF16) for i in range(2)]; t_ab = Ts(2)
        yst = [sb(f"d_y{i}", [128, 512], BF16) for i in range(2)]; t_y = Ts(2)
        pab = [ps(f"d_pab{i}", [128, 512]) for i in range(4)]; t_pab = Ts(4)
        py = [ps(f"d_py{i}", [128, 512]) for i in range(2)]; t_py = Ts(2)
        for c0 in range(0, 32, 8):
            dma('sp', fsb[:, c0:c0 + 8, :], f_scr[c0 * 128:(c0 + 8) * 128, :].rearrange("(c p) n -> p c n", p=128), d_f, [t_f], 'd_f')
        cdf = sb("d_cdf", [128, 2, 128], F32); t_cdf = T()
        dma('sp', cdf[:, 0, :], dft_cd, [], [t_cdf], 'd_cd')
        dma('sp', cdf[:, 1, :], dft_nsd, [], [t_cdf], 'd_cd')
        cp('dve', cd[:], cdf[:, 0, :], [t_cdf], [t_cd])
        cp('dve', nsd[:], cdf[:, 1, :], [t_cdf], [t_cd])
        pieces = [(ob_, w, hf) for ob_ in range(4) for w in range(2) for hf in range(2)]

        def d_load(pidx):
            ob_, w, hf = pieces[pidx]
            src = dft_c if w == 0 else dft_s
            for k0 in range(0, 16, 8):
                r0 = hf * 2048 + k0 * 128
                dma('sp', tab[pidx % 8][:, k0:k0 + 8, :],
                    src[r0:r0 + 1024, ob_ * 512:(ob_ + 1) * 512].rearrange("(k p) n -> p k n", p=128),
                    [], [t_tab[pidx % 8]], f'd_t{pidx % 8}')

        for p_ in range(4):
            d_load(p_)
        gi = 0
        for ob_ in range(4):
            if ob_ + 1 < 4:
                for p_ in range(4):
                    d_load((ob_ + 1) * 4 + p_)
            for g in range(4):
                for w in range(2):
                    P = pab[(gi * 2 + w) % 4]; tp = t_pab[(gi * 2 + w) % 4]
                    for hf in range(2):
                        pidx = ob_ * 4 + w * 2 + hf
                        TB = tab[pidx % 8]; ttb = t_tab[pidx % 8]
                        for k in range(16):
                            sc_ = hf * 16 + k
                            mm(P[:], fsb[:, sc_, g * 128:(g + 1) * 128], TB[:, k, :], sc_ == 0, sc_ == 31, [t_f, ttb], [tp])
                    cp('act' if w else 'dve', ab[gi % 2][:, w, :], P[:], [tp], [t_ab[gi % 2]])
                PY = py[gi % 2]; tpy = t_py[gi % 2]
                mm(PY[:], cd[:], ab[gi % 2][:, 0, :], True, False, [t_cd, t_ab[gi % 2]], [tpy])
                mm(PY[:], nsd[:], ab[gi % 2][:, 1, :], False, True, [t_cd, t_ab[gi % 2]], [tpy])
                cp('act', yst[gi % 2][:], PY[:], [tpy], [t_y[gi % 2]])
                dma('sp', yT_scr[ob_, :, g, :], yst[gi % 2][:], [t_y[gi % 2]], [d_yT[ob_]], f'd_st{gi % 2}')
                gi += 1
        end_phase()

    if not any(stop(x) for x in 'ABCD'):
        wa = sb("e_wa", [128, 12, D], BF16); t_wa = Ts(4)
        wf = sb("e_wf", [128, 4, D], BF16); t_wf = Ts(4)
        onb = [sb(f"e_on{i}", [128, 12, 512], BF16) for i in range(2)]; t_onb = Ts(2)
        yb = [sb(f"e_y{i}", [128, 4, 512], BF16) for i in range(2)]; t_yb = Ts(2)
        gab = [sb(f"e_ga{i}", [128, 512], BF16) for i in range(4)]; t_gab = Ts(4)
        gfb = [sb(f"e_gf{i}", [128, 512], BF16) for i in range(4)]; t_gfb = Ts(4)
        m1 = [sb(f"e_m1{i}", [128, 512], F32) for i in range(2)]; t_m1 = Ts(2)
        m2 = [sb(f"e_m2{i}", [128, 512], F32) for i in range(2)]; t_m2 = Ts(2)
        mst = [sb(f"e_ms{i}", [128, 16, 512], BF16) for i in range(2)]; t_mst = Ts(2)
        pa = [ps(f"e_pa{i}", [128, 512]) for i in range(3)]; t_pa = Ts(3)
        pf = [ps(f"e_pf{i}", [128, 512]) for i in range(3)]; t_pf = Ts(3)

        def e_load(tb):
            b = tb % 2
            load_blk('sp', onb[b], oT_scr, tb, 12, [d_oT[tb]], [t_onb[b]], f'e_on{b}', ksplit=12)
            load_blk('sp', yb[b], yT_scr, tb, 4, [d_yT[tb]], [t_yb[b]], f'e_y{b}')

        e_load(0)
        for cq in range(4):
            load_wb(wf[:, :, cq * 512:(cq + 1) * 512], 'wf', 0, 4, cq * 512, 512, [t_wf[cq]], 'e_wf', ksplit=4)
            load_wb(wa[:, :, cq * 512:(cq + 1) * 512], 'wa', 0, 12, cq * 512, 512, [t_wa[cq]], 'e_wa', ksplit=12)
        ci = 0
        for tb in range(4):
            if tb + 1 < 4:
                e_load(tb + 1)
            b = tb % 2
            for dc in range(16):
                PA = pa[ci % 3]; tpa = t_pa[ci % 3]
                PF = pf[ci % 3]; tpf = t_pf[ci % 3]
                M1 = m1[ci % 2]; tm1 = t_m1[ci % 2]
                M2 = m2[ci % 2]; tm2 = t_m2[ci % 2]
                GA = gab[ci % 4]; tga = t_gab[ci % 4]
                GF = gfb[ci % 4]; tgf = t_gfb[ci % 4]
                dma('sp', GA[:], ga_scr[tb, :, dc, :], [d_ga[tb]], [tga], f'e_ga{ci % 4}')
                dma('sp', GF[:], gf_scr[tb, :, dc, :], [d_gf[tb]], [tgf], f'e_gf{ci % 4}')
                ci += 1
                for k in range(12):
                    mm(PA[:], wa[:, k, dc * 128:(dc + 1) * 128], onb[b][:, k, :], k == 0, k == 11, [t_wa[dc // 4], t_onb[b]], [tpa])
                for k in range(4):
                    mm(PF[:], wf[:, k, dc * 128:(dc + 1) * 128], yb[b][:, k, :], k == 0, k == 3, [t_wf[dc // 4], t_yb[b]], [tpf])
                tt('dve', M1[:], PA[:], GA[:], ALU.mult, [tpa, tga], [tm1])
                tt('dve', M2[:], PF[:], GF[:], ALU.mult, [tpf, tgf], [tm2])
                tt('dve', mst[b][:, dc, :], M1[:], M2[:], ALU.add, [tm1, tm2], [t_mst[b]])
            store_blk('sp', mT_scr, mst[b], tb, 16, [t_mst[b]], [d_mT[tb]], f'e_st{b}')
        end_phase()

    if not any(stop(x) for x in ['A', 'B', 'C', 'D', 'E1']):
        wo = sb("f_wo", [128, 16, D], BF16); t_wo = Ts(4)
        gb = sb("f_gb", [128, D], F32); t_g = T()
        idb = sb("f_id", [128, 128], BF16); t_id = T()
        mb = [sb(f"f_m{i}", [128, 16, 512], BF16) for i in range(2)]; t_mb = Ts(2)
        xts = [sb(f"f_x{i}", [128, D], F32) for i in range(2)]; t_xs = Ts(2)
        x2t = [sb(f"f_x2{i}", [128, D], F32) for i in range(3)]; t_x2 = Ts(3)
        junk = sb("f_junk", [128, D], F32); t_junk = T()
        hb = [sb(f"f_hb{i}", [128, D], BF16) for i in range(2)]; t_hb = Ts(2)
        ssq = [sb(f"f_ss{i}", [128, 4], F32) for i in range(2)]; t_ss = Ts(2)
        hTb = [sb(f"f_hT{i}", [128, 16, 512], BF16) for i in range(2)]; t_hT = Ts(2)
        po = [ps(f"f_po{i}", [128, 512]) for i in range(4)]; t_po = Ts(4)
        pT = [ps(f"f_pT{i}", [128, 16, 128], BF16) for i in range(2)]; t_pT = Ts(2)
        dma('sp', gb[:], bcast_rows(g_ffn, D), [], [t_g], 'f_g')
        idf = sb("f_idf", [128, 128], F32); t_idf = T()
        dma('sp', idf[:], ident, [], [t_idf], 'f_id')
        cp('dve', idb[:], idf[:], [t_idf], [t_id])

        def f_load_m(tb):
            load_blk('sp', mb[tb % 2], mT_scr, tb, 16, [d_mT[tb]], [t_mb[tb % 2]], f'f_m{tb % 2}')

        def f_load_x(i):
            dma('sp', xts[i % 2][:], xs[i * 128:(i + 1) * 128, :], [], [t_xs[i % 2]], f'f_x{i % 2}')

        def f_part2(i):
            tb = i // 4; sub = i % 4; b = i % 2
            cp('act', hTb[tb % 2][:, :, sub * 128:(sub + 1) * 128], pT[b][:], [t_pT[b]], [t_hT[tb % 2]])
            if sub == 3:
                store_blk('sp', h2T_scr, hTb[tb % 2], tb, 16, [t_hT[tb % 2]], [d_h2T[tb]], f'f_st{tb % 2}')

        def f_mm(i):
            tb = i // 4; sub = i % 4
            if sub == 0 and tb + 1 < 4:
                f_load_m(tb + 1)
            if i + 1 < 16:
                f_load_x(i + 1)
            M = mb[tb % 2]; tm = t_mb[tb % 2]
            for cb in range(4):
                for k in range(16):
                    mm(po[cb][:], M[:, k, sub * 128:(sub + 1) * 128], wo[:, k, cb * 512:(cb + 1) * 512], k == 0, k == 15, [tm, t_wo[cb]], [t_po[cb]])

        def f_post(i):
            tb = i // 4
            b = i % 2; b3 = i % 3
            for cb in range(4):
                tt('dve', x2t[b3][:, cb * 512:(cb + 1) * 512], po[cb][:], xts[b][:, cb * 512:(cb + 1) * 512], ALU.add, [t_po[cb], t_xs[b]], [t_x2[b3]])
            dma('sp', x2_scr[i * 128:(i + 1) * 128, :], x2t[b3][:], [t_x2[b3]], [d_x2[tb]], f'f_sx{b3}')
            norm_sq(x2t[b3][:], ssq[b], junk, t_x2[b3], t_ss[b], t_junk)

        f_load_m(0)
        f_load_x(0)
        for cb in range(4):
            load_wb(wo[:, :, cb * 512:(cb + 1) * 512], 'wo', 0, 16, cb * 512, 512, [t_wo[cb]], 'f_wo', ksplit=8)
        f_mm(0)
        f_post(0)
        for i in range(16):
            if i + 1 < 16:
                f_mm(i + 1)
            b = i % 2; b3 = i % 3
            norm_rest(x2t[b3][:], gb, idb, ssq[b], hb[b], pT[b], t_x2[b3], t_g, t_id, t_ss[b], t_hb[b], t_pT[b])
            if i + 1 < 16:
                f_post(i + 1)
            if i > 0:
                f_part2(i - 1)
        f_part2(15)
        end_phase()

    if not any(stop(x) for x in ['A', 'B', 'C', 'D', 'E1', 'E2']):
        wg = [sb(f"g_wg{i}", [128, 16, 512], BF16) for i in range(2)]; t_wg = Ts(2)
        wu = [sb(f"g_wu{i}", [128, 16, 512], BF16) for i in range(2)]; t_wu = Ts(2)
        hbk = [sb(f"g_h{i}", [128, 16, 512], BF16) for i in range(2)]; t_h = Ts(2)
        sl_ = [sb(f"g_s{i}", [128, 512], F32) for i in range(2)]; t_sl = Ts(2)
        ast = [sb(f"g_a{i}", [128, 4, 512], BF16) for i in range(2)]; t_ast = Ts(2)
        pg = [ps(f"g_pg{i}", [128, 512]) for i in range(3)]; t_pg = Ts(3)
        pu = [ps(f"g_pu{i}", [128, 512]) for i in range(3)]; t_pu = Ts(3)
        steps = [(sl, tb) for sl in range(11) for tb in range(4)]

        def g_load_w(sl):
            load_wb(wg[sl % 2], 'wg', 0, 16, sl * 512, 512, [t_wg[sl % 2]], f'g_wg{sl % 2}', ksplit=8)
            load_wb(wu[sl % 2], 'wu', 0, 16, sl * 512, 512, [t_wu[sl % 2]], f'g_wu{sl % 2}', ksplit=8)

        def g_load_h(si):
            sl, tb = steps[si]
            load_blk('sp', hbk[si % 2], h2T_scr, tb, 16, [d_h2T[tb]], [t_h[si % 2]], f'g_h{si % 2}')

        g_load_w(0)
        g_load_h(0)
        ci = 0
        for si, (sl, tb) in enumerate(steps):
            if si + 1 < len(steps):
                g_load_h(si + 1)
            if tb == 0 and sl + 1 < 11:
                g_load_w(sl + 1)
            if tb >= 2:
                S.wait_on('pool', S.ops['act'][-1])
                issue_conversions(2, which=1)
            H = hbk[si % 2]; th = t_h[si % 2]
            A = ast[si % 2]; ta = t_ast[si % 2]
            for c in range(4):
                PG = pg[ci % 3]; tpg = t_pg[ci % 3]
                PU = pu[ci % 3]; tpu = t_pu[ci % 3]
                SL = sl_[ci % 2]; tsl = t_sl[ci % 2]
                ci += 1
                for k in range(16):
                    mm(PG[:], wg[sl % 2][:, k, c * 128:(c + 1) * 128], H[:, k, :], k == 0, k == 15, [t_wg[sl % 2], th], [tpg])
                for k in range(16):
                    mm(PU[:], wu[sl % 2][:, k, c * 128:(c + 1) * 128], H[:, k, :], k == 0, k == 15, [t_wu[sl % 2], th], [tpu])
                act(SL[:], PG[:], AF.Silu, [tpg], [tsl])
                tt('dve', A[:, c, :], SL[:], PU[:], ALU.mult, [tsl, tpu], [ta])
            dma('sp', aT_scr[tb, :, sl * 4:(sl + 1) * 4, :], A[:], [ta], [d_aT[tb]], f'g_st{si % 2}')
        end_phase()

    if not any(stop(x) for x in ['A', 'B', 'C', 'D', 'E1', 'E2', 'F']):
        KH = 22
        wdh = sb("h_wd", [128, KH, D], BF16); t_wdh = Ts(4)
        ab_ = [sb(f"h_a{i}", [128, KH, 512], BF16) for i in range(2)]; t_ab = Ts(2)
        xin = [sb(f"h_xi{i}", [128, D], F32) for i in range(2)]; t_xin = Ts(2)
        xo = [sb(f"h_xo{i}", [128, D], F32) for i in range(2)]; t_xo = Ts(2)
        ot = [sb(f"h_ot{i}", [128, D], F32) for i in range(2)]; t_ot = Ts(2)
        gb = sb("h_gb", [128, D], F32); t_g = T()
        ssq = [sb(f"h_ss{i}", [128, 4], F32) for i in range(2)]; t_ss = Ts(2)
        pd = [[ps(f"h_pd{s_}{i}", [128, 512]) for i in range(4)] for s_ in range(2)]
        t_pd = [Ts(4) for _ in range(2)]
        dma('sp', gb[:], bcast_rows(g_final, D), [], [t_g], 'h_g')
        gsteps = [(p_, tt_) for p_ in range(2) for tt_ in range(16)]

        def h_load_w(p_):
            for cb in range(4):
                load_wb(wdh[:, :, cb * 512:(cb + 1) * 512], 'wd', p_ * KH * 128, KH, cb * 512, 512, [t_wdh[cb]], 'h_wd', ksplit=11)

        def h_load_a(p_, tb):
            j = (p_ * 4 + tb) % 2
            dma('sp', ab_[j][:, 0:11, :], aT_scr[tb, :, p_ * KH:p_ * KH + 11, :], [d_aT[tb]], [t_ab[j]], f'h_a{j}')
            dma('sp', ab_[j][:, 11:22, :], aT_scr[tb, :, p_ * KH + 11:(p_ + 1) * KH, :], [d_aT[tb]], [t_ab[j]], f'h_a{j}')

        def h_load_x(gi_):
            p_, tt_ = gsteps[gi_]
            src = x2_scr if p_ == 0 else x3_scr
            dep = d_x2[tt_ // 4] if p_ == 0 else d_x3[tt_ // 4]
            dma('sp', xin[gi_ % 2][:], src[tt_ * 128:(tt_ + 1) * 128, :], [dep], [t_xin[gi_ % 2]], f'h_xi{gi_ % 2}')

        def h_norm(gi_):
            p_, tt_ = gsteps[gi_]
            b_ = gi_ % 2
            ts('dve', ssq[b_][:, 1:2], ssq[b_][:, 0:1], 1.0 / D, EPS, ALU.mult, ALU.add, [t_ss[b_]], [t_ss[b_]])
            act(ssq[b_][:, 2:3], ssq[b_][:, 1:2], AF.Sqrt, [t_ss[b_]], [t_ss[b_]])
            recip(ssq[b_][:, 3:4], ssq[b_][:, 2:3], [t_ss[b_]], [t_ss[b_]])
            stt('dve', ot[b_][:], xo[b_][:], ssq[b_][:, 3:4], gb[:], ALU.mult, ALU.mult, [t_xo[b_], t_ss[b_], t_g], [t_ot[b_]])
            dma('sp', out[tt_ * 128:(tt_ + 1) * 128, :], ot[b_][:], [t_ot[b_]], [d_out[tt_ // 4]], f'h_so{b_}')

        h_load_a(0, 0)
        h_load_x(0)
        h_load_w(0)
        for gi_, (p_, tt_) in enumerate(gsteps):
            tb = tt_ // 4; sub = tt_ % 4
            if sub == 0:
                nxt = p_ * 4 + tb + 1
                if nxt < 8:
                    h_load_a(nxt // 4, nxt % 4)
            if gi_ + 1 < len(gsteps):
                h_load_x(gi_ + 1)
            A = ab_[(p_ * 4 + tb) % 2]; ta = t_ab[(p_ * 4 + tb) % 2]
            b_ = gi_ % 2
            P = pd[gi_ % 2]; tp = t_pd[gi_ % 2]
            for cb in range(4):
                for k in range(KH):
                    mm(P[cb][:], A[:, k, sub * 128:(sub + 1) * 128], wdh[:, k, cb * 512:(cb + 1) * 512], k == 0, k == KH - 1, [ta, t_wdh[cb]], [tp[cb]])
                if p_ == 0 and tt_ == 15:
                    load_wb(wdh[:, :, cb * 512:(cb + 1) * 512], 'wd', KH * 128, KH, cb * 512, 512, [t_wdh[cb]], 'h_wd', ksplit=11)
            for cb in range(4):
                tt('dve', xo[b_][:, cb * 512:(cb + 1) * 512], P[cb][:], xin[b_][:, cb * 512:(cb + 1) * 512], ALU.add, [tp[cb], t_xin[b_]], [t_xo[b_]])
            if p_ == 0:
                dma('sp', x3_scr[tt_ * 128:(tt_ + 1) * 128, :], xo[b_][:], [t_xo[b_]], [d_x3[tb]], f'h_sx{b_}')
            else:
                act(ot[b_][:], xo[b_][:], AF.Square, [t_xo[b_]], [t_ot[b_], t_ss[b_]], accum=ssq[b_][:, 0:1])
                if tt_ > 0:
                    h_norm(gi_ - 1)
        h_norm(len(gsteps) - 1)
        end_phase()

    alld = d_hT + d_qT + d_kT + d_v + d_f + d_ga + d_gf + d_oT + d_yT + d_mT + d_x2 + d_h2T + d_aT + d_x3 + d_out
    S.finish('sp', alld)
    es.close()
    S.emit()
    return nc


def host_tables(half):
    pos = np.concatenate([np.arange(half * T_OWN, (half + 1) * T_OWN), np.arange((1 - half) * T_OWN, (2 - half) * T_OWN)])
    inv_freq = (np.float32(ROPE_THETA) ** (-np.arange(0, 32, 2, dtype=np.float32) / np.float32(32))).astype(np.float32)
    ang = pos.astype(np.float32)[:, None] * inv_freq[None, :]
    c = np.cos(ang).astype(np.float32).T
    s = np.sin(ang).astype(np.float32).T
    rope_c = np.ascontiguousarray(np.concatenate([c, c], 0))
    rope_s = np.ascontiguousarray(np.concatenate([-s, s], 0))
    prod = (pos[:, None].astype(np.int64) * pos[None, :T_OWN].astype(np.int64)) % S_FULL
    a = (2.0 * np.pi / S_FULL) * prod.astype(np.float64)
    dft_c = np.cos(a).astype(np.float32).astype(ml_dtypes.bfloat16)
    dft_s = np.sin(a).astype(np.float32).astype(ml_dtypes.bfloat16)
    return rope_c, rope_s, dft_c, dft_s


def host_consts():
    perm = np.zeros((32, 32), np.float32)
    for m in range(32):
        perm[(m + 16) % 32, m] = 1.0
    ident = np.eye(128, dtype=np.float32)
    k = np.arange(128)
    a = (2.0 * np.pi / 128) * ((k[:, None] * k[None, :]) % 128).astype(np.float64)
    nrm = 1.0 / math.sqrt(S_FULL * 128.0)
    cd = (np.cos(a) * nrm).astype(np.float32)
    nsd = (-np.sin(a) * nrm).astype(np.float32)
    return perm, ident, cd, nsd


def make_in_maps(x, g_mix, w_in, lambda_q1, lambda_k1, lambda_q2, lambda_k2, g_subln,
                 w_attn_branch, w_four_branch, w_out, g_ffn, w_gate, w_up, w_down, g_final):
    f = lambda a: np.ascontiguousarray(np.asarray(a, dtype=np.float32))
    x = f(x)
    perm, ident, cd, nsd = host_consts()
    tabs = [host_tables(0), host_tables(1)]
    common = {
        "w_in": f(w_in), "w_attn": f(w_attn_branch), "w_four": f(w_four_branch), "w_out": f(w_out),
        "w_gate": f(w_gate), "w_up": f(w_up), "w_down": f(w_down),
        "g_mix": f(g_mix).reshape(1, D), "g_ffn": f(g_ffn).reshape(1, D), "g_final": f(g_final).reshape(1, D),
        "g_sub": np.ascontiguousarray(f(g_subln).reshape(2, 128).T),
        "lams": np.ascontiguousarray(np.stack([f(lambda_q1), f(lambda_k1), f(lambda_q2), f(lambda_k2)], 0)),
        "perm": perm, "ident": ident, "dft_cd": cd, "dft_nsd": nsd,
    }
    in_maps = []
    for c in range(8):
        b, half = c // 2, c % 2
        xs = np.ascontiguousarray(np.concatenate(
            [x[b, half * T_OWN:(half + 1) * T_OWN], x[b, (1 - half) * T_OWN:(2 - half) * T_OWN]], 0))
        rope_c, rope_s, dft_c, dft_s = tabs[half]
        m = dict(common)
        m.update({"xs": xs, "rope_c": rope_c, "rope_s": rope_s, "dft_c": dft_c, "dft_s": dft_s})
        in_maps.append(m)
    return in_maps


_NC_CACHE = {}


def kernel(**inputs):
    in_maps = make_in_maps(**inputs)
    if 'nc' not in _NC_CACHE:
        _NC_CACHE['nc'] = build_nc()
    nc = _NC_CACHE['nc']
    res = run_bass_kernel_spmd(nc, in_maps, core_ids=list(range(8)))
    outp = np.empty((4, S_FULL, D), np.float32)
    for c in range(8):
        b, half = c // 2, c % 2
        outp[b, half * T_OWN:(half + 1) * T_OWN] = res.results[c]["out"]
    return outp
```
